# Optimizing a Trainium2 kernel written in Bass

```python
import jax, jax.numpy as jnp
from jax import lax
import numpy as np

D_MODEL = 1024
BATCH = 4
SEQ = 4096
DEPTH = 1
DEC_BATCH = 128
DEC_SEQ = 8
PAST_LEN = 16384
PAGE_SIZE = 128

HEAD_DIM = 64
ATT_HEADS = 8
KV_HEADS = 2
Q_PER_KV = ATT_HEADS // KV_HEADS
ATT_WIDTH = ATT_HEADS * HEAD_DIM
KV_WIDTH = KV_HEADS * HEAD_DIM
GM_HEADS = 8
GM_WIDTH = GM_HEADS * HEAD_DIM
MIX_WIDTH = ATT_WIDTH + GM_WIDTH
IN_WIDTH = ATT_WIDTH + 2 * KV_WIDTH + 2 * GM_WIDTH
WINDOW = 128
CHUNK = 128
ROPE_THETA = 500000.0
ROT_DIM = HEAD_DIM // 4
D_FF = ((8 * D_MODEL + 3 * 256 - 1) // (3 * 256)) * 256
ALPHA = (2 * DEPTH) ** 0.25
BETA = (8 * DEPTH) ** -0.25
LN_EPS = 1e-5
ATT_SCALE = HEAD_DIM ** -0.5

kernel_name = "hymba_sgu_swa_sink_deepnorm_adaln_step"


def _layer_norm(x, gain=None, bias=None):
    xf = x.astype(jnp.float32)
    mu = jnp.mean(xf, axis=-1, keepdims=True)
    var = jnp.mean(jnp.square(xf - mu), axis=-1, keepdims=True)
    y = (xf - mu) * lax.rsqrt(var + LN_EPS)
    if gain is not None:
        y = y * gain.astype(jnp.float32) + bias.astype(jnp.float32)
    return y.astype(x.dtype)


def _adaln(c, w_ada, b_ada):
    mod = jax.nn.silu(c) @ w_ada + b_ada
    return jnp.split(mod[:, None, :], 6, axis=-1)


def _modulate(x, shift, scale):
    return _layer_norm(x) * (1 + scale) + shift


def _rotary(x, pos):
    half = ROT_DIM // 2
    inv_freq = jnp.power(jnp.float32(ROPE_THETA), -jnp.arange(half, dtype=jnp.float32) * 2.0 / ROT_DIM)
    ang = pos.astype(jnp.float32)[:, None] * inv_freq[None, :]
    cos = jnp.cos(ang)[:, None, :]
    sin = jnp.sin(ang)[:, None, :]
    xf = x.astype(jnp.float32)
    x1 = xf[..., :half]
    x2 = xf[..., half:ROT_DIM]
    out = jnp.concatenate([x1 * cos - x2 * sin, x2 * cos + x1 * sin, xf[..., ROT_DIM:]], axis=-1)
    return out.astype(x.dtype)


def _project(h, w_in, pos, sgu_g, sgu_b):
    n, s, _ = h.shape
    z = h @ w_in
    q, k, v, zu, zv = jnp.split(z, [ATT_WIDTH, ATT_WIDTH + KV_WIDTH, ATT_WIDTH + 2 * KV_WIDTH,
                                    ATT_WIDTH + 2 * KV_WIDTH + GM_WIDTH], axis=-1)
    q = _rotary(q.reshape(n, s, ATT_HEADS, HEAD_DIM), pos)
    k = _rotary(k.reshape(n, s, KV_HEADS, HEAD_DIM), pos)
    v = v.reshape(n, s, KV_HEADS, HEAD_DIM)
    u = jax.nn.gelu(zu)
    gv = _layer_norm(jax.nn.gelu(zv), sgu_g, sgu_b)
    return q, k, v, u, gv


def _sink_weights(scores, mask, sinks):
    scores = jnp.where(mask, scores, -jnp.inf)
    sink = jnp.broadcast_to(sinks.astype(jnp.float32).reshape(KV_HEADS, Q_PER_KV, 1, 1),
                            scores.shape[:-1] + (1,))
    p = jax.nn.softmax(jnp.concatenate([scores, sink], axis=-1), axis=-1)
    return p[..., :-1]


def _attn_prompt(q, k, v, sinks):
    n, s = q.shape[0], q.shape[1]
    nb = s // WINDOW
    qb = q.reshape(n, nb, WINDOW, KV_HEADS, Q_PER_KV, HEAD_DIM)
    kb = k.reshape(n, nb, WINDOW, KV_HEADS, HEAD_DIM)
    vb = v.reshape(n, nb, WINDOW, KV_HEADS, HEAD_DIM)
    pad = ((0, 0), (1, 0), (0, 0), (0, 0), (0, 0))
    kk = jnp.concatenate([jnp.pad(kb, pad)[:, :-1], kb], axis=2)
    vv = jnp.concatenate([jnp.pad(vb, pad)[:, :-1], vb], axis=2)
    scores = jnp.einsum('bnqkgd,bnskd->bnkgqs', qb, kk, preferred_element_type=jnp.float32) * ATT_SCALE
    qi = jnp.arange(WINDOW)[:, None]
    sj = jnp.arange(2 * WINDOW)[None, :]
    rel = qi + WINDOW - sj
    band = (rel >= 0) & (rel < WINDOW)
    valid = (jnp.arange(nb)[:, None, None] > 0) | (sj[None] >= WINDOW)
    mask = (band[None] & valid)[None, :, None, None]
    p = _sink_weights(scores, mask, sinks)
    o = jnp.einsum('bnkgqs,bnskd->bnqkgd', p.astype(vv.dtype), vv)
    return o.reshape(n, s, ATT_WIDTH)


def _attn_sample(q, k, v, k_buf, v_buf, sinks):
    n, t = q.shape[0], q.shape[1]
    wb = k_buf.shape[1]
    kk = jnp.concatenate([k_buf, k], axis=1)
    vv = jnp.concatenate([v_buf, v], axis=1)
    qg = q.reshape(n, t, KV_HEADS, Q_PER_KV, HEAD_DIM)
    scores = jnp.einsum('bqkgd,bskd->bkgqs', qg, kk, preferred_element_type=jnp.float32) * ATT_SCALE
    q_pos = PAST_LEN + jnp.arange(t)
    k_pos = PAST_LEN - wb + jnp.arange(wb + t)
    rel = q_pos[:, None] - k_pos[None, :]
    mask = (rel >= 0) & (rel < WINDOW)
    p = _sink_weights(scores, mask, sinks)
    o = jnp.einsum('bkgqs,bskd->bqkgd', p.astype(vv.dtype), vv)
    return o.reshape(n, t, ATT_WIDTH), kk[:, -wb:], vv[:, -wb:]


def _sgu_prompt(u, gv, w_s, b_s):
    n, s, _ = u.shape
    nc = s // CHUNK
    vc = gv.reshape(n, nc, CHUNK, GM_HEADS, HEAD_DIM)
    sv = jnp.einsum('hts,bnshd->bnthd', jnp.tril(w_s), vc) + b_s.T[:, :, None]
    return u * sv.reshape(n, s, GM_WIDTH)


def _sgu_sample(u, gv, w_s, b_s):
    n, t, _ = u.shape
    vt = gv.reshape(n, t, GM_HEADS, HEAD_DIM)
    w = jnp.tril(w_s)[:, :t, :t]
    sv = jnp.einsum('hts,bshd->bthd', w, vt) + b_s[:, :t].T[:, :, None]
    return u * sv.reshape(n, t, GM_WIDTH)


def _ffn(h, w_gate, w_up, w_down):
    return (jax.nn.silu(h @ w_gate) * (h @ w_up)) @ w_down


def _post(x, f, gate, g, b):
    return _layer_norm(ALPHA * x + gate * f, g, b)


def setup_inputs(seed: int = 0) -> dict:
    key = jax.random.key(seed)
    ks = jax.random.split(key, 24)
    f32 = jnp.float32
    wb = min(WINDOW, PAST_LEN)
    nrm = lambda k, shape: jax.random.normal(k, shape, dtype=f32)
    return {
        "x_prompt": nrm(ks[0], (BATCH, SEQ, D_MODEL)),
        "x_sample": nrm(ks[1], (DEC_BATCH, DEC_SEQ, D_MODEL)),
        "cache_k_win": nrm(ks[2], (DEPTH, DEC_BATCH, wb, KV_HEADS, HEAD_DIM)),
        "cache_v_win": nrm(ks[3], (DEPTH, DEC_BATCH, wb, KV_HEADS, HEAD_DIM)),
        "c_prompt": nrm(ks[4], (BATCH, D_MODEL)),
        "c_sample": nrm(ks[5], (DEC_BATCH, D_MODEL)),
        "w_ada": nrm(ks[6], (DEPTH, D_MODEL, 6 * D_MODEL)) * (0.5 * D_MODEL ** -0.5),
        "b_ada": nrm(ks[7], (DEPTH, 6 * D_MODEL)) * 0.02,
        "w_in": nrm(ks[8], (DEPTH, D_MODEL, IN_WIDTH)) * D_MODEL ** -0.5,
        "attn_sinks": nrm(ks[9], (DEPTH, ATT_HEADS)),
        "sgu_ln_g": 1.0 + 0.02 * nrm(ks[10], (DEPTH, GM_WIDTH)),
        "sgu_ln_b": 0.02 * nrm(ks[11], (DEPTH, GM_WIDTH)),
        "w_s": nrm(ks[12], (DEPTH, GM_HEADS, CHUNK, CHUNK)) * CHUNK ** -0.5,
        "b_s": 1.0 + 0.02 * nrm(ks[13], (DEPTH, GM_HEADS, CHUNK)),
        "w_o": nrm(ks[14], (DEPTH, MIX_WIDTH, D_MODEL)) * (MIX_WIDTH ** -0.5 * BETA),
        "ln1_g": 1.0 + 0.02 * nrm(ks[15], (DEPTH, D_MODEL)),
        "ln1_b": 0.02 * nrm(ks[16], (DEPTH, D_MODEL)),
        "w_gate": nrm(ks[17], (DEPTH, D_MODEL, D_FF)) * D_MODEL ** -0.5,
        "w_up": nrm(ks[18], (DEPTH, D_MODEL, D_FF)) * D_MODEL ** -0.5,
        "w_down": nrm(ks[19], (DEPTH, D_FF, D_MODEL)) * (D_FF ** -0.5 * BETA),
        "ln2_g": 1.0 + 0.02 * nrm(ks[20], (DEPTH, D_MODEL)),
        "ln2_b": 0.02 * nrm(ks[21], (DEPTH, D_MODEL)),
    }


def reference(x_prompt, x_sample, cache_k_win, cache_v_win, c_prompt, c_sample,
              w_ada, b_ada, w_in, attn_sinks, sgu_ln_g, sgu_ln_b, w_s, b_s, w_o,
              ln1_g, ln1_b, w_gate, w_up, w_down, ln2_g, ln2_b):
    yp, ys = x_prompt, x_sample
    pos_p = jnp.arange(yp.shape[1], dtype=jnp.int32)
    pos_s = PAST_LEN + jnp.arange(ys.shape[1], dtype=jnp.int32)
    kwp, vwp, kws, vws, sgv = [], [], [], [], []
    for l in range(DEPTH):
        mp = _adaln(c_prompt, w_ada[l], b_ada[l])
        ms = _adaln(c_sample, w_ada[l], b_ada[l])
        q, k, v, u, gv = _project(_modulate(yp, mp[0], mp[1]), w_in[l], pos_p, sgu_ln_g[l], sgu_ln_b[l])
        mix = jnp.concatenate([_attn_prompt(q, k, v, attn_sinks[l]),
                               _sgu_prompt(u, gv, w_s[l], b_s[l])], axis=-1) @ w_o[l]
        yp = _post(yp, mix, mp[2], ln1_g[l], ln1_b[l])
        kwp.append(k[:, -WINDOW:])
        vwp.append(v[:, -WINDOW:])
        yp = _post(yp, _ffn(_modulate(yp, mp[3], mp[4]), w_gate[l], w_up[l], w_down[l]),
                   mp[5], ln2_g[l], ln2_b[l])
        q, k, v, u, gv = _project(_modulate(ys, ms[0], ms[1]), w_in[l], pos_s, sgu_ln_g[l], sgu_ln_b[l])
        a, k_new, v_new = _attn_sample(q, k, v, cache_k_win[l], cache_v_win[l], attn_sinks[l])
        mix = jnp.concatenate([a, _sgu_sample(u, gv, w_s[l], b_s[l])], axis=-1) @ w_o[l]
        ys = _post(ys, mix, ms[2], ln1_g[l], ln1_b[l])
        kws.append(k_new)
        vws.append(v_new)
        sgv.append(gv)
        ys = _post(ys, _ffn(_modulate(ys, ms[3], ms[4]), w_gate[l], w_up[l], w_down[l]),
                   ms[5], ln2_g[l], ln2_b[l])
    new_k_win_prompt = jnp.stack(kwp, axis=0)
    new_v_win_prompt = jnp.stack(vwp, axis=0)
    new_k_win_sample = jnp.stack(kws, axis=0)
    new_v_win_sample = jnp.stack(vws, axis=0)
    new_sgu_v_sample = jnp.stack(sgv, axis=0)
    return (yp, ys, new_k_win_prompt, new_v_win_prompt, new_k_win_sample, new_v_win_sample, new_sgu_v_sample)
```

```python
import os
from contextlib import ExitStack
import numpy as np
import concourse.bass as bass
import concourse.mybir as mybir
from concourse.bass_utils import run_bass_kernel_spmd

F32 = mybir.dt.float32
BF16 = mybir.dt.bfloat16
I32 = mybir.dt.int32
AF = mybir.ActivationFunctionType
ALU = mybir.AluOpType

D = 1024
DFF = 2816
NF = 22
INW = 1792
ALPHA = 2.0 ** 0.25
EPS = 1e-5
NEG = -30000.0
PAST = 16384
NPT = 16


class Rec:
    def __init__(self, nc, es, decisions=None):
        self.nc = nc
        self.es = es
        self.decisions = decisions
        self.acc_log = {}
        self.eng = {"pe": nc.tensor, "act": nc.scalar, "dve": nc.vector, "pool": nc.gpsimd, "sp": nc.sync}
        self.ops = {k: [] for k in self.eng}
        self.sem = {k: es.enter_context(nc.semaphore("s_" + k)) for k in self.eng}
        self.cnt = {k: 0 for k in self.eng}
        self.waited = {k: {} for k in self.eng}
        self.dsem = {}
        self.psum_acc = {}
        self.track = bool(os.environ.get("KCHECK"))

    def _flat(self, deps, out):
        for d in deps:
            if d is None:
                continue
            if isinstance(d, tuple) and len(d) == 2 and isinstance(d[0], str):
                out.append(d)
            else:
                self._flat(d, out)

    def _waits(self, eng, deps):
        fl = []
        self._flat(deps, fl)
        for key, val in fl:
            if key == eng:
                continue
            if self.waited[eng].get(key, 0) >= val:
                continue
            self.waited[eng][key] = val
            self.ops[eng].append(("wait", key, val))

    def op(self, eng, fn, deps=()):
        self._waits(eng, deps)
        if eng in ("act", "dve", "pool") and self.cnt[eng] > 0:
            need = self.cnt[eng]
            if self.decisions is not None:
                need = self.decisions[eng][self.cnt[eng]]
            if need > 0 and self.waited[eng].get(eng, 0) < need:
                self.waited[eng][eng] = need
                self.ops[eng].append(("wait", eng, need))
        self.cnt[eng] += 1
        self.ops[eng].append(("op", fn))
        return (eng, self.cnt[eng])

    def dma(self, eng, semname, out, in_, deps=()):
        self._waits(eng, deps)
        if semname not in self.dsem:
            self.dsem[semname] = [self.es.enter_context(self.nc.semaphore("d_" + semname)), 0]
        self.dsem[semname][1] += 16
        self.ops[eng].append(("dma", out, in_, semname))
        return (semname, self.dsem[semname][1])

    def semh(self, key):
        return self.sem[key] if key in self.sem else self.dsem[key][0]

    def replay(self, eng, e):
        n = 0
        acc = self.psum_acc.setdefault(eng, [])
        for it in self.ops[eng]:
            if it[0] == "wait":
                e.wait_ge(self.semh(it[1]), it[2])
            elif it[0] == "op":
                bi = it[1](e)
                bi.then_inc(self.sem[eng], 1)
                n += 1
                if self.decisions is None and eng in ("act", "dve", "pool"):
                    self.acc_log.setdefault(eng, []).append((self._regions(bi.ins.ins), self._regions(bi.ins.outs)))
                if self.track:
                    banks = set()
                    for a in list(bi.ins.ins) + list(bi.ins.outs):
                        if getattr(a, "memref", None) == "ps":
                            es_ = 2 if "bfloat16" in str(a.dtype) else 4
                            row = 16384 // es_
                            col0 = a.offset % row
                            ext = 1 + sum((c - 1) * abs(st) for st, c in list(a.ap)[1:])
                            for b in range((col0 * es_) // 2048, ((col0 + ext - 1) * es_) // 2048 + 1):
                                banks.add(b)
                    if banks:
                        acc.append((n, banks))
            else:
                e.dma_start(out=it[1], in_=it[2]).then_inc(self.dsem[it[3]][0], 16)

    @staticmethod
    def _regions(args):
        out = []
        for a in args:
            mr = getattr(a, "memref", None)
            if mr is None:
                continue
            ds = str(a.dtype)
            es_ = 2 if ("bfloat16" in ds or "float16" in ds or "int16" in ds) else (1 if "int8" in ds else 4)
            ap = list(a.ap)
            pst = abs(ap[0][0]) if ap and ap[0][0] != 0 else 0
            off = a.offset % pst if pst > 0 else a.offset
            ext = 1 + sum((c - 1) * abs(st) for st, c in ap[1:])
            out.append((mr, off * es_, (off + ext) * es_))
        return out

    def make_decisions(self):
        def hit(A, B):
            for (m1, l1, h1) in A:
                for (m2, l2, h2) in B:
                    if m1 == m2 and l1 < h2 and l2 < h1:
                        return True
            return False
        dec = {}
        for eng, log in self.acc_log.items():
            d = [0] * (len(log) + 1)
            for i, (rd, wr) in enumerate(log):
                need = 0
                for j in range(i - 1, max(-1, i - 40), -1):
                    prd, pwr = log[j]
                    if hit(pwr, rd) or hit(pwr, wr) or hit(prd, wr):
                        need = j + 1
                        break
                if i >= 40:
                    need = max(need, i - 39)
                d[i] = need
            dec[eng] = d
        return dec

    def check_psum(self):
        engs = list(self.ops)
        pos = {k: 0 for k in engs}
        cnt = {k: 0 for k in engs}
        vc = {k: {x: 0 for x in engs} for k in engs}
        hist = {k: {0: dict(vc[k])} for k in engs}
        dcount = {}
        progress = True
        while progress:
            progress = False
            for eng in engs:
                while pos[eng] < len(self.ops[eng]):
                    it = self.ops[eng][pos[eng]]
                    if it[0] == "wait":
                        key, val = it[1], it[2]
                        if key in cnt:
                            if cnt[key] < val:
                                break
                            src = hist[key][val]
                        else:
                            if dcount.get(key, 0) < val:
                                break
                            src = hist[key][val]
                        for x in engs:
                            vc[eng][x] = max(vc[eng][x], src[x])
                    elif it[0] == "op":
                        cnt[eng] += 1
                        vc[eng][eng] = cnt[eng]
                        hist[eng][cnt[eng]] = dict(vc[eng])
                    else:
                        key = it[3]
                        dcount[key] = dcount.get(key, 0) + 16
                        hist.setdefault(key, {})[dcount[key]] = dict(vc[eng])
                    pos[eng] += 1
                    progress = True
        assert all(pos[k] == len(self.ops[k]) for k in engs), "deadlock in recorded program"
        import bisect
        nbad = 0
        per_bank = {}
        for eng, lst in self.psum_acc.items():
            for idx, banks in lst:
                for b in banks:
                    per_bank.setdefault(b, {}).setdefault(eng, []).append(idx)
        for b, d in per_bank.items():
            for E, xs in d.items():
                for F, fs in d.items():
                    if E == F:
                        continue
                    for x in xs:
                        seen = hist[E][x][F]
                        j = bisect.bisect_right(fs, seen)
                        if j < len(fs) and hist[F][fs[j]][E] < x:
                            nbad += 1
                            if nbad <= 20:
                                print("PSUM UNORDERED bank", b, E, x, "vs", F, fs[j])
        print("check_psum: unordered pairs =", nbad)
        return nbad


def build_nc(decisions="auto"):
    if decisions == "auto":
        _, rec1 = build_nc(decisions=None)
        return build_nc(decisions=rec1.make_decisions())[0]
    nc = bass.Bass("TRN2", target_bir_lowering=False)

    def din(name, shape, dt=F32):
        return nc.dram_tensor(name, list(shape), dt, kind="ExternalInput").ap()

    def dout(name, shape):
        return nc.dram_tensor(name, list(shape), F32, kind="ExternalOutput").ap()

    xp = din("xp", [NPT * 128, D])
    xh = din("xh", [128, D])
    xs = din("xs", [128, D])
    c17 = din("c17", [17, D])
    ck = din("ck", [16, 128, 128])
    cv = din("cv", [16, 128, 128])
    w_ada = din("w_ada", [D, 6 * D])
    b_adaT = din("b_adaT", [128, 48])
    b_gate = din("b_gate", [2, D])
    w_in = din("w_in", [D, INW])
    w_o = din("w_o", [D, D])
    w_gate = din("w_gate", [D, DFF])
    w_up = din("w_up", [D, DFF])
    w_down = din("w_down", [DFF, D])
    sinks = din("sinks", [1, 8])
    vecs = din("vecs", [6, D])
    wsT = din("wsT", [2, 128, 8, 128])
    trilT = din("trilT", [2, 128, 128])
    bsT = din("bsT", [2, 128, 8])
    masks = din("masks", [4, 128, 512])
    m01c = din("m01c", [128, 8])
    posf = din("posf", [128, 18])
    invf = din("invf", [128, 8])
    identin = din("identin", [128, 128])

    yp = dout("yp", [NPT * 128, D])
    ys = dout("ys", [128, D])
    kwin = dout("kwin", [128, 128])
    vwin = dout("vwin", [128, 128])
    outk = dout("outk", [16, 128, 128])
    outv = dout("outv", [16, 128, 128])
    sgv = dout("sgv", [128, 512])

    es = ExitStack()
    with es:
        R = Rec(nc, es, decisions)
        sb_bytes = [0]

        def sb(name, shape, dt=F32):
            n = 1
            for s_ in shape[1:]:
                n *= s_
            sb_bytes[0] += n * (2 if dt == BF16 else 4)
            return es.enter_context(nc.sbuf_tensor(name, list(shape), dt))

        ps = es.enter_context(nc.psum_tensor("ps", [128, 4096], F32))

        def bank(j, n=1):
            return ps[:, j * 512:(j + n) * 512]

        def bankbf(j, n=1):
            return ps[:, j * 512:(j + n) * 512].bitcast(BF16)

        ident = sb("ident", [128, 128], BF16)
        identf = sb("identf", [128, 128], F32)
        w_in_sb = sb("w_in_sb", [128, 8, INW], BF16)
        w_o_sb = sb("w_o_sb", [128, 8, D], BF16)
        gates = sb("gates", [128, 4, D], F32)
        lnv = sb("lnv", [128, 4, D], F32)
        sgu_gb = sb("sgu_gb", [128, D], F32)
        wsT_sb = sb("wsT_sb", [128, 2, 8, 128], BF16)
        bsT_sb = sb("bsT_sb", [128, 2, 8], F32)
        mask_sb = sb("mask_sb", [128, 4, 512], BF16)
        m01c_sb = sb("m01c_sb", [128, 8], F32)
        cos_sb = sb("cos_sb", [128, 18, 8], F32)
        sin_sb = sb("sin_sb", [128, 18, 8], F32)
        modT = sb("modT", [128, 4, 8, 17], F32)
        expsink = sb("expsink", [128, 8], F32)
        epst = sb("epst", [128, 1], F32)
        dmy = sb("dmy", [128, 4], F32)
        badaT_sb = sb("badaT_sb", [128, 48], F32)
        x1g = sb("x1g", [128, 4, D], F32)
        h2T = sb("h2T", [128, 8, 512], BF16)
        hidT = sb("hidT", [128, NF, 512], BF16)
        NSLOT = 3
        ring = sb("ring", [128, NSLOT, 4096], BF16)
        xinb = sb("xinb", [128, 2, D], F32)
        xin = xinb[:, 0, :]
        xn = sb("xn", [128, D], BF16)
        xn2 = sb("xn2", [128, D], BF16)
        hT = sb("hT", [128, 8, 128], BF16)
        qkf = sb("qkf", [128, 640], F32)
        vf = sb("vf", [128, 128], F32)
        qkb = sb("qkb", [128, 640], BF16)
        vext = sb("vext", [128, 2, 2, 65], BF16)
        qT = sb("qT", [64, 1024], BF16)
        kT = sb("kT", [64, 2, 256], BF16)
        u_sb = sb("u_sb", [128, 512], F32)
        gvf = sb("gvf", [128, 512], F32)
        gvb = sb("gvb", [128, 512], BF16)
        PT = sb("PT", [128, 2048], BF16)
        mixcat = sb("mixcat", [128, D], BF16)
        mixcatT = sb("mixcatT", [128, 8, 128], BF16)
        tmpA = sb("tmpA", [128, D], F32)
        st = sb("st", [128, 12], F32)
        mv = sb("mv", [128, 2], F32)
        rstd = sb("rstd", [128, 1], F32)
        nmr = sb("nmr", [128, 1], F32)
        st2 = sb("st2", [128, 12], F32)
        mv2 = sb("mv2", [128, 2], F32)
        rstd2 = sb("rstd2", [128, 1], F32)
        nmr2 = sb("nmr2", [128, 1], F32)
        st3 = sb("st3", [128, 12], F32)
        mv3 = sb("mv3", [128, 2], F32)
        rstd3 = sb("rstd3", [128, 1], F32)
        nmr3 = sb("nmr3", [128, 1], F32)
        den = sb("den", [128, 8], F32)
        rden = sb("rden", [128, 8], F32)
        sg_sb = sb("sg_sb", [128, 512], F32)
        Osum = sb("Osum", [128, 2, 4, 65], F32)
        siluT = sb("siluT", [128, 8, 17], BF16)
        posb = sb("posb", [128, 18], F32)
        invb = sb("invb", [128, 8], F32)
        print("SBUF bytes/partition:", sb_bytes[0])
        assert sb_bytes[0] < 212700, sb_bytes[0]
        HF = hidT[:].rearrange("p f n -> p (f n)")
        kTc = HF[0:64, 0:4096].rearrange("p (b h j) -> p b h j", b=16, h=2)
        cvb = HF[:, 4096:4096 + 2080].rearrange("p (b h c) -> p b h c", b=16, h=2)
        PTc = HF[:, 6656:7680]
        qTs = HF[0:64, 7680:8704].rearrange("p (k b r) -> p k b r", k=2, b=16)
        ang = tmpA[:, 256:400].rearrange("p (a b) -> p a b", b=8)
        kfl = tmpA[:, 512:656].rearrange("p (a b) -> p a b", b=8)
        ckb = h2T[:].rearrange("p k n -> p (k n)")[:, 0:2048].rearrange("p (b c) -> p b c", b=16)
        OcT_sb = tmpA
        c17_sb = x1g[0:17, 2, :]
        silu_bf = mixcat[0:17, :]
        siluTe = HF[:, 0:2048].rearrange("p (g k t) -> p g k t", g=2, k=8)
        bg1 = x1g[:, 1, :]
        rt = HF[:, 8704:9344].bitcast(F32).rearrange("p (a b) -> p a b", a=4)
        kin = tmpA[:, 768:912].bitcast(I32).rearrange("p (a b) -> p a b", b=8)
        trl = PT[:, 0:256].rearrange("p (g t) -> p g t", g=2)
        bg2 = HF[:, 2048:4096].bitcast(F32)

        t_id = R.dma("sp", "c_id", identf[:], identin)
        t_c17 = R.dma("sp", "c_c17", c17_sb, c17)
        t_badaT = R.dma("sp", "c_bada", badaT_sb[:], b_adaT)
        t_bg = R.dma("sp", "c_bg", bg1, b_gate[0:1, :].broadcast_to([128, D]))
        t_bg2 = R.dma("sp", "c_bg2", bg2, b_gate[1:2, :].broadcast_to([128, D]))
        t_pos = R.dma("sp", "c_pos", posb[:], posf)
        t_inv = R.dma("sp", "c_inv", invb[:], invf)
        t_bsT = R.dma("sp", "c_bsT", bsT_sb[:], bsT.rearrange("g p h -> p g h"))
        t_m01 = R.dma("sp", "c_m01", m01c_sb[:], m01c)
        t_snk = R.dma("sp", "c_snk", expsink[:], sinks[0:1, :].broadcast_to([128, 8]))
        t_lnv = [R.dma("sp", "c_lnv%d" % i, lnv[:, i, :], vecs[1 + i:2 + i, :].broadcast_to([128, D])) for i in range(4)]
        t_sgu = R.dma("sp", "c_sgu", sgu_gb[:], vecs[0:1, :].broadcast_to([128, D]))
        t_mask = R.dma("pool", "c_mask", mask_sb[:], masks.rearrange("m p n -> p m n"))
        t_wsT = R.dma("pool", "c_wsT", wsT_sb[:], wsT.rearrange("g p h t -> p g h t"))
        t_trl = R.dma("pool", "c_trl", trl, trilT.rearrange("g p t -> p g t"))

        t = R.op("dve", lambda e: e.tensor_copy(out=ident[:], in_=identf[:]), [t_id])
        t_ident = t
        R.op("dve", lambda e: e.memset(epst[:], EPS))
        R.op("dve", lambda e: e.memset(dmy[:], 0.0))
        t_vmem = R.op("dve", lambda e: e.memset(vext[:], 1.0))
        t_ws = R.op("dve", lambda e: e.tensor_tensor(
            out=wsT_sb[:], in0=wsT_sb[:],
            in1=trl.unsqueeze(2).broadcast_to([128, 2, 8, 128]), op=ALU.mult), [t_wsT, t_trl])
        t_es = R.op("act", lambda e: e.activation(out=expsink[:], in_=expsink[:], func=AF.Exp), [t_snk])

        TWO_PI = float(2.0 * np.pi)

        def trig(dst, shift):
            R.op("dve", lambda e: e.tensor_tensor(
                out=ang, in0=posb[:].unsqueeze(2).broadcast_to([128, 18, 8]),
                in1=invb[:].unsqueeze(1).broadcast_to([128, 18, 8]), op=ALU.mult), [t_pos, t_inv])
            if shift != 0.0:
                R.op("dve", lambda e: e.tensor_scalar(out=ang, in0=ang, scalar1=shift, scalar2=None, op0=ALU.add))
            R.op("dve", lambda e: e.tensor_scalar(out=kfl, in0=ang, scalar1=1.0 / TWO_PI, scalar2=None, op0=ALU.mult))
            R.op("dve", lambda e: e.tensor_copy(out=kin, in_=kfl))
            R.op("dve", lambda e: e.tensor_copy(out=kfl, in_=kin))
            R.op("dve", lambda e: e.scalar_tensor_tensor(out=ang, in0=kfl, scalar=-TWO_PI, in1=ang,
                                                         op0=ALU.mult, op1=ALU.add))
            tt = R.op("dve", lambda e: e.tensor_scalar(out=ang, in0=ang, scalar1=3.14159, scalar2=-3.14159,
                                                       op0=ALU.min, op1=ALU.max))
            ta = R.op("act", lambda e: e.activation(out=dst[:], in_=ang, func=AF.Sin), [tt])
            return ta

        ta = trig(sin_sb, 0.0)
        R._waits("dve", [ta])
        t_trig = trig(cos_sb, float(np.pi / 2))


        R.op("act", lambda e: e.activation(out=silu_bf, in_=c17_sb, func=AF.Silu), [t_c17])
        t_sl = ("act", R.cnt["act"])
        tp = None
        for k in range(8):
            tp = R.op("pe", lambda e, k=k: e.transpose(out=bankbf(4)[:, k * 32:k * 32 + 17], in_=mixcat[0:17, k * 128:(k + 1) * 128],
                                                       identity=ident[0:17, 0:17]), [t_sl, t_ident])
        t_sT = R.op("dve", lambda e: e.tensor_copy(
            out=siluT[:], in_=bankbf(4)[:, 0:256].rearrange("p (k c) -> p k c", c=32)[:, :, 0:17]), [tp])
        R.op("dve", lambda e: e.tensor_copy(out=siluTe[:, 0, :, :], in_=siluT[:, :, 0:1].broadcast_to([128, 8, 128])))
        t_sTe = R.op("dve", lambda e: e.tensor_copy(
            out=siluTe[:, 1, :, :].rearrange("p k (b t) -> p k b t", t=8),
            in_=siluT[:, :, 1:17].unsqueeze(3).broadcast_to([128, 8, 16, 8])))

        slot_free = [None] * NSLOT
        ring_n = [0]

        def ring_load(loads):
            s_ = ring_n[0] % NSLOT
            ring_n[0] += 1
            tok = None
            for i, (dstf, src) in enumerate(loads):
                tok = R.dma("pool", "ring%d" % s_, dstf(ring[:, s_, :]), src, [slot_free[s_]] if i == 0 else ())
            return s_, tok

        ada = {"last": None, "slots": {}}
        wo_tok = [None]

        def ada_load(cc):
            ada["slots"][cc] = ring_load([(lambda sl: sl.rearrange("p (k n) -> p k n", k=8),
                                           w_ada[:, cc * 512:(cc + 1) * 512].rearrange("(k p) n -> p k n", p=128))])

        def ada_compute(cc, pz=None, gb=(0, 1), mb=2):
            s_, tl = ada["slots"][cc]
            wa = ring[:, s_, :].rearrange("p (k n) -> p k n", k=8)
            which = cc // 2
            if which in (2, 5):
                half = cc % 2
                gi = 0 if which == 2 else 2
                bsrc = bg1 if which == 2 else bg2
                bdep = t_bg if which == 2 else t_bg2
                for grp in range(2):
                    tm = None
                    for k in range(8):
                        tm = R.op("pe", lambda e, k=k, grp=grp, wa=wa: e.matmul(
                            out=bank(gb[grp]), lhsT=siluTe[:, grp, k, :], rhs=wa[:, k, :], start=(k == 0), stop=(k == 7)),
                            [tl, t_sTe, pz, ada["last"] if k == 0 else None])
                    ada["last"] = R.op("dve", lambda e, grp=grp, gi=gi, half=half, bsrc=bsrc: e.tensor_tensor(
                        out=gates[:, gi + grp, half * 512:(half + 1) * 512], in0=bank(gb[grp]),
                        in1=bsrc[:, half * 512:(half + 1) * 512], op=ALU.add), [tm, bdep])
                slot_free[s_] = tm
            else:
                mi = {0: 0, 1: 1, 3: 2, 4: 3}[which]
                tm = None
                for j4 in range(4):
                    for k in range(8):
                        tm = R.op("pe", lambda e, k=k, j4=j4, wa=wa: e.matmul(
                            out=bank(mb)[:, j4 * 32:j4 * 32 + 17], lhsT=wa[:, k, j4 * 128:(j4 + 1) * 128],
                            rhs=siluT[:, k, :], start=(k == 0), stop=(k == 7)),
                            [tl, t_sT, pz, ada["last"] if (k == 0 and j4 == 0) else None])
                for j4 in range(4):
                    jc = (cc % 2) * 4 + j4
                    acol = cc * 4 + j4
                    ada["last"] = R.op("dve", lambda e, j4=j4, jc=jc, mi=mi, acol=acol: e.tensor_scalar(
                        out=modT[:, mi, jc, :], in0=bank(mb)[:, j4 * 32:j4 * 32 + 17],
                        scalar1=badaT_sb[:, acol:acol + 1], scalar2=(1.0 if mi in (1, 3) else 0.0),
                        op0=ALU.add, op1=ALU.add), [tm, t_badaT])
                slot_free[s_] = tm
            return ada["last"]

        for cc in range(3):
            ada_load(cc)
        t_win = R.dma("pool", "w_in", w_in_sb[:], w_in.rearrange("(k p) n -> p k n", p=128))
        for cc in range(4):
            ada_compute(cc)
            if cc + 3 < 6:
                ada_load(cc + 3)
            if cc == 2:
                t_wo = R.dma("pool", "w_o", w_o_sb[:], w_o.rearrange("(k p) n -> p k n", p=128))
        t_ada = ada["last"]
        ada_load(6)

        def ada_deferred(step):
            plan = {0: [("c", 4), ("l", 7), ("c", 5), ("l", 8)], 1: [("c", 6), ("l", 9), ("c", 7), ("l", 10)],
                    2: [("c", 8), ("l", 11), ("c", 9)], 3: [("c", 10), ("c", 11)]}
            for kind_, cc in plan[step]:
                if kind_ == "l":
                    ada_load(cc)
                else:
                    tk_ = ada_compute(cc, pz=[state["pS_free"], state["pQK_free"]], gb=(7, 5), mb=7)
                    state["pS_free"] = [state["pS_free"], tk_]
                    state["pQK_free"] = [state["pQK_free"], tk_]
            if step == 0:
                x1g_free[1] = [x1g_free[1], ada["last"]]
                wo_tok[0] = R.op("dve", lambda e: e.tensor_tensor(
                    out=w_o_sb[:], in0=w_o_sb[:], in1=gates[:, 0, :].unsqueeze(1).broadcast_to([128, 8, D]), op=ALU.mult),
                    [t_wo, ada["last"]])
            if step == 3:
                state["hidT_free"] = [state.get("hidT_free"), ("pe", R.cnt["pe"]), ada["last"]]

        state = {"tmpA_free": t_ada, "pT_free": t_ada,
                 "pZ_free": t_ada, "pS_free": None, "pQK_free": None, "mixcat_free": t_ada,
                 "PT_free": [t_ada, t_vmem, t_ws], "hT_free": None, "xn_free": None, "qT_free": None,
                 "u_free": None, "gv_free": None, "qkf_free": None, "mixcatT_free": None}
        x1g_free = [t_ada] * 4

        def ln_stats(src, deps, sb_=None):
            st_, mv_, rstd_, nmr_ = sb_ if sb_ is not None else (st, mv, rstd, nmr)
            n = src.shape[-1]
            nch = n // 512
            if sb_ is None:
                deps = [deps, state.get("stat_free")]
            for c in range(nch):
                R.op("dve", lambda e, c=c: e.bn_stats(out=st_[:, c * 6:(c + 1) * 6], in_=src[:, c * 512:(c + 1) * 512]), deps)
            t1 = R.op("dve", lambda e: e.bn_aggr(out=mv_[:], in_=st_[:, 0:6 * nch]))
            R.op("act", lambda e: e.activation(out=rstd_[:], in_=mv_[:, 1:2], func=AF.Ln, bias=epst[:, 0:1], scale=1.0), [t1])
            t2 = R.op("act", lambda e: e.activation(out=rstd_[:], in_=rstd_[:], func=AF.Exp, scale=-0.5))
            t3 = R.op("dve", lambda e: e.scalar_tensor_tensor(out=nmr_[:], in0=mv_[:, 0:1], scalar=-1.0, in1=rstd_[:],
                                                              op0=ALU.mult, op1=ALU.mult), [t2])
            return t3

        def ln_stats_a(src, deps, sb_=None):
            st_, mv_, rstd_, nmr_ = sb_ if sb_ is not None else (st, mv, rstd, nmr)
            nch = src.shape[-1] // 512
            if sb_ is None:
                deps = [deps, state.get("stat_free")]
            for c in range(nch):
                R.op("dve", lambda e, c=c: e.bn_stats(out=st_[:, c * 6:(c + 1) * 6], in_=src[:, c * 512:(c + 1) * 512]), deps)
            return R.op("dve", lambda e: e.bn_aggr(out=mv_[:], in_=st_[:, 0:6 * nch]))

        def ln_stats_b(t1, sb_=None):
            st_, mv_, rstd_, nmr_ = sb_ if sb_ is not None else (st, mv, rstd, nmr)
            R.op("act", lambda e: e.activation(out=rstd_[:], in_=mv_[:, 1:2], func=AF.Ln, bias=epst[:, 0:1], scale=1.0), [t1])
            t2 = R.op("act", lambda e: e.activation(out=rstd_[:], in_=rstd_[:], func=AF.Exp, scale=-0.5))
            return R.op("dve", lambda e: e.scalar_tensor_tensor(out=nmr_[:], in0=mv_[:, 0:1], scalar=-1.0, in1=rstd_[:],
                                                                op0=ALU.mult, op1=ALU.mult), [t2])

        def ln_stats_b_act(t1, sb_):
            st_, mv_, rstd_, nmr_ = sb_
            R.op("act", lambda e: e.activation(out=rstd_[:], in_=mv_[:, 1:2], func=AF.Ln, bias=epst[:, 0:1], scale=1.0), [t1])
            R.op("act", lambda e: e.activation(out=rstd_[:], in_=rstd_[:], func=AF.Exp, scale=-0.5))
            R.op("act", lambda e: e.activation(out=nmr_[:], in_=mv_[:, 0:1], func=AF.Identity, scale=rstd_[:, 0:1]))
            return R.op("act", lambda e: e.mul(out=nmr_[:], in_=nmr_[:], mul=-1.0))

        def norm_pre_b(src, t1, sb_=None):
            st_, mv_, rstd_, nmr_ = sb_ if sb_ is not None else (st, mv, rstd, nmr)
            t3 = ln_stats_b(t1, sb_) if sb_ is None else ln_stats_b_act(t1, sb_)
            return R.op("act", lambda e: e.activation(out=xn[:], in_=src, func=AF.Identity, bias=nmr_[:, 0:1], scale=rstd_[:, 0:1]),
                        [t3, state["xn_free"]])

        def norm_pre(src, deps, sb_=None):
            return norm_pre_b(src, ln_stats_a(src, deps, sb_), sb_)

        def _tb(grp, c):
            if grp == 0 and c >= 4:
                return bankbf(6)[:, (c - 4) * 128:(c - 3) * 128]
            return bankbf(4)[:, c * 128:(c + 1) * 128]

        def norm_post_a(t4, grp):
            tp_ = None
            for c in range(8):
                tp_ = R.op("pe", lambda e, c=c: e.transpose(out=_tb(grp, c), in_=xn[:, c * 128:(c + 1) * 128], identity=ident[:]),
                           [t4, state["pT_free"], state["pQK_free"] if grp == 0 else None, t_ident])
            state["xn_free"] = tp_
            return tp_

        def norm_post_b(tp_, grp, mi_sh, mi_sc, dstT_fn):
            te = []
            if grp == 0:
                for c in range(8):
                    if c < 4:
                        te.append(R.op("act", lambda e, c=c: e.activation(
                            out=dstT_fn(c), in_=_tb(grp, c), func=AF.Identity,
                            scale=modT[:, mi_sc, c, 0:1], bias=modT[:, mi_sh, c, 0:1]), [tp_, state["hT_free"]]))
                    else:
                        te.append(R.op("dve", lambda e, c=c: e.tensor_scalar(
                            out=dstT_fn(c), in0=_tb(grp, c),
                            scalar1=modT[:, mi_sc, c, 0:1], scalar2=modT[:, mi_sh, c, 0:1], op0=ALU.mult, op1=ALU.add),
                            [tp_, state["hT_free"]]))
                te = [te[3], te[7]]
                state["pQK_free"] = [state["pQK_free"], te[1]]
            else:
                for c in range(8):
                    R.op("dve", lambda e, c=c: e.tensor_tensor(
                        out=sg_sb[:, 0:128].rearrange("p (b t) -> p b t", t=8),
                        in0=_tb(grp, c).rearrange("p (b t) -> p b t", t=8),
                        in1=modT[:, mi_sc, c, 1:17].unsqueeze(2).broadcast_to([128, 16, 8]), op=ALU.mult),
                        [tp_, state["hT_free"]])
                    te = R.op("dve", lambda e, c=c: e.tensor_tensor(
                        out=dstT_fn(c).rearrange("p (b t) -> p b t", t=8),
                        in0=sg_sb[:, 0:128].rearrange("p (b t) -> p b t", t=8),
                        in1=modT[:, mi_sh, c, 1:17].unsqueeze(2).broadcast_to([128, 16, 8]), op=ALU.add))
            state["pT_free"] = te
            return te

        def norm_post(t4, grp, mi_sh, mi_sc, dstT_fn):
            return norm_post_b(norm_post_a(t4, grp), grp, mi_sh, mi_sc, dstT_fn)

        def norm_T(src, grp, mi_sh, mi_sc, dstT_fn, deps):
            t4 = norm_pre(src, deps)
            return norm_post(t4, grp, mi_sh, mi_sc, dstT_fn)

        def post_ln(psrc, xres, gate_ap, g_ap, b_ap, dst, deps, dst_free=None, gelu_hint=False, on_pool=True):
            if gate_ap is None:
                R.op("dve", lambda e: e.scalar_tensor_tensor(out=tmpA[:], in0=xres, scalar=ALPHA, in1=psrc,
                                                             op0=ALU.mult, op1=ALU.add), [deps, state["tmpA_free"]])
                tpz = ("dve", R.cnt["dve"])
            else:
                R.op("dve", lambda e: e.tensor_tensor(out=tmpA[:], in0=psrc, in1=gate_ap, op=ALU.mult),
                     [deps, state["tmpA_free"]])
                tpz = ("dve", R.cnt["dve"])
                R.op("dve", lambda e: e.scalar_tensor_tensor(out=tmpA[:], in0=xres, scalar=ALPHA, in1=tmpA[:],
                                                             op0=ALU.mult, op1=ALU.add))
            txr = ("dve", R.cnt["dve"])
            t3 = ln_stats(tmpA[:], ())
            if gelu_hint:
                R.op("act", lambda e: e.activation(out=dmy[:, 2:3], in_=dmy[:, 3:4], func=AF.Gelu_apprx_tanh))
            if not on_pool:
                R.op("dve", lambda e: e.tensor_scalar(out=tmpA[:], in0=tmpA[:], scalar1=rstd[:, 0:1], scalar2=nmr[:, 0:1],
                                                      op0=ALU.mult, op1=ALU.add), [t3])
                R.op("dve", lambda e: e.tensor_tensor(out=tmpA[:], in0=tmpA[:], in1=g_ap, op=ALU.mult))
                t5 = R.op("dve", lambda e: e.tensor_tensor(out=dst, in0=tmpA[:], in1=b_ap, op=ALU.add), [dst_free])
                state["tmpA_free"] = t5
                state["stat_free"] = t5
                return tpz, txr, t5
            R.op("pool", lambda e: e.tensor_scalar(out=tmpA[:], in0=tmpA[:], scalar1=rstd[:, 0:1], scalar2=nmr[:, 0:1],
                                                   op0=ALU.mult, op1=ALU.add), [t3])
            R.op("pool", lambda e: e.tensor_tensor(out=tmpA[:], in0=tmpA[:], in1=g_ap, op=ALU.mult))
            t5 = R.op("pool", lambda e: e.tensor_tensor(out=dst, in0=tmpA[:], in1=b_ap, op=ALU.add), [dst_free])
            state["tmpA_free"] = t5
            state["stat_free"] = t5
            return tpz, txr, t5

        out_tok = []
        t_kTc = [None]
        t_cvb = [None]

        xin_free = [t_ada, t_ada]
        sb2 = (st2, mv2, rstd2, nmr2)
        sb3 = (st3, mv3, rstd3, nmr3)
        sbP = [(sb("stP%d" % i, [128, 12], F32), sb("mvP%d" % i, [128, 2], F32), sb("rsP%d" % i, [128, 1], F32), sb("nmP%d" % i, [128, 1], F32)) for i in range(4)]
        sbG = (st, mv, sb("rsG", [128, 1], F32), sb("nmG", [128, 1], F32))

        def pe_fill(n, bk, deps=()):
            tk_ = None
            for _ in range(n):
                tk_ = R.op("pe", lambda e, bk=bk: e.matmul(out=bank(bk), lhsT=ident[:], rhs=mask_sb[:, 2, :], start=True, stop=True),
                           [t_mask, t_ident, deps])
            return tk_

        pending = []
        pending_pe = []

        def flush_pending_pe():
            while pending_pe:
                pending_pe.pop(0)()

        def flush_pending():
            while pending:
                pending.pop(0)()

        def front_load(T):
            T["t_x"] = R.dma("sp", "xin%d" % T["xi"], xinb[:, T["xi"], :], T["src"], [xin_free[T["xi"]]])

        def front_pre_a(T):
            T["t1"] = ln_stats_a(xinb[:, T["xi"], :], [T["t_x"]], sb2)

        def front_pre_b(T):
            T["t4"] = norm_pre_b(xinb[:, T["xi"], :], T["t1"], sb2)
            if T["kind"] == "halo":
                xin_free[T["xi"]] = T["t4"]

        def front_pre(T):
            front_pre_a(T)
            front_pre_b(T)

        def front_post_a(T):
            T["tp"] = norm_post_a(T["t4"], 1 if T["kind"] == "sample" else 0)

        def front_post_b(T):
            grp = 1 if T["kind"] == "sample" else 0
            T["t_h"] = norm_post_b(T["tp"], grp, 0, 1, lambda c: hT[:, c, :])

        def front_post(T):
            front_post_a(T)
            front_post_b(T)

        def mixer_tile(T, N=None, F=None, hook=None):
            kind, ti, slot, prev_slot, x1dst, x1free = T["kind"], T["ti"], T["slot"], T["prev_slot"], T["x1dst"], T["x1free"]
            xin_ = xinb[:, T["xi"], :]
            grp = 1 if kind == "sample" else 0
            if "t_h" not in T:
                front_load(T)
                front_pre(T)
                front_post(T)
            t_h = T["t_h"]
            if N is not None:
                front_load(N)
            groups = [(0, 0, 512), (1, 512, 256), (2, 768, 512), (3, 1280, 512)]
            if kind == "halo":
                groups = [(1, 512, 256)]
            tz = None
            tzg = {}
            dep01 = state.pop("pZ01_once", None) or state["pZ_free"]
            for (bk, c0, w) in groups:
                for k in range(8):
                    tz = R.op("pe", lambda e, bk=bk, c0=c0, w=w, k=k: e.matmul(
                        out=bank(bk)[:, 0:w], lhsT=hT[:, k, :], rhs=w_in_sb[:, k, c0:c0 + w], start=(k == 0), stop=(k == 7)),
                        [t_h, t_win, dep01 if bk < 2 else state["pZ_free"]])
                tzg[bk] = tz
            state["hT_free"] = tz
            flush_pending_pe()
            if hook is not None:
                hook()
            if kind != "halo":
                tf_ = pe_fill(32 if hook is None else 16, 4, [state["pT_free"]])
                state["pT_free"] = [state["pT_free"], tf_]
            if kind != "halo":
                R.op("act", lambda e: e.copy(out=qkf[:, 0:512], in_=bank(0)), [tzg[0], state["qkf_free"]])
            R.op("act", lambda e: e.copy(out=qkf[:, 512:640], in_=bank(1)[:, 0:128]), [tzg[1], state["qkf_free"]])
            t_v = R.op("act", lambda e: e.copy(out=vf[:], in_=bank(1)[:, 128:256]), [tzg[1], state["qkf_free"]])
            if kind != "halo":
                R.op("act", lambda e: e.activation(out=u_sb[:], in_=bank(2), func=AF.Gelu_apprx_tanh), [tzg[2], state["u_free"]])
                t_gl = R.op("act", lambda e: e.activation(out=gvf[:], in_=bank(3), func=AF.Gelu_apprx_tanh), [tzg[3], state["gv_free"]])
                t_zfree = t_gl
                R.op("act", lambda e: e.activation(out=dmy[:, 0:1], in_=dmy[:, 1:2], func=AF.Exp))
            else:
                t_zfree = t_v
            h0 = 8 if kind == "halo" else 0
            nh = 10 - h0
            qv = qkf[:].rearrange("p (h d) -> p h d", d=64)[:, h0:10, :]
            x1_ = qv[:, :, 0:8]
            x2_ = qv[:, :, 8:16]
            cs = cos_sb[:, ti, :].unsqueeze(1).broadcast_to([128, nh, 8])
            sn = sin_sb[:, ti, :].unsqueeze(1).broadcast_to([128, nh, 8])

            def rtv(i):
                return rt[:, i, 0:nh * 8].rearrange("p (h d) -> p h d", d=8)
            R.op("dve", lambda e: e.tensor_tensor(out=rtv(0), in0=x1_, in1=cs, op=ALU.mult), [t_v, t_trig])
            R.op("dve", lambda e: e.tensor_tensor(out=rtv(1), in0=x2_, in1=sn, op=ALU.mult))
            R.op("dve", lambda e: e.tensor_tensor(out=rtv(2), in0=x2_, in1=cs, op=ALU.mult))
            R.op("dve", lambda e: e.tensor_tensor(out=rtv(3), in0=x1_, in1=sn, op=ALU.mult))
            R.op("dve", lambda e: e.tensor_tensor(out=x1_, in0=rtv(0), in1=rtv(1), op=ALU.subtract))
            t_rot = R.op("dve", lambda e: e.tensor_tensor(out=x2_, in0=rtv(2), in1=rtv(3), op=ALU.add))
            t1g = ln_stats_a(gvf[:], [t_gl], sbG) if kind != "halo" else None
            flush_pending()
            if F is not None:
                F["t1"] = ln_stats_a(F["src"], [F["tok"]], sb3)
            t_qkb = R.op("act", lambda e: e.copy(out=qkb[:, h0 * 64:640], in_=qkf[:, h0 * 64:640]), [t_rot])
            t_vx = R.op("act", lambda e: e.copy(out=vext[:, slot, :, 0:64], in_=vf[:].rearrange("p (h d) -> p h d", d=64)),
                        [state["PT_free"]])
            t_kvout = []
            if kind == "sample":
                for b in range(16):
                    t_kvout.append(R.dma("sp", "okv", outk[b, 120:128, :], qkf[b * 8:(b + 1) * 8, 512:640], [t_rot]))
                    t_kvout.append(R.dma("sp", "okv", outv[b, 120:128, :], vf[b * 8:(b + 1) * 8, :], [t_v]))
            if kind == "prompt" and ti == NPT:
                t_kvout.append(R.dma("sp", "okv", kwin, qkf[:, 512:640], [t_rot]))
                t_kvout.append(R.dma("sp", "okv", vwin, vf[:], [t_v]))
            state["qkf_free"] = [t_qkb, t_vx] + t_kvout[-1:]
            tt_ = None
            ttk = None
            for h in ([8, 9] + list(range(h0, 8))):
                if h < 8:
                    o_ = bankbf(5)[0:64, h * 128:(h + 1) * 128]
                else:
                    o_ = bankbf(6)[0:64, (h - 8) * 128:(h - 7) * 128]
                tt_ = R.op("pe", lambda e, h=h, o_=o_: e.transpose(out=o_, in_=qkb[:, h * 64:(h + 1) * 64], identity=ident[:]),
                           [t_qkb, state["pQK_free"]])
                if h == 9:
                    ttk = tt_
            if kind != "halo":
                tf_ = pe_fill(10, 4, [state["pT_free"]])
                state["pT_free"] = [state["pT_free"], tf_]
            t_kT = R.op("act", lambda e: e.copy(out=kT[:, slot, :], in_=bankbf(6)[0:64, 0:256]), [ttk, state["PT_free"]])
            if kind == "halo":
                state["pQK_free"] = t_kT
                state["pZ_free"] = t_zfree
                if N is not None:
                    front_pre(N)
                    front_post(N)
                return
            t_qT = R.op("act", lambda e: e.copy(out=qT[:], in_=bankbf(5)[0:64, :]), [tt_, state["qT_free"]])
            if kind == "sample":
                for kvh_ in range(2):
                    t_qT = R.op("act", lambda e, kvh_=kvh_: e.copy(
                        out=qTs[:, kvh_, :, :].rearrange("p b (g t) -> p b g t", t=8),
                        in_=bankbf(5)[0:64, kvh_ * 512:(kvh_ + 1) * 512].rearrange("p (g b t) -> p b g t", g=4, t=8)))
            state["pQK_free"] = t_qT

            if kind == "prompt":
                blks = [(prev_slot, 0 if ti == 1 else 1), (slot, 2)]
            else:
                blks = [(slot, 3)]
            nb = len(blks)
            tsc = None
            jj = 0
            sc_banks = []
            tsck = {}
            for kvh in range(2):
                for (ks, mi_) in blks:
                    bk = jj
                    jj += 1
                    sc_banks.append((bk, kvh, ks))
                    R.op("pe", lambda e, bk=bk, kvh=kvh, ks=ks: e.matmul(
                        out=bank(bk), lhsT=kT[:, ks, kvh * 128:(kvh + 1) * 128], rhs=qT[:, kvh * 512:(kvh + 1) * 512],
                        start=True, stop=False), [t_kT, t_qT, t_zfree, state["pZ_free"]])
                    tsc = R.op("pe", lambda e, bk=bk, mi_=mi_: e.matmul(
                        out=bank(bk), lhsT=ident[:], rhs=mask_sb[:, mi_, :], start=False, stop=True), [t_mask])
                tsck[kvh] = tsc
            if kind == "sample":
                for b in range(16):
                    for kvh in range(2):
                        col = b * 64 + kvh * 32
                        tsc = R.op("pe", lambda e, b=b, kvh=kvh, col=col: e.matmul(
                            out=ps[:, 1024 + col:1024 + col + 32],
                            lhsT=kTc[:, b, kvh, :],
                            rhs=qTs[:, kvh, b, :],
                            start=True, stop=True), [t_kTc[0]])
            state["qT_free"] = tsc
            tf_ = pe_fill(8, 4, [state["pT_free"]])
            state["pT_free"] = [state["pT_free"], tf_]

            t3 = ln_stats_b_act(t1g, sbG)
            t4 = R.op("act", lambda e: e.activation(out=gvf[:], in_=gvf[:], func=AF.Identity, bias=sbG[3][:, 0:1], scale=sbG[2][:, 0:1]), [t3])
            R.op("dve", lambda e: e.tensor_tensor(out=gvf[:], in0=gvf[:], in1=sgu_gb[:, 0:512], op=ALU.mult), [t4, t_sgu])
            t5 = R.op("dve", lambda e: e.tensor_tensor(out=gvf[:], in0=gvf[:], in1=sgu_gb[:, 512:1024], op=ALU.add))
            t_sgvo = None
            if kind == "sample":
                t_sgvo = R.dma("sp", "osgv", sgv, gvf[:], [t5])
                out_tok.append(t_sgvo)

            if N is not None:
                front_pre_a(N)
            texp = None
            texpk = {}
            for (bk, kvh, ks) in sc_banks:
                texp = R.op("act", lambda e, bk=bk: e.activation(out=PT[:, bk * 512:(bk + 1) * 512], in_=bank(bk),
                                                                 func=AF.Exp, scale=0.125),
                            [tsck[kvh] if kind == "prompt" else tsc, state["PT_free"]])
                texpk[kvh] = texp
            if kind == "sample":
                R.op("act", lambda e: e.activation(out=PTc[:], in_=bank(2, 2), func=AF.Exp, scale=0.125), [tsc])
                tpc = ("act", R.cnt["act"])
                texp = R.op("dve", lambda e: e.tensor_tensor(
                    out=PTc[:].rearrange("p (a t) -> p a t", t=8), in0=PTc[:].rearrange("p (a t) -> p a t", t=8),
                    in1=m01c_sb[:].unsqueeze(1).broadcast_to([128, 128, 8]), op=ALU.mult), [tpc, t_m01])
                texp = [texp, tpc]
            t6 = R.op("act", lambda e: e.copy(out=gvb[:], in_=gvf[:]), [t5, state["gv_free"]])
            if N is not None:
                front_pre_b(N)
            if F is not None:
                t3f = ln_stats_b_act(F["t1"], sb3)
                F["t4"] = R.op("act", lambda e: e.activation(out=xn2[:], in_=F["src"], func=AF.Identity, bias=nmr3[:, 0:1],
                                                             scale=rstd3[:, 0:1]), [t3f, state.get("xn2_free")])
            tpv = None
            for h in range(8):
                kvh, g = h // 4, h % 4
                ocol = (h // 4) * 512 + (h % 4) * 65
                for bi, (ks, mi_) in enumerate(blks):
                    bk = kvh * nb + bi
                    tpv = R.op("pe", lambda e, bk=bk, g=g, ks=ks, kvh=kvh, ocol=ocol, bi=bi: e.matmul(
                        out=ps[:, ocol:ocol + 65], lhsT=PT[:, bk * 512 + g * 128:bk * 512 + (g + 1) * 128],
                        rhs=vext[:, ks, kvh, :], start=(bi == 0), stop=(bi == nb - 1)),
                        [texpk[kvh] if kind == "prompt" else texp, t_vx])
            state["PT_free"] = tpv
            tm = None
            for h in range(8):
                tm = R.op("pe", lambda e, h=h: e.matmul(out=bank(7)[:, h * 64:(h + 1) * 64], lhsT=wsT_sb[:, grp, h, :],
                                                        rhs=gvb[:, h * 64:(h + 1) * 64], start=True, stop=True),
                          [t6, t_ws, state["pS_free"]])
            state["gv_free"] = [tm, t_sgvo]
            if kind == "prompt":
                tf_ = pe_fill(10, 5, [state["pQK_free"]])
                state["pQK_free"] = [state["pQK_free"], tf_]
            if N is not None:
                front_post_a(N)
            Oview = ps[:, 0:1024].rearrange("p (a n) -> p a n", a=2)[:, :, 0:260].rearrange("p a (g c) -> p a g c", c=65)
            if kind == "sample":
                for b in range(16):
                    for kvh in range(2):
                        col = b * 64 + kvh * 32
                        tpv = R.op("pe", lambda e, b=b, kvh=kvh, col=col: e.matmul(
                            out=ps[0:65, 1024 + col:1024 + col + 32], lhsT=cvb[:, b, kvh, :], rhs=PTc[:, col:col + 32],
                            start=True, stop=True), [texp, t_cvb[0]])
                t_oc = None
                for kvh_ in range(2):
                    t_oc = R.op("act", lambda e, kvh_=kvh_: e.copy(
                        out=OcT_sb[0:65, kvh_ * 512:(kvh_ + 1) * 512].rearrange("p (g b t) -> p b g t", g=4, t=8),
                        in_=ps[0:65, 1024:2048].rearrange("p (b k r) -> p b k r", k=2, r=32)[:, :, kvh_, :].rearrange("p b (g t) -> p b g t", t=8)),
                        [tpv, state["tmpA_free"]])
                ttr = None
                for h in range(8):
                    src_ = OcT_sb[0:65, h * 128:(h + 1) * 128]
                    ocol = (5 + h // 4) * 512 + (h % 4) * 65
                    ttr = R.op("pe", lambda e, src_=src_, ocol=ocol: e.transpose(
                        out=ps[:, ocol:ocol + 65], in_=src_, identity=identf[0:65, 0:65]), [t_oc, state["pQK_free"]])
                Ocv = ps[:, 2560:3584].rearrange("p (a n) -> p a n", a=2)[:, :, 0:260].rearrange("p a (g c) -> p a g c", c=65)
                t_o1 = R.op("act", lambda e: e.copy(out=Osum[:], in_=Ocv), [ttr])
                t_o2 = R.op("dve", lambda e: e.tensor_tensor(out=Osum[:], in0=Osum[:], in1=Oview, op=ALU.add), [t_o1, tpv])
                state["pQK_free"] = t_o1
                Osrc = Osum[:]
                tpv = t_o2
            else:
                Osrc = Oview
            R.op("dve", lambda e: e.tensor_tensor(out=den[:].rearrange("p (a g) -> p a g", a=2), in0=Osrc[:, :, :, 64],
                                                  in1=expsink[:].rearrange("p (a g) -> p a g", a=2), op=ALU.add), [tpv, t_es])
            R.op("dve", lambda e: e.reciprocal(out=rden[:], in_=den[:]))
            t_att = R.op("dve", lambda e: e.tensor_tensor(
                out=mixcat[:, 0:512].rearrange("p (a g d) -> p a g d", a=2, g=4),
                in0=Osrc[:, :, :, 0:64],
                in1=rden[:].rearrange("p (a g) -> p a g", a=2).unsqueeze(3).broadcast_to([128, 2, 4, 64]), op=ALU.mult),
                [state["mixcat_free"]])
            R.op("dve", lambda e: e.tensor_tensor(
                out=sg_sb[:].rearrange("p (h d) -> p h d", d=64), in0=bank(7).rearrange("p (h d) -> p h d", d=64),
                in1=bsT_sb[:, grp, :].unsqueeze(2).broadcast_to([128, 8, 64]), op=ALU.add), [tm, t_bsT])
            tsg = R.op("dve", lambda e: e.tensor_tensor(out=mixcat[:, 512:1024], in0=sg_sb[:], in1=u_sb[:], op=ALU.mult),
                       [state["mixcat_free"]])
            state["pS_free"] = tsg
            state["u_free"] = tsg
            tpa = None
            for c in range(4):
                tpa = R.op("pe", lambda e, c=c: e.transpose(out=bankbf(5)[:, c * 128:(c + 1) * 128],
                                                            in_=mixcat[:, c * 128:(c + 1) * 128], identity=ident[:]),
                           [t_att, state["pQK_free"]])
            tp_ = None
            for c in range(4, 8):
                tp_ = R.op("pe", lambda e, c=c: e.transpose(out=bankbf(7)[:, (c - 4) * 128:(c - 3) * 128],
                                                            in_=mixcat[:, c * 128:(c + 1) * 128], identity=ident[:]),
                           [tsg, state["pS_free"]])
            state["mixcat_free"] = tp_
            t_mTa = R.op("act", lambda e: e.copy(out=mixcatT[:, 0:4, :].rearrange("p k t -> p (k t)"), in_=bankbf(5)[:, 0:512]),
                         [tpa, state["mixcatT_free"]])
            t_mT = R.op("act", lambda e: e.copy(out=mixcatT[:, 4:8, :].rearrange("p k t -> p (k t)"), in_=bankbf(7)[:, 0:512]), [tp_])
            state["pQK_free"] = t_mTa
            state["pS_free"] = [state["pS_free"], t_mT]
            two = None
            for half in range(2):
                for k in range(8):
                    two = R.op("pe", lambda e, half=half, k=k: e.matmul(
                        out=bank(2 + half), lhsT=mixcatT[:, k, :], rhs=w_o_sb[:, k, half * 512:(half + 1) * 512],
                        start=(k == 0), stop=(k == 7)), [t_mTa if k < 4 else t_mT, t_wo, wo_tok[0], t_att])
            state["mixcatT_free"] = two
            if F is not None:
                def _ftr(F=F, tsg=tsg):
                    tpf_ = None
                    for c in range(8):
                        tpf_ = R.op("pe", lambda e, c=c: e.transpose(out=bankbf(7)[:, c * 128:(c + 1) * 128],
                                                                     in_=xn2[:, c * 128:(c + 1) * 128], identity=ident[:]),
                                    [F["t4"], tsg, state["pS_free"], t_ident])
                    state["xn2_free"] = tpf_
                    F["tpf"] = tpf_
                pending_pe.append(_ftr)
            if N is not None:
                front_post_b(N)
            tpz, txr, t5 = post_ln(bank(2, 2), xin_, (gates[:, 1, :] if kind == "sample" else None), lnv[:, 0, :], lnv[:, 1, :], x1dst, [two, t_lnv[0], t_lnv[1]], dst_free=x1free, gelu_hint=(N is not None), on_pool=False)
            state["pZ_free"] = tpz
            state["pZ01_once"] = t_att
            xin_free[T["xi"]] = txr
            if F is not None:
                def _evac(F=F):
                    tpf = F["tpf"]
                    tef = None
                    for c in range(8):
                        tef = R.op("dve", lambda e, c=c: e.tensor_scalar(
                            out=F["dst"](c), in0=bankbf(7)[:, c * 128:(c + 1) * 128],
                            scalar1=modT[:, 3, c, 0:1], scalar2=modT[:, 2, c, 0:1], op0=ALU.mult, op1=ALU.add),
                            [tpf, state.get("h2T_free")])
                    state["pS_free"] = [state["pS_free"], tef]
                    F["th"] = tef
                pending.append(_evac)
            return t5

        def ffn_group(ntiles, grp, x1_toks, y_dsts, ysem, h2_toks=None, post_hook=None, pre=None):
            N = ntiles * 128
            th = []
            for t_ in range(ntiles):
                if h2_toks is not None and h2_toks[t_] is not None:
                    th.append(h2_toks[t_])
                else:
                    th.append(norm_T(x1g[:, t_, :], grp, 2, 3, lambda c, t_=t_: h2T[:, c, t_ * 128:(t_ + 1) * 128],
                                     [x1_toks[t_], state.get("h2T_free")]))
            thid = None
            for f2 in range(NF // 2):
                if pre is not None and f2 in pre:
                    wg, wu, tl, s_ = pre[f2]
                else:
                    s_, tl = ring_load([
                        (lambda sl: sl[:, 0:2048].rearrange("p (k n) -> p k n", k=8),
                         w_gate[:, f2 * 256:(f2 + 1) * 256].rearrange("(k p) n -> p k n", p=128)),
                        (lambda sl: sl[:, 2048:4096].rearrange("p (k n) -> p k n", k=8),
                         w_up[:, f2 * 256:(f2 + 1) * 256].rearrange("(k p) n -> p k n", p=128))])
                    wg = ring[:, s_, 0:2048].rearrange("p (k n) -> p k n", k=8)
                    wu = ring[:, s_, 2048:4096].rearrange("p (k n) -> p k n", k=8)
                tlast = None
                for j in range(2):
                    f = f2 * 2 + j
                    bA, bB = (0, 1) if f % 2 == 0 else (2, 3)
                    key = "pF%d" % (f % 2)
                    for k in range(8):
                        R.op("pe", lambda e, k=k, j=j, bA=bA, wg=wg: e.matmul(
                            out=bank(bA)[:, 0:N], lhsT=wg[:, k, j * 128:(j + 1) * 128], rhs=h2T[:, k, 0:N],
                            start=(k == 0), stop=(k == 7)), [tl, th, state.get(key), state["pZ_free"]])
                    tg = ("pe", R.cnt["pe"])
                    for k in range(8):
                        R.op("pe", lambda e, k=k, j=j, bB=bB, wu=wu: e.matmul(
                            out=bank(bB)[:, 0:N], lhsT=wu[:, k, j * 128:(j + 1) * 128], rhs=h2T[:, k, 0:N],
                            start=(k == 0), stop=(k == 7)))
                    tu = ("pe", R.cnt["pe"])
                    tlast = tu
                    ts = R.op("act", lambda e, bA=bA: e.activation(out=sg_sb[:, 0:N], in_=bank(bA)[:, 0:N], func=AF.Silu),
                              [tg, thid])
                    thid = R.op("dve", lambda e, f=f, bB=bB: e.tensor_tensor(out=hidT[:, f, 0:N], in0=sg_sb[:, 0:N],
                                                                              in1=bank(bB)[:, 0:N], op=ALU.mult),
                                [ts, tu, state.get("hidT_free")])
                    state[key] = thid
                if s_ is not None:
                    slot_free[s_] = tlast
            state["h2T_free"] = tlast
            td = None
            for f2 in range(NF // 2):
                s_, tl = ring_load([(lambda sl: sl[:, 0:2048].rearrange("p (j n) -> p j n", j=2),
                                     w_down[f2 * 256:(f2 + 1) * 256, :].rearrange("(j p) n -> p j n", p=128))])
                wd = ring[:, s_, 0:2048].rearrange("p (j n) -> p j n", j=2)
                for j in range(2):
                    f = f2 * 2 + j
                    for t_ in range(ntiles):
                        for half in range(2):
                            td = R.op("pe", lambda e, f=f, j=j, t_=t_, half=half, wd=wd: e.matmul(
                                out=bank(2 * t_ + half), lhsT=hidT[:, f, t_ * 128:(t_ + 1) * 128],
                                rhs=wd[:, j, half * 512:(half + 1) * 512], start=(f == 0), stop=(f == NF - 1)),
                                [tl, thid, state["pT_free"], state["pQK_free"], state["pS_free"], state["pZ_free"],
                                 state.get("pF0"), state.get("pF1")])
                slot_free[s_] = td
            state["hidT_free"] = td
            last = None
            t1s, t4s = {}, {}

            def stage_a(t_):
                nonlocal last
                xt = x1g[:, t_, :]
                R.op("dve", lambda e: e.tensor_tensor(out=tmpA[:], in0=bank(2 * t_, 2), in1=gates[:, 2 + grp, :], op=ALU.mult),
                     [td, state["tmpA_free"]])
                last = ("dve", R.cnt["dve"])
                R.op("dve", lambda e: e.scalar_tensor_tensor(out=xt, in0=xt, scalar=ALPHA, in1=tmpA[:], op0=ALU.mult, op1=ALU.add))
                state["tmpA_free"] = ("dve", R.cnt["dve"])
                t1s[t_] = ln_stats_a(xt, (), sbP[t_])

            def stage_b(t_):
                xt = x1g[:, t_, :]
                t3 = ln_stats_b_act(t1s[t_], sbP[t_])
                t4s[t_] = R.op("act", lambda e: e.activation(out=xt, in_=xt, func=AF.Identity, bias=sbP[t_][3][:, 0:1],
                                                             scale=sbP[t_][2][:, 0:1]), [t3])

            def stage_c(t_):
                xt = x1g[:, t_, :]
                R.op("dve", lambda e: e.tensor_tensor(out=xt, in0=xt, in1=lnv[:, 2, :], op=ALU.mult), [t4s[t_], t_lnv[2]])
                t5 = R.op("dve", lambda e: e.tensor_tensor(out=xt, in0=xt, in1=lnv[:, 3, :], op=ALU.add), [t_lnv[3]])
                ty = R.dma("sp", "yout%d" % t_, y_dsts[t_], xt, [t5])
                x1g_free[t_] = ty
                out_tok.append(ty)

            order = []
            for step in range(ntiles + 2):
                if step < ntiles:
                    order.append(("a", step))
                if 0 <= step - 1 < ntiles:
                    order.append(("b", step - 1))
                if 0 <= step - 2 < ntiles:
                    order.append(("c", step - 2))
            for kind_, t_ in order:
                {"a": stage_a, "b": stage_b, "c": stage_c}[kind_](t_)
                if post_hook is not None and kind_ == "a" and t_ == 0:
                    post_hook(("dve", R.cnt["dve"]))
            state.pop("pZ01_once", None)
            for k_ in ("pT_free", "pQK_free", "pS_free", "pZ_free"):
                state[k_] = [state[k_], last] if state[k_] is not None else last
            state["x1g_free"] = ("dve", R.cnt["dve"])

        def SAMPLE_PREP(bank0_free=None):
            t_ckb = R.dma("pool", "c_ck", ckb, ck.rearrange("b j c -> j b c"), [state.get("h2T_free")])
            t_cm = R.op("dve", lambda e: e.memset(cvb, 1.0), [state.get("hidT_free")])
            for h_ in range(2):
                t_cvb[0] = R.dma("pool", "c_cv", cvb[:, :, h_, 0:64], cv[:, :, h_ * 64:(h_ + 1) * 64].rearrange("b j d -> j b d"), [t_cm])
            R.dma("sp", "roll", outk[:, 0:120, :], ck[:, 8:128, :])
            R.dma("sp", "roll", outv[:, 0:120, :], cv[:, 8:128, :])
            tev = [state["pZ_free"], state.get("hidT_free"), bank0_free]
            for r_ in range(4):
                tp_ = None
                for i in range(8):
                    b = r_ * 4 + i // 2
                    kvh = i % 2
                    tp_ = R.op("pe", lambda e, b=b, kvh=kvh, i=i: e.transpose(
                        out=bankbf(0, 1)[0:64, i * 128:(i + 1) * 128], in_=ckb[:, b, kvh * 64:(kvh + 1) * 64], identity=ident[:]),
                        [t_ckb, t_ident, tev])
                tev = R.op("act", lambda e, r_=r_: e.copy(out=kTc[:, r_ * 4:(r_ + 1) * 4, :, :].rearrange("p b h j -> p (b h j)"),
                                                          in_=bankbf(0, 1)[0:64, :]), [tp_])
            t_kTc[0] = tev
            state["pZ_free"] = [state["pZ_free"], tev]
            state.pop("pZ01_once", None)


        tiles = [dict(kind="halo", src=xh, ti=0, slot=0, prev_slot=None, x1dst=None, x1free=None, xi=0)]
        for i in range(16):
            slot = (i + 1) % 2
            tiles.append(dict(kind="prompt", src=xp[i * 128:(i + 1) * 128, :], ti=i + 1, slot=slot, prev_slot=1 - slot,
                              x1dst=x1g[:, i % 4, :], x1free=None, xi=(i + 1) % 2, slot4=i % 4))
        tiles.append(dict(kind="sample", src=xs, ti=17, slot=0, prev_slot=None, x1dst=x1g[:, 0, :], x1free=None, xi=1, slot4=0))
        mixer_tile(tiles[0], tiles[1])
        for g_ in range(4):
            toks = []
            fds = []
            for t_ in range(4):
                i = 1 + g_ * 4 + t_
                tiles[i]["x1free"] = x1g_free[t_]
                Fd = None
                if t_ >= 1 and (g_ != 0 or t_ == 3):
                    Fd = dict(src=x1g[:, t_ - 1, :], tok=toks[t_ - 1],
                              dst=(lambda c, tt=t_ - 1: h2T[:, c, tt * 128:(tt + 1) * 128]))
                hk = (lambda st_=i - 1: ada_deferred(st_)) if i in (1, 2, 3, 4) else None
                toks.append(mixer_tile(tiles[i], tiles[i + 1], Fd, hk))
                fds.append(Fd)
            flush_pending_pe()
            flush_pending()
            h2_toks = [fds[t_ + 1]["th"] if (t_ < 3 and fds[t_ + 1] is not None) else None for t_ in range(4)]
            if g_ == 3:
                wo_tok[0] = R.dma("pool", "w_o2", w_o_sb[:], w_o.rearrange("(k p) n -> p k n", p=128), [state["mixcatT_free"]])
            ffn_group(4, 0, toks, [yp[(g_ * 4 + t_) * 128:(g_ * 4 + t_ + 1) * 128, :] for t_ in range(4)], "y", h2_toks,
                      post_hook=(SAMPLE_PREP if g_ == 3 else None))
        spre = {}

        def _gsrc(f2):
            return w_gate[:, f2 * 256:(f2 + 1) * 256].rearrange("(k p) n -> p k n", p=128)

        def _usrc(f2):
            return w_up[:, f2 * 256:(f2 + 1) * 256].rearrange("(k p) n -> p k n", p=128)

        def _pf(f2, gview, uview, deps):
            R.dma("pool", "pfx%d" % f2, gview, _gsrc(f2), deps)
            tok = R.dma("pool", "pfx%d" % f2, uview, _usrc(f2))
            spre[f2] = (gview, uview, tok, None)

        for f2 in range(3):
            s_, tl = ring_load([
                (lambda sl: sl[:, 0:2048].rearrange("p (k n) -> p k n", k=8), _gsrc(f2)),
                (lambda sl: sl[:, 2048:4096].rearrange("p (k n) -> p k n", k=8), _usrc(f2))])
            spre[f2] = (ring[:, s_, 0:2048].rearrange("p (k n) -> p k n", k=8),
                        ring[:, s_, 2048:4096].rearrange("p (k n) -> p k n", k=8), tl, s_)
        xs_bf = x1g[:, 1:3, :].rearrange("p a n -> p (a n)").bitcast(BF16)
        _pf(3, xs_bf[:, 0:2048].rearrange("p (k n) -> p k n", k=8), xs_bf[:, 2048:4096].rearrange("p (k n) -> p k n", k=8),
            [x1g_free[1], x1g_free[2]])
        dve_now = ("dve", R.cnt["dve"])
        _pf(4, gates[:, 0, :].bitcast(BF16).rearrange("p (k n) -> p k n", k=8),
            gates[:, 2, :].bitcast(BF16).rearrange("p (k n) -> p k n", k=8), [dve_now])

        def _pf_win():
            wflat = w_in_sb[:].rearrange("p k n -> p (k n)")
            tzs = ("pe", R.cnt["pe"])
            for i_, f2 in enumerate((5, 6, 7)):
                _pf(f2, wflat[:, i_ * 4096:i_ * 4096 + 2048].rearrange("p (k n) -> p k n", k=8),
                    wflat[:, i_ * 4096 + 2048:(i_ + 1) * 4096].rearrange("p (k n) -> p k n", k=8), [tzs])

        tiles[17]["x1free"] = x1g_free[0]
        tk = mixer_tile(tiles[17], None, None, _pf_win)
        ffn_group(1, 1, [tk], [ys], "y", pre=spre)

        R._waits("sp", out_tok)
        R._waits("sp", [("okv", R.dsem["okv"][1]), ("roll", R.dsem["roll"][1])])

        with nc.Block() as block:
            @block.sync
            def _(e):
                R.replay("sp", e)

            @block.tensor
            def _(e):
                R.replay("pe", e)

            @block.scalar
            def _(e):
                R.replay("act", e)

            @block.vector
            def _(e):
                R.replay("dve", e)

            @block.gpsimd
            def _(e):
                R.replay("pool", e)
        if R.track and decisions is not None:
            R.check_psum()
    if decisions == "auto":
        return nc
    return nc, R


_NC = None


def _host_consts():
    tril = np.tril(np.ones((128, 128), np.float32))
    trilT_p = np.ascontiguousarray(tril.T)
    blk = np.zeros((128, 128), np.float32)
    for b in range(16):
        blk[b * 8:(b + 1) * 8, b * 8:(b + 1) * 8] = tril[:8, :8].T
    s_idx = np.arange(128)[:, None]
    q_idx = np.arange(128)[None, :]
    maskP = np.where(s_idx > q_idx, 0.0, NEG).astype(np.float32)
    maskC = np.where(s_idx <= q_idx, 0.0, NEG).astype(np.float32)
    sb_, st_ = s_idx // 8, s_idx % 8
    qb_, qt_ = q_idx // 8, q_idx % 8
    maskS = np.where((sb_ == qb_) & (st_ <= qt_), 0.0, NEG).astype(np.float32)
    m01c = (np.arange(128)[:, None] > np.arange(8)[None, :]).astype(np.float32)
    invf = (500000.0 ** (-np.arange(8, dtype=np.float32) * 2.0 / 16.0)).astype(np.float32)
    return trilT_p, blk, maskP, maskC, maskS, m01c, invf


def _prep(x_prompt, x_sample, cache_k_win, cache_v_win, c_prompt, c_sample,
          w_ada, b_ada, w_in, attn_sinks, sgu_ln_g, sgu_ln_b, w_s, b_s, w_o,
          ln1_g, ln1_b, w_gate, w_up, w_down, ln2_g, ln2_b):
    f32 = np.float32
    A = lambda a: np.ascontiguousarray(np.asarray(a, dtype=f32))
    x_prompt, x_sample = A(x_prompt), A(x_sample)
    ckw, cvw = A(cache_k_win), A(cache_v_win)
    trilT_p, blk, maskP, maskC, maskS, m01c, invf = _host_consts()
    w_s0 = A(w_s)[0]
    wsT_p = np.ascontiguousarray(w_s0.transpose(2, 0, 1))
    wsT_s = np.zeros((128, 8, 128), f32)
    for b in range(16):
        wsT_s[b * 8:(b + 1) * 8, :, b * 8:(b + 1) * 8] = w_s0[:, :8, :8].transpose(2, 0, 1)
    wsT = np.stack([wsT_p, wsT_s], 0)
    trilT = np.stack([trilT_p, blk], 0)
    b_s0 = A(b_s)[0]
    bsT = np.stack([np.ascontiguousarray(b_s0.T), np.tile(np.ascontiguousarray(b_s0[:, :8].T), (16, 1))], 0)
    b_ada0 = A(b_ada)[0]
    b_adaT = np.ascontiguousarray(b_ada0.reshape(48, 128).T)
    b_gate = np.stack([b_ada0[2048:3072], b_ada0[5120:6144]], 0)
    vecs = np.zeros((6, D), f32)
    vecs[0, :512] = A(sgu_ln_g)[0]
    vecs[0, 512:] = A(sgu_ln_b)[0]
    vecs[1], vecs[2], vecs[3], vecs[4] = A(ln1_g)[0], A(ln1_b)[0], A(ln2_g)[0], A(ln2_b)[0]
    common = {
        "w_ada": A(w_ada)[0], "b_adaT": b_adaT, "b_gate": np.ascontiguousarray(b_gate), "w_in": A(w_in)[0],
        "w_o": A(w_o)[0], "w_gate": A(w_gate)[0], "w_up": A(w_up)[0], "w_down": A(w_down)[0],
        "sinks": A(attn_sinks), "vecs": vecs, "wsT": np.ascontiguousarray(wsT), "trilT": np.ascontiguousarray(trilT),
        "bsT": np.ascontiguousarray(bsT), "m01c": m01c, "invf": np.tile(invf[None, :], (128, 1)).astype(f32),
        "identin": np.eye(128, dtype=f32),
    }
    maskNone = np.full((128, 128), NEG, f32)
    in_maps = []
    for c in range(8):
        b, hf = c // 2, c % 2
        m = dict(common)
        m["xp"] = np.ascontiguousarray(x_prompt[b, hf * 2048:(hf + 1) * 2048])
        m["xh"] = np.ascontiguousarray(x_prompt[b, 2048 - 128:2048]) if hf == 1 else np.zeros((128, D), f32)
        m["xs"] = np.ascontiguousarray(x_sample[c * 16:(c + 1) * 16].reshape(128, D))
        m["c17"] = np.ascontiguousarray(np.concatenate([A(c_prompt)[b:b + 1], A(c_sample)[c * 16:(c + 1) * 16]], 0))
        m["ck"] = np.ascontiguousarray(ckw[0, c * 16:(c + 1) * 16].reshape(16, 128, 128))
        m["cv"] = np.ascontiguousarray(cvw[0, c * 16:(c + 1) * 16].reshape(16, 128, 128))
        mp0 = maskP if hf == 1 else maskNone
        m["masks"] = np.ascontiguousarray(np.stack([np.tile(mp0, (1, 4)), np.tile(maskP, (1, 4)),
                                                    np.tile(maskC, (1, 4)), np.tile(maskS, (1, 4))], 0))
        pos = np.zeros((128, 18), f32)
        pos[:, 0] = hf * 2048 - 128 + np.arange(128)
        for i in range(16):
            pos[:, 1 + i] = hf * 2048 + i * 128 + np.arange(128)
        pos[:, 17] = PAST + (np.arange(128) % 8)
        m["posf"] = pos
        in_maps.append(m)
    return in_maps


def _assemble(r):
    f32 = np.float32
    y_prompt = np.stack([np.concatenate([r[2 * b]["yp"], r[2 * b + 1]["yp"]], 0) for b in range(4)], 0)
    y_sample = np.concatenate([r[c]["ys"].reshape(16, 8, D) for c in range(8)], 0)
    kwp = np.stack([r[2 * b + 1]["kwin"].reshape(128, 2, 64) for b in range(4)], 0)[None]
    vwp = np.stack([r[2 * b + 1]["vwin"].reshape(128, 2, 64) for b in range(4)], 0)[None]
    kws = np.concatenate([r[c]["outk"].reshape(16, 128, 2, 64) for c in range(8)], 0)[None]
    vws = np.concatenate([r[c]["outv"].reshape(16, 128, 2, 64) for c in range(8)], 0)[None]
    sg = np.concatenate([r[c]["sgv"].reshape(16, 8, 512) for c in range(8)], 0)[None]
    return (y_prompt.astype(f32), y_sample.astype(f32), kwp.astype(f32), vwp.astype(f32),
            kws.astype(f32), vws.astype(f32), sg.astype(f32))


def kernel(**inputs):
    global _NC
    in_maps = _prep(**inputs)
    if _NC is None:
        _NC = build_nc()
    res = run_bass_kernel_spmd(_NC, in_maps, core_ids=list(range(8)))
    return _assemble(res.results)
```

```python
import os
from contextlib import ExitStack
import numpy as np
import concourse.bass as bass
import concourse.mybir as mybir
from concourse.bass_utils import run_bass_kernel_spmd

F32 = mybir.dt.float32
BF16 = mybir.dt.bfloat16
I32 = mybir.dt.int32
AF = mybir.ActivationFunctionType
ALU = mybir.AluOpType

D = 1024
DFF = 2816
NF = 22
INW = 1792
ALPHA = 2.0 ** 0.25
EPS = 1e-5
NEG = -30000.0
PAST = 16384
NPT = 16


class Rec:
    def __init__(self, nc, es, decisions=None):
        self.nc = nc
        self.es = es
        self.decisions = decisions
        self.acc_log = {}
        self.eng = {"pe": nc.tensor, "act": nc.scalar, "dve": nc.vector, "pool": nc.gpsimd, "sp": nc.sync}
        self.ops = {k: [] for k in self.eng}
        self.sem = {k: es.enter_context(nc.semaphore("s_" + k)) for k in self.eng}
        self.cnt = {k: 0 for k in self.eng}
        self.waited = {k: {} for k in self.eng}
        self.dsem = {}
        self.psum_acc = {}
        self.track = bool(os.environ.get("KCHECK"))

    def _flat(self, deps, out):
        for d in deps:
            if d is None:
                continue
            if isinstance(d, tuple) and len(d) == 2 and isinstance(d[0], str):
                out.append(d)
            else:
                self._flat(d, out)

    def _waits(self, eng, deps):
        fl = []
        self._flat(deps, fl)
        for key, val in fl:
            if key == eng:
                continue
            if self.waited[eng].get(key, 0) >= val:
                continue
            self.waited[eng][key] = val
            self.ops[eng].append(("wait", key, val))

    def op(self, eng, fn, deps=()):
        self._waits(eng, deps)
        if eng in ("act", "dve", "pool") and self.cnt[eng] > 0:
            need = self.cnt[eng]
            if self.decisions is not None:
                need = self.decisions[eng][self.cnt[eng]]
            if need > 0 and self.waited[eng].get(eng, 0) < need:
                self.waited[eng][eng] = need
                self.ops[eng].append(("wait", eng, need))
        self.cnt[eng] += 1
        self.ops[eng].append(("op", fn))
        return (eng, self.cnt[eng])

    def dma(self, eng, semname, out, in_, deps=()):
        self._waits(eng, deps)
        if semname not in self.dsem:
            self.dsem[semname] = [self.es.enter_context(self.nc.semaphore("d_" + semname)), 0]
        self.dsem[semname][1] += 16
        self.ops[eng].append(("dma", out, in_, semname))
        return (semname, self.dsem[semname][1])

    def semh(self, key):
        return self.sem[key] if key in self.sem else self.dsem[key][0]

    def replay(self, eng, e):
        n = 0
        acc = self.psum_acc.setdefault(eng, [])
        for it in self.ops[eng]:
            if it[0] == "wait":
                e.wait_ge(self.semh(it[1]), it[2])
            elif it[0] == "op":
                bi = it[1](e)
                bi.then_inc(self.sem[eng], 1)
                n += 1
                if self.decisions is None and eng in ("act", "dve", "pool"):
                    self.acc_log.setdefault(eng, []).append((self._regions(bi.ins.ins), self._regions(bi.ins.outs)))
                if self.track:
                    banks = set()
                    for a in list(bi.ins.ins) + list(bi.ins.outs):
                        if getattr(a, "memref", None) == "ps":
                            es_ = 2 if "bfloat16" in str(a.dtype) else 4
                            row = 16384 // es_
                            col0 = a.offset % row
                            ext = 1 + sum((c - 1) * abs(st) for st, c in list(a.ap)[1:])
                            for b in range((col0 * es_) // 2048, ((col0 + ext - 1) * es_) // 2048 + 1):
                                banks.add(b)
                    if banks:
                        acc.append((n, banks))
            else:
                e.dma_start(out=it[1], in_=it[2]).then_inc(self.dsem[it[3]][0], 16)

    @staticmethod
    def _regions(args):
        out = []
        for a in args:
            mr = getattr(a, "memref", None)
            if mr is None:
                continue
            ds = str(a.dtype)
            es_ = 2 if ("bfloat16" in ds or "float16" in ds or "int16" in ds) else (1 if "int8" in ds else 4)
            ap = list(a.ap)
            pst = abs(ap[0][0]) if ap and ap[0][0] != 0 else 0
            off = a.offset % pst if pst > 0 else a.offset
            ext = 1 + sum((c - 1) * abs(st) for st, c in ap[1:])
            out.append((mr, off * es_, (off + ext) * es_))
        return out

    def make_decisions(self):
        def hit(A, B):
            for (m1, l1, h1) in A:
                for (m2, l2, h2) in B:
                    if m1 == m2 and l1 < h2 and l2 < h1:
                        return True
            return False
        dec = {}
        for eng, log in self.acc_log.items():
            d = [0] * (len(log) + 1)
            for i, (rd, wr) in enumerate(log):
                need = 0
                for j in range(i - 1, max(-1, i - 40), -1):
                    prd, pwr = log[j]
                    if hit(pwr, rd) or hit(pwr, wr) or hit(prd, wr):
                        need = j + 1
                        break
                if i >= 40:
                    need = max(need, i - 39)
                d[i] = need
            dec[eng] = d
        return dec

    def check_psum(self):
        engs = list(self.ops)
        pos = {k: 0 for k in engs}
        cnt = {k: 0 for k in engs}
        vc = {k: {x: 0 for x in engs} for k in engs}
        hist = {k: {0: dict(vc[k])} for k in engs}
        dcount = {}
        progress = True
        while progress:
            progress = False
            for eng in engs:
                while pos[eng] < len(self.ops[eng]):
                    it = self.ops[eng][pos[eng]]
                    if it[0] == "wait":
                        key, val = it[1], it[2]
                        if key in cnt:
                            if cnt[key] < val:
                                break
                            src = hist[key][val]
                        else:
                            if dcount.get(key, 0) < val:
                                break
                            src = hist[key][val]
                        for x in engs:
                            vc[eng][x] = max(vc[eng][x], src[x])
                    elif it[0] == "op":
                        cnt[eng] += 1
                        vc[eng][eng] = cnt[eng]
                        hist[eng][cnt[eng]] = dict(vc[eng])
                    else:
                        key = it[3]
                        dcount[key] = dcount.get(key, 0) + 16
                        hist.setdefault(key, {})[dcount[key]] = dict(vc[eng])
                    pos[eng] += 1
                    progress = True
        assert all(pos[k] == len(self.ops[k]) for k in engs), "deadlock in recorded program"
        import bisect
        nbad = 0
        per_bank = {}
        for eng, lst in self.psum_acc.items():
            for idx, banks in lst:
                for b in banks:
                    per_bank.setdefault(b, {}).setdefault(eng, []).append(idx)
        for b, d in per_bank.items():
            for E, xs in d.items():
                for F, fs in d.items():
                    if E == F:
                        continue
                    for x in xs:
                        seen = hist[E][x][F]
                        j = bisect.bisect_right(fs, seen)
                        if j < len(fs) and hist[F][fs[j]][E] < x:
                            nbad += 1
                            if nbad <= 20:
                                print("PSUM UNORDERED bank", b, E, x, "vs", F, fs[j])
        print("check_psum: unordered pairs =", nbad)
        return nbad


def build_nc(decisions="auto"):
    if decisions == "auto":
        _, rec1 = build_nc(decisions=None)
        return build_nc(decisions=rec1.make_decisions())[0]
    nc = bass.Bass("TRN2", target_bir_lowering=False)

    def din(name, shape, dt=F32):
        return nc.dram_tensor(name, list(shape), dt, kind="ExternalInput").ap()

    def dout(name, shape):
        return nc.dram_tensor(name, list(shape), F32, kind="ExternalOutput").ap()

    xp = din("xp", [NPT * 128, D])
    xh = din("xh", [128, D])
    xs = din("xs", [128, D])
    c17 = din("c17", [17, D])
    ck = din("ck", [16, 128, 128])
    cv = din("cv", [16, 128, 128])
    w_ada = din("w_ada", [D, 6 * D])
    b_adaT = din("b_adaT", [128, 48])
    b_gate = din("b_gate", [2, D])
    w_in = din("w_in", [D, INW])
    w_o = din("w_o", [D, D])
    w_gate = din("w_gate", [D, DFF])
    w_up = din("w_up", [D, DFF])
    w_down = din("w_down", [DFF, D])
    sinks = din("sinks", [1, 8])
    vecs = din("vecs", [6, D])
    wsT = din("wsT", [2, 128, 8, 128])
    trilT = din("trilT", [2, 128, 128])
    bsT = din("bsT", [2, 128, 8])
    masks = din("masks", [4, 128, 512])
    m01c = din("m01c", [128, 8])
    posf = din("posf", [128, 18])
    invf = din("invf", [128, 8])
    identin = din("identin", [128, 128])

    yp = dout("yp", [NPT * 128, D])
    ys = dout("ys", [128, D])
    kwin = dout("kwin", [128, 128])
    vwin = dout("vwin", [128, 128])
    outk = dout("outk", [16, 128, 128])
    outv = dout("outv", [16, 128, 128])
    sgv = dout("sgv", [128, 512])

    es = ExitStack()
    with es:
        R = Rec(nc, es, decisions)
        sb_bytes = [0]

        def sb(name, shape, dt=F32):
            n = 1
            for s_ in shape[1:]:
                n *= s_
            sb_bytes[0] += n * (2 if dt == BF16 else 4)
            return es.enter_context(nc.sbuf_tensor(name, list(shape), dt))

        ps = es.enter_context(nc.psum_tensor("ps", [128, 4096], F32))

        def bank(j, n=1):
            return ps[:, j * 512:(j + n) * 512]

        def bankbf(j, n=1):
            return ps[:, j * 512:(j + n) * 512].bitcast(BF16)

        ident = sb("ident", [128, 128], BF16)
        identf = sb("identf", [128, 128], F32)
        w_in_sb = sb("w_in_sb", [128, 8, INW], BF16)
        w_o_sb = sb("w_o_sb", [128, 8, D], BF16)
        gates = sb("gates", [128, 4, D], F32)
        lnv = sb("lnv", [128, 4, D], F32)
        sgu_gb = sb("sgu_gb", [128, D], F32)
        wsT_sb = sb("wsT_sb", [128, 2, 8, 128], BF16)
        bsT_sb = sb("bsT_sb", [128, 2, 8], F32)
        mask_sb = sb("mask_sb", [128, 4, 512], BF16)
        m01c_sb = sb("m01c_sb", [128, 8], F32)
        cos_sb = sb("cos_sb", [128, 18, 8], F32)
        sin_sb = sb("sin_sb", [128, 18, 8], F32)
        modT = sb("modT", [128, 4, 8, 17], F32)
        expsink = sb("expsink", [128, 8], F32)
        epst = sb("epst", [128, 1], F32)
        dmy = sb("dmy", [128, 4], F32)
        badaT_sb = sb("badaT_sb", [128, 48], F32)
        x1g = sb("x1g", [128, 4, D], F32)
        h2T = sb("h2T", [128, 8, 512], BF16)
        hidT = sb("hidT", [128, NF, 512], BF16)
        NSLOT = 3
        ring = sb("ring", [128, NSLOT, 4096], BF16)
        xinb = sb("xinb", [128, 2, D], F32)
        xin = xinb[:, 0, :]
        xn = sb("xn", [128, D], BF16)
        xn2 = sb("xn2", [128, D], BF16)
        hT = sb("hT", [128, 8, 128], BF16)
        qkf = sb("qkf", [128, 640], F32)
        vf = sb("vf", [128, 128], F32)
        qkb = sb("qkb", [128, 640], BF16)
        vext = sb("vext", [128, 2, 2, 65], BF16)
        qT = sb("qT", [64, 1024], BF16)
        kT = sb("kT", [64, 2, 256], BF16)
        u_sb = sb("u_sb", [128, 512], F32)
        gvf = sb("gvf", [128, 512], F32)
        gvb = sb("gvb", [128, 512], BF16)
        PT = sb("PT", [128, 2048], BF16)
        mixcat = sb("mixcat", [128, D], BF16)
        mixcatT = sb("mixcatT", [128, 8, 128], BF16)
        tmpA = sb("tmpA", [128, D], F32)
        st = sb("st", [128, 12], F32)
        mv = sb("mv", [128, 2], F32)
        rstd = sb("rstd", [128, 1], F32)
        nmr = sb("nmr", [128, 1], F32)
        st2 = sb("st2", [128, 12], F32)
        mv2 = sb("mv2", [128, 2], F32)
        rstd2 = sb("rstd2", [128, 1], F32)
        nmr2 = sb("nmr2", [128, 1], F32)
        st3 = sb("st3", [128, 12], F32)
        mv3 = sb("mv3", [128, 2], F32)
        rstd3 = sb("rstd3", [128, 1], F32)
        nmr3 = sb("nmr3", [128, 1], F32)
        den = sb("den", [128, 8], F32)
        rden = sb("rden", [128, 8], F32)
        sg_sb = sb("sg_sb", [128, 512], F32)
        Osum = sb("Osum", [128, 2, 4, 65], F32)
        siluT = sb("siluT", [128, 8, 17], BF16)
        posb = sb("posb", [128, 18], F32)
        invb = sb("invb", [128, 8], F32)
        print("SBUF bytes/partition:", sb_bytes[0])
        assert sb_bytes[0] < 212700, sb_bytes[0]
        HF = hidT[:].rearrange("p f n -> p (f n)")
        kTc = HF[0:64, 0:4096].rearrange("p (b h j) -> p b h j", b=16, h=2)
        cvb = HF[:, 4096:4096 + 2080].rearrange("p (b h c) -> p b h c", b=16, h=2)
        PTc = HF[:, 6656:7680]
        qTs = HF[0:64, 7680:8704].rearrange("p (k b r) -> p k b r", k=2, b=16)
        ang = tmpA[:, 256:400].rearrange("p (a b) -> p a b", b=8)
        kfl = tmpA[:, 512:656].rearrange("p (a b) -> p a b", b=8)
        ckb = h2T[:].rearrange("p k n -> p (k n)")[:, 0:2048].rearrange("p (b c) -> p b c", b=16)
        OcT_sb = tmpA
        c17_sb = x1g[0:17, 2, :]
        silu_bf = mixcat[0:17, :]
        siluTe = HF[:, 0:2048].rearrange("p (g k t) -> p g k t", g=2, k=8)
        bg1 = x1g[:, 1, :]
        rt = HF[:, 8704:9344].bitcast(F32).rearrange("p (a b) -> p a b", a=4)
        kin = tmpA[:, 768:912].bitcast(I32).rearrange("p (a b) -> p a b", b=8)
        trl = PT[:, 0:256].rearrange("p (g t) -> p g t", g=2)
        bg2 = HF[:, 2048:4096].bitcast(F32)

        t_id = R.dma("sp", "c_id", identf[:], identin)
        t_c17 = R.dma("sp", "c_c17", c17_sb, c17)
        t_badaT = R.dma("sp", "c_bada", badaT_sb[:], b_adaT)
        t_bg = R.dma("sp", "c_bg", bg1, b_gate[0:1, :].broadcast_to([128, D]))
        t_bg2 = R.dma("sp", "c_bg2", bg2, b_gate[1:2, :].broadcast_to([128, D]))
        t_pos = R.dma("sp", "c_pos", posb[:], posf)
        t_inv = R.dma("sp", "c_inv", invb[:], invf)
        t_bsT = R.dma("sp", "c_bsT", bsT_sb[:], bsT.rearrange("g p h -> p g h"))
        t_m01 = R.dma("sp", "c_m01", m01c_sb[:], m01c)
        t_snk = R.dma("sp", "c_snk", expsink[:], sinks[0:1, :].broadcast_to([128, 8]))
        t_lnv = [R.dma("sp", "c_lnv%d" % i, lnv[:, i, :], vecs[1 + i:2 + i, :].broadcast_to([128, D])) for i in range(4)]
        t_sgu = R.dma("sp", "c_sgu", sgu_gb[:], vecs[0:1, :].broadcast_to([128, D]))
        t_mask = R.dma("pool", "c_mask", mask_sb[:], masks.rearrange("m p n -> p m n"))
        t_wsT = R.dma("pool", "c_wsT", wsT_sb[:], wsT.rearrange("g p h t -> p g h t"))
        t_trl = R.dma("pool", "c_trl", trl, trilT.rearrange("g p t -> p g t"))

        t = R.op("dve", lambda e: e.tensor_copy(out=ident[:], in_=identf[:]), [t_id])
        t_ident = t
        R.op("dve", lambda e: e.memset(epst[:], EPS))
        R.op("dve", lambda e: e.memset(dmy[:], 0.0))
        t_vmem = R.op("dve", lambda e: e.memset(vext[:], 1.0))
        t_ws = R.op("dve", lambda e: e.tensor_tensor(
            out=wsT_sb[:], in0=wsT_sb[:],
            in1=trl.unsqueeze(2).broadcast_to([128, 2, 8, 128]), op=ALU.mult), [t_wsT, t_trl])
        t_es = R.op("act", lambda e: e.activation(out=expsink[:], in_=expsink[:], func=AF.Exp), [t_snk])

        TWO_PI = float(2.0 * np.pi)

        def trig(dst, shift):
            R.op("dve", lambda e: e.tensor_tensor(
                out=ang, in0=posb[:].unsqueeze(2).broadcast_to([128, 18, 8]),
                in1=invb[:].unsqueeze(1).broadcast_to([128, 18, 8]), op=ALU.mult), [t_pos, t_inv])
            if shift != 0.0:
                R.op("dve", lambda e: e.tensor_scalar(out=ang, in0=ang, scalar1=shift, scalar2=None, op0=ALU.add))
            R.op("dve", lambda e: e.tensor_scalar(out=kfl, in0=ang, scalar1=1.0 / TWO_PI, scalar2=None, op0=ALU.mult))
            R.op("dve", lambda e: e.tensor_copy(out=kin, in_=kfl))
            R.op("dve", lambda e: e.tensor_copy(out=kfl, in_=kin))
            R.op("dve", lambda e: e.scalar_tensor_tensor(out=ang, in0=kfl, scalar=-TWO_PI, in1=ang,
                                                         op0=ALU.mult, op1=ALU.add))
            tt = R.op("dve", lambda e: e.tensor_scalar(out=ang, in0=ang, scalar1=3.14159, scalar2=-3.14159,
                                                       op0=ALU.min, op1=ALU.max))
            ta = R.op("act", lambda e: e.activation(out=dst[:], in_=ang, func=AF.Sin), [tt])
            return ta

        ta = trig(sin_sb, 0.0)
        R._waits("dve", [ta])
        t_trig = trig(cos_sb, float(np.pi / 2))


        R.op("act", lambda e: e.activation(out=silu_bf, in_=c17_sb, func=AF.Silu), [t_c17])
        t_sl = ("act", R.cnt["act"])
        tp = None
        for k in range(8):
            tp = R.op("pe", lambda e, k=k: e.transpose(out=bankbf(4)[:, k * 32:k * 32 + 17], in_=mixcat[0:17, k * 128:(k + 1) * 128],
                                                       identity=ident[0:17, 0:17]), [t_sl, t_ident])
        t_sT = R.op("dve", lambda e: e.tensor_copy(
            out=siluT[:], in_=bankbf(4)[:, 0:256].rearrange("p (k c) -> p k c", c=32)[:, :, 0:17]), [tp])
        R.op("dve", lambda e: e.tensor_copy(out=siluTe[:, 0, :, :], in_=siluT[:, :, 0:1].broadcast_to([128, 8, 128])))
        t_sTe = R.op("dve", lambda e: e.tensor_copy(
            out=siluTe[:, 1, :, :].rearrange("p k (b t) -> p k b t", t=8),
            in_=siluT[:, :, 1:17].unsqueeze(3).broadcast_to([128, 8, 16, 8])))

        slot_free = [None] * NSLOT
        ring_n = [0]

        def ring_load(loads):
            s_ = ring_n[0] % NSLOT
            ring_n[0] += 1
            tok = None
            for i, (dstf, src) in enumerate(loads):
                tok = R.dma("pool", "ring%d" % s_, dstf(ring[:, s_, :]), src, [slot_free[s_]] if i == 0 else ())
            return s_, tok

        ada = {"last": None, "slots": {}}
        wo_tok = [None]

        def ada_load(cc):
            ada["slots"][cc] = ring_load([(lambda sl: sl.rearrange("p (k n) -> p k n", k=8),
                                           w_ada[:, cc * 512:(cc + 1) * 512].rearrange("(k p) n -> p k n", p=128))])

        def ada_compute(cc, pz=None, gb=(0, 1), mb=2):
            s_, tl = ada["slots"][cc]
            wa = ring[:, s_, :].rearrange("p (k n) -> p k n", k=8)
            which = cc // 2
            if which in (2, 5):
                half = cc % 2
                gi = 0 if which == 2 else 2
                bsrc = bg1 if which == 2 else bg2
                bdep = t_bg if which == 2 else t_bg2
                for grp in range(2):
                    tm = None
                    for k in range(8):
                        tm = R.op("pe", lambda e, k=k, grp=grp, wa=wa: e.matmul(
                            out=bank(gb[grp]), lhsT=siluTe[:, grp, k, :], rhs=wa[:, k, :], start=(k == 0), stop=(k == 7)),
                            [tl, t_sTe, pz, ada["last"] if k == 0 else None])
                    ada["last"] = R.op("dve", lambda e, grp=grp, gi=gi, half=half, bsrc=bsrc: e.tensor_tensor(
                        out=gates[:, gi + grp, half * 512:(half + 1) * 512], in0=bank(gb[grp]),
                        in1=bsrc[:, half * 512:(half + 1) * 512], op=ALU.add), [tm, bdep])
                slot_free[s_] = tm
            else:
                mi = {0: 0, 1: 1, 3: 2, 4: 3}[which]
                tm = None
                for j4 in range(4):
                    for k in range(8):
                        tm = R.op("pe", lambda e, k=k, j4=j4, wa=wa: e.matmul(
                            out=bank(mb)[:, j4 * 32:j4 * 32 + 17], lhsT=wa[:, k, j4 * 128:(j4 + 1) * 128],
                            rhs=siluT[:, k, :], start=(k == 0), stop=(k == 7)),
                            [tl, t_sT, pz, ada["last"] if (k == 0 and j4 == 0) else None])
                for j4 in range(4):
                    jc = (cc % 2) * 4 + j4
                    acol = cc * 4 + j4
                    ada["last"] = R.op("dve", lambda e, j4=j4, jc=jc, mi=mi, acol=acol: e.tensor_scalar(
                        out=modT[:, mi, jc, :], in0=bank(mb)[:, j4 * 32:j4 * 32 + 17],
                        scalar1=badaT_sb[:, acol:acol + 1], scalar2=(1.0 if mi in (1, 3) else 0.0),
                        op0=ALU.add, op1=ALU.add), [tm, t_badaT])
                slot_free[s_] = tm
            return ada["last"]

        for cc in range(3):
            ada_load(cc)
        t_win = R.dma("pool", "w_in", w_in_sb[:], w_in.rearrange("(k p) n -> p k n", p=128))
        for cc in range(4):
            ada_compute(cc)
            if cc + 3 < 6:
                ada_load(cc + 3)
            if cc == 2:
                t_wo = R.dma("pool", "w_o", w_o_sb[:], w_o.rearrange("(k p) n -> p k n", p=128))
        t_ada = ada["last"]
        ada_load(6)

        def ada_deferred(step):
            plan = {0: [("c", 4), ("l", 7), ("c", 5), ("l", 8)], 1: [("c", 6), ("l", 9), ("c", 7), ("l", 10)],
                    2: [("c", 8), ("l", 11), ("c", 9)], 3: [("c", 10), ("c", 11)]}
            for kind_, cc in plan[step]:
                if kind_ == "l":
                    ada_load(cc)
                else:
                    tk_ = ada_compute(cc, pz=[state["pS_free"], state["pQK_free"]], gb=(7, 5), mb=7)
                    state["pS_free"] = [state["pS_free"], tk_]
                    state["pQK_free"] = [state["pQK_free"], tk_]
            if step == 0:
                x1g_free[1] = [x1g_free[1], ada["last"]]
                wo_tok[0] = R.op("dve", lambda e: e.tensor_tensor(
                    out=w_o_sb[:], in0=w_o_sb[:], in1=gates[:, 0, :].unsqueeze(1).broadcast_to([128, 8, D]), op=ALU.mult),
                    [t_wo, ada["last"]])
            if step == 3:
                state["hidT_free"] = [state.get("hidT_free"), ("pe", R.cnt["pe"]), ada["last"]]

        state = {"tmpA_free": t_ada, "pT_free": t_ada,
                 "pZ_free": t_ada, "pS_free": None, "pQK_free": None, "mixcat_free": t_ada,
                 "PT_free": [t_ada, t_vmem, t_ws], "hT_free": None, "xn_free": None, "qT_free": None,
                 "u_free": None, "gv_free": None, "qkf_free": None, "mixcatT_free": None}
        x1g_free = [t_ada] * 4

        def ln_stats(src, deps, sb_=None):
            st_, mv_, rstd_, nmr_ = sb_ if sb_ is not None else (st, mv, rstd, nmr)
            n = src.shape[-1]
            nch = n // 512
            if sb_ is None:
                deps = [deps, state.get("stat_free")]
            for c in range(nch):
                R.op("dve", lambda e, c=c: e.bn_stats(out=st_[:, c * 6:(c + 1) * 6], in_=src[:, c * 512:(c + 1) * 512]), deps)
            t1 = R.op("dve", lambda e: e.bn_aggr(out=mv_[:], in_=st_[:, 0:6 * nch]))
            R.op("act", lambda e: e.activation(out=rstd_[:], in_=mv_[:, 1:2], func=AF.Ln, bias=epst[:, 0:1], scale=1.0), [t1])
            t2 = R.op("act", lambda e: e.activation(out=rstd_[:], in_=rstd_[:], func=AF.Exp, scale=-0.5))
            t3 = R.op("dve", lambda e: e.scalar_tensor_tensor(out=nmr_[:], in0=mv_[:, 0:1], scalar=-1.0, in1=rstd_[:],
                                                              op0=ALU.mult, op1=ALU.mult), [t2])
            return t3

        def ln_stats_a(src, deps, sb_=None):
            st_, mv_, rstd_, nmr_ = sb_ if sb_ is not None else (st, mv, rstd, nmr)
            nch = src.shape[-1] // 512
            if sb_ is None:
                deps = [deps, state.get("stat_free")]
            for c in range(nch):
                R.op("dve", lambda e, c=c: e.bn_stats(out=st_[:, c * 6:(c + 1) * 6], in_=src[:, c * 512:(c + 1) * 512]), deps)
            return R.op("dve", lambda e: e.bn_aggr(out=mv_[:], in_=st_[:, 0:6 * nch]))

        def ln_stats_b(t1, sb_=None):
            st_, mv_, rstd_, nmr_ = sb_ if sb_ is not None else (st, mv, rstd, nmr)
            R.op("act", lambda e: e.activation(out=rstd_[:], in_=mv_[:, 1:2], func=AF.Ln, bias=epst[:, 0:1], scale=1.0), [t1])
            t2 = R.op("act", lambda e: e.activation(out=rstd_[:], in_=rstd_[:], func=AF.Exp, scale=-0.5))
            return R.op("dve", lambda e: e.scalar_tensor_tensor(out=nmr_[:], in0=mv_[:, 0:1], scalar=-1.0, in1=rstd_[:],
                                                                op0=ALU.mult, op1=ALU.mult), [t2])

        def ln_stats_b_act(t1, sb_):
            st_, mv_, rstd_, nmr_ = sb_
            R.op("act", lambda e: e.activation(out=rstd_[:], in_=mv_[:, 1:2], func=AF.Ln, bias=epst[:, 0:1], scale=1.0), [t1])
            R.op("act", lambda e: e.activation(out=rstd_[:], in_=rstd_[:], func=AF.Exp, scale=-0.5))
            R.op("act", lambda e: e.activation(out=nmr_[:], in_=mv_[:, 0:1], func=AF.Identity, scale=rstd_[:, 0:1]))
            return R.op("act", lambda e: e.mul(out=nmr_[:], in_=nmr_[:], mul=-1.0))

        def norm_pre_b(src, t1, sb_=None):
            st_, mv_, rstd_, nmr_ = sb_ if sb_ is not None else (st, mv, rstd, nmr)
            t3 = ln_stats_b(t1, sb_) if sb_ is None else ln_stats_b_act(t1, sb_)
            return R.op("act", lambda e: e.activation(out=xn[:], in_=src, func=AF.Identity, bias=nmr_[:, 0:1], scale=rstd_[:, 0:1]),
                        [t3, state["xn_free"]])

        def norm_pre(src, deps, sb_=None):
            return norm_pre_b(src, ln_stats_a(src, deps, sb_), sb_)

        def _tb(grp, c):
            if grp == 0 and c >= 4:
                return bankbf(6)[:, (c - 4) * 128:(c - 3) * 128]
            return bankbf(4)[:, c * 128:(c + 1) * 128]

        def norm_post_a(t4, grp):
            tp_ = None
            for c in range(8):
                tp_ = R.op("pe", lambda e, c=c: e.transpose(out=_tb(grp, c), in_=xn[:, c * 128:(c + 1) * 128], identity=ident[:]),
                           [t4, state["pT_free"], state["pQK_free"] if grp == 0 else None, t_ident])
            state["xn_free"] = tp_
            return tp_

        def norm_post_b(tp_, grp, mi_sh, mi_sc, dstT_fn):
            te = []
            if grp == 0:
                for c in range(8):
                    if c < 4:
                        te.append(R.op("act", lambda e, c=c: e.activation(
                            out=dstT_fn(c), in_=_tb(grp, c), func=AF.Identity,
                            scale=modT[:, mi_sc, c, 0:1], bias=modT[:, mi_sh, c, 0:1]), [tp_, state["hT_free"]]))
                    else:
                        te.append(R.op("dve", lambda e, c=c: e.tensor_scalar(
                            out=dstT_fn(c), in0=_tb(grp, c),
                            scalar1=modT[:, mi_sc, c, 0:1], scalar2=modT[:, mi_sh, c, 0:1], op0=ALU.mult, op1=ALU.add),
                            [tp_, state["hT_free"]]))
                te = [te[3], te[7]]
                state["pQK_free"] = [state["pQK_free"], te[1]]
            else:
                for c in range(8):
                    R.op("dve", lambda e, c=c: e.tensor_tensor(
                        out=sg_sb[:, 0:128].rearrange("p (b t) -> p b t", t=8),
                        in0=_tb(grp, c).rearrange("p (b t) -> p b t", t=8),
                        in1=modT[:, mi_sc, c, 1:17].unsqueeze(2).broadcast_to([128, 16, 8]), op=ALU.mult),
                        [tp_, state["hT_free"]])
                    te = R.op("dve", lambda e, c=c: e.tensor_tensor(
                        out=dstT_fn(c).rearrange("p (b t) -> p b t", t=8),
                        in0=sg_sb[:, 0:128].rearrange("p (b t) -> p b t", t=8),
                        in1=modT[:, mi_sh, c, 1:17].unsqueeze(2).broadcast_to([128, 16, 8]), op=ALU.add))
            state["pT_free"] = te
            return te

        def norm_post(t4, grp, mi_sh, mi_sc, dstT_fn):
            return norm_post_b(norm_post_a(t4, grp), grp, mi_sh, mi_sc, dstT_fn)

        def norm_T(src, grp, mi_sh, mi_sc, dstT_fn, deps):
            t4 = norm_pre(src, deps)
            return norm_post(t4, grp, mi_sh, mi_sc, dstT_fn)

        def post_ln(psrc, xres, gate_ap, g_ap, b_ap, dst, deps, dst_free=None, gelu_hint=False, on_pool=True):
            if gate_ap is None:
                R.op("dve", lambda e: e.scalar_tensor_tensor(out=tmpA[:], in0=xres, scalar=ALPHA, in1=psrc,
                                                             op0=ALU.mult, op1=ALU.add), [deps, state["tmpA_free"]])
                tpz = ("dve", R.cnt["dve"])
            else:
                R.op("dve", lambda e: e.tensor_tensor(out=tmpA[:], in0=psrc, in1=gate_ap, op=ALU.mult),
                     [deps, state["tmpA_free"]])
                tpz = ("dve", R.cnt["dve"])
                R.op("dve", lambda e: e.scalar_tensor_tensor(out=tmpA[:], in0=xres, scalar=ALPHA, in1=tmpA[:],
                                                             op0=ALU.mult, op1=ALU.add))
            txr = ("dve", R.cnt["dve"])
            t3 = ln_stats(tmpA[:], ())
            if gelu_hint:
                R.op("act", lambda e: e.activation(out=dmy[:, 2:3], in_=dmy[:, 3:4], func=AF.Gelu_apprx_tanh))
            if not on_pool:
                R.op("dve", lambda e: e.tensor_scalar(out=tmpA[:], in0=tmpA[:], scalar1=rstd[:, 0:1], scalar2=nmr[:, 0:1],
                                                      op0=ALU.mult, op1=ALU.add), [t3])
                R.op("dve", lambda e: e.tensor_tensor(out=tmpA[:], in0=tmpA[:], in1=g_ap, op=ALU.mult))
                t5 = R.op("dve", lambda e: e.tensor_tensor(out=dst, in0=tmpA[:], in1=b_ap, op=ALU.add), [dst_free])
                state["tmpA_free"] = t5
                state["stat_free"] = t5
                return tpz, txr, t5
            R.op("pool", lambda e: e.tensor_scalar(out=tmpA[:], in0=tmpA[:], scalar1=rstd[:, 0:1], scalar2=nmr[:, 0:1],
                                                   op0=ALU.mult, op1=ALU.add), [t3])
            R.op("pool", lambda e: e.tensor_tensor(out=tmpA[:], in0=tmpA[:], in1=g_ap, op=ALU.mult))
            t5 = R.op("pool", lambda e: e.tensor_tensor(out=dst, in0=tmpA[:], in1=b_ap, op=ALU.add), [dst_free])
            state["tmpA_free"] = t5
            state["stat_free"] = t5
            return tpz, txr, t5

        out_tok = []
        t_kTc = [None]
        t_cvb = [None]

        xin_free = [t_ada, t_ada]
        sb2 = (st2, mv2, rstd2, nmr2)
        sb3 = (st3, mv3, rstd3, nmr3)
        sbP = [(sb("stP%d" % i, [128, 12], F32), sb("mvP%d" % i, [128, 2], F32), sb("rsP%d" % i, [128, 1], F32), sb("nmP%d" % i, [128, 1], F32)) for i in range(4)]
        sbG = (st, mv, sb("rsG", [128, 1], F32), sb("nmG", [128, 1], F32))

        def pe_fill(n, bk, deps=()):
            tk_ = None
            for _ in range(n):
                tk_ = R.op("pe", lambda e, bk=bk: e.matmul(out=bank(bk), lhsT=ident[:], rhs=mask_sb[:, 2, :], start=True, stop=True),
                           [t_mask, t_ident, deps])
            return tk_

        pending = []
        pending_pe = []

        def flush_pending_pe():
            while pending_pe:
                pending_pe.pop(0)()

        def flush_pending():
            while pending:
                pending.pop(0)()

        def front_load(T):
            T["t_x"] = R.dma("sp", "xin%d" % T["xi"], xinb[:, T["xi"], :], T["src"], [xin_free[T["xi"]]])

        def front_pre_a(T):
            T["t1"] = ln_stats_a(xinb[:, T["xi"], :], [T["t_x"]], sb2)

        def front_pre_b(T):
            T["t4"] = norm_pre_b(xinb[:, T["xi"], :], T["t1"], sb2)
            if T["kind"] == "halo":
                xin_free[T["xi"]] = T["t4"]

        def front_pre(T):
            front_pre_a(T)
            front_pre_b(T)

        def front_post_a(T):
            T["tp"] = norm_post_a(T["t4"], 1 if T["kind"] == "sample" else 0)

        def front_post_b(T):
            grp = 1 if T["kind"] == "sample" else 0
            T["t_h"] = norm_post_b(T["tp"], grp, 0, 1, lambda c: hT[:, c, :])

        def front_post(T):
            front_post_a(T)
            front_post_b(T)

        def mixer_tile(T, N=None, F=None, hook=None):
            kind, ti, slot, prev_slot, x1dst, x1free = T["kind"], T["ti"], T["slot"], T["prev_slot"], T["x1dst"], T["x1free"]
            xin_ = xinb[:, T["xi"], :]
            grp = 1 if kind == "sample" else 0
            if "t_h" not in T:
                front_load(T)
                front_pre(T)
                front_post(T)
            t_h = T["t_h"]
            if N is not None:
                front_load(N)
            groups = [(0, 0, 512), (1, 512, 256), (2, 768, 512), (3, 1280, 512)]
            if kind == "halo":
                groups = [(1, 512, 256)]
            tz = None
            tzg = {}
            dep01 = state.pop("pZ01_once", None) or state["pZ_free"]
            for (bk, c0, w) in groups:
                for k in range(8):
                    tz = R.op("pe", lambda e, bk=bk, c0=c0, w=w, k=k: e.matmul(
                        out=bank(bk)[:, 0:w], lhsT=hT[:, k, :], rhs=w_in_sb[:, k, c0:c0 + w], start=(k == 0), stop=(k == 7)),
                        [t_h, t_win, dep01 if bk < 2 else state["pZ_free"]])
                tzg[bk] = tz
            state["hT_free"] = tz
            flush_pending_pe()
            if hook is not None:
                hook()
            if kind != "halo":
                tf_ = pe_fill(32 if hook is None else 16, 4, [state["pT_free"]])
                state["pT_free"] = [state["pT_free"], tf_]
            if kind != "halo":
                R.op("act", lambda e: e.copy(out=qkf[:, 0:512], in_=bank(0)), [tzg[0], state["qkf_free"]])
            R.op("act", lambda e: e.copy(out=qkf[:, 512:640], in_=bank(1)[:, 0:128]), [tzg[1], state["qkf_free"]])
            t_v = R.op("act", lambda e: e.copy(out=vf[:], in_=bank(1)[:, 128:256]), [tzg[1], state["qkf_free"]])
            if kind != "halo":
                R.op("act", lambda e: e.activation(out=u_sb[:], in_=bank(2), func=AF.Gelu_apprx_tanh), [tzg[2], state["u_free"]])
                t_gl = R.op("act", lambda e: e.activation(out=gvf[:], in_=bank(3), func=AF.Gelu_apprx_tanh), [tzg[3], state["gv_free"]])
                t_zfree = t_gl
                R.op("act", lambda e: e.activation(out=dmy[:, 0:1], in_=dmy[:, 1:2], func=AF.Exp))
            else:
                t_zfree = t_v
            h0 = 8 if kind == "halo" else 0
            nh = 10 - h0
            qv = qkf[:].rearrange("p (h d) -> p h d", d=64)[:, h0:10, :]
            x1_ = qv[:, :, 0:8]
            x2_ = qv[:, :, 8:16]
            cs = cos_sb[:, ti, :].unsqueeze(1).broadcast_to([128, nh, 8])
            sn = sin_sb[:, ti, :].unsqueeze(1).broadcast_to([128, nh, 8])

            def rtv(i):
                return rt[:, i, 0:nh * 8].rearrange("p (h d) -> p h d", d=8)
            R.op("dve", lambda e: e.tensor_tensor(out=rtv(0), in0=x1_, in1=cs, op=ALU.mult), [t_v, t_trig])
            R.op("dve", lambda e: e.tensor_tensor(out=rtv(1), in0=x2_, in1=sn, op=ALU.mult))
            R.op("dve", lambda e: e.tensor_tensor(out=rtv(2), in0=x2_, in1=cs, op=ALU.mult))
            R.op("dve", lambda e: e.tensor_tensor(out=rtv(3), in0=x1_, in1=sn, op=ALU.mult))
            R.op("dve", lambda e: e.tensor_tensor(out=x1_, in0=rtv(0), in1=rtv(1), op=ALU.subtract))
            t_rot = R.op("dve", lambda e: e.tensor_tensor(out=x2_, in0=rtv(2), in1=rtv(3), op=ALU.add))
            t1g = ln_stats_a(gvf[:], [t_gl], sbG) if kind != "halo" else None
            flush_pending()
            if F is not None:
                F["t1"] = ln_stats_a(F["src"], [F["tok"]], sb3)
            t_qkb = R.op("act", lambda e: e.copy(out=qkb[:, h0 * 64:640], in_=qkf[:, h0 * 64:640]), [t_rot])
            t_vx = R.op("act", lambda e: e.copy(out=vext[:, slot, :, 0:64], in_=vf[:].rearrange("p (h d) -> p h d", d=64)),
                        [state["PT_free"]])
            t_kvout = []
            if kind == "sample":
                for b in range(16):
                    t_kvout.append(R.dma("sp", "okv", outk[b, 120:128, :], qkf[b * 8:(b + 1) * 8, 512:640], [t_rot]))
                    t_kvout.append(R.dma("sp", "okv", outv[b, 120:128, :], vf[b * 8:(b + 1) * 8, :], [t_v]))
            if kind == "prompt" and ti == NPT:
                t_kvout.append(R.dma("sp", "okv", kwin, qkf[:, 512:640], [t_rot]))
                t_kvout.append(R.dma("sp", "okv", vwin, vf[:], [t_v]))
            state["qkf_free"] = [t_qkb, t_vx] + t_kvout[-1:]
            tt_ = None
            ttk = None
            for h in ([8, 9] + list(range(h0, 8))):
                if h < 8:
                    o_ = bankbf(5)[0:64, h * 128:(h + 1) * 128]
                else:
                    o_ = bankbf(6)[0:64, (h - 8) * 128:(h - 7) * 128]
                tt_ = R.op("pe", lambda e, h=h, o_=o_: e.transpose(out=o_, in_=qkb[:, h * 64:(h + 1) * 64], identity=ident[:]),
                           [t_qkb, state["pQK_free"]])
                if h == 9:
                    ttk = tt_
            if kind != "halo":
                tf_ = pe_fill(10, 4, [state["pT_free"]])
                state["pT_free"] = [state["pT_free"], tf_]
            t_kT = R.op("act", lambda e: e.copy(out=kT[:, slot, :], in_=bankbf(6)[0:64, 0:256]), [ttk, state["PT_free"]])
            if kind == "halo":
                state["pQK_free"] = t_kT
                state["pZ_free"] = t_zfree
                if N is not None:
                    front_pre(N)
                    front_post(N)
                return
            t_qT0 = R.op("act", lambda e: e.copy(out=qT[:, 0:512], in_=bankbf(5)[0:64, 0:512]), [tt_, state["qT_free"]])
            t_qT = R.op("act", lambda e: e.copy(out=qT[:, 512:1024], in_=bankbf(5)[0:64, 512:1024]))
            if kind == "sample":
                for kvh_ in range(2):
                    t_qT = R.op("act", lambda e, kvh_=kvh_: e.copy(
                        out=qTs[:, kvh_, :, :].rearrange("p b (g t) -> p b g t", t=8),
                        in_=bankbf(5)[0:64, kvh_ * 512:(kvh_ + 1) * 512].rearrange("p (g b t) -> p b g t", g=4, t=8)))
            state["pQK_free"] = t_qT

            if kind == "prompt":
                blks = [(prev_slot, 0 if ti == 1 else 1), (slot, 2)]
            else:
                blks = [(slot, 3)]
            nb = len(blks)
            tsc = None
            jj = 0
            sc_banks = []
            tsck = {}
            for kvh in range(2):
                for (ks, mi_) in blks:
                    bk = jj
                    jj += 1
                    sc_banks.append((bk, kvh, ks))
                    R.op("pe", lambda e, bk=bk, kvh=kvh, ks=ks: e.matmul(
                        out=bank(bk), lhsT=kT[:, ks, kvh * 128:(kvh + 1) * 128], rhs=qT[:, kvh * 512:(kvh + 1) * 512],
                        start=True, stop=False), [t_kT, t_qT0 if kvh == 0 else t_qT, t_zfree, state["pZ_free"]])
                    tsc = R.op("pe", lambda e, bk=bk, mi_=mi_: e.matmul(
                        out=bank(bk), lhsT=ident[:], rhs=mask_sb[:, mi_, :], start=False, stop=True), [t_mask])
                tsck[kvh] = tsc
            if kind == "sample":
                for b in range(16):
                    for kvh in range(2):
                        col = b * 64 + kvh * 32
                        tsc = R.op("pe", lambda e, b=b, kvh=kvh, col=col: e.matmul(
                            out=ps[:, 1024 + col:1024 + col + 32],
                            lhsT=kTc[:, b, kvh, :],
                            rhs=qTs[:, kvh, b, :],
                            start=True, stop=True), [t_kTc[0]])
            state["qT_free"] = tsc
            tf_ = pe_fill(8, 4, [state["pT_free"]])
            state["pT_free"] = [state["pT_free"], tf_]

            t3 = ln_stats_b_act(t1g, sbG)
            t4 = R.op("act", lambda e: e.activation(out=gvf[:], in_=gvf[:], func=AF.Identity, bias=sbG[3][:, 0:1], scale=sbG[2][:, 0:1]), [t3])
            R.op("dve", lambda e: e.tensor_tensor(out=gvf[:], in0=gvf[:], in1=sgu_gb[:, 0:512], op=ALU.mult), [t4, t_sgu])
            t5 = R.op("dve", lambda e: e.tensor_tensor(out=gvf[:], in0=gvf[:], in1=sgu_gb[:, 512:1024], op=ALU.add))
            t_sgvo = None
            if kind == "sample":
                t_sgvo = R.dma("sp", "osgv", sgv, gvf[:], [t5])
                out_tok.append(t_sgvo)

            if N is not None:
                front_pre_a(N)
            texp = None
            texpk = {}
            for (bk, kvh, ks) in sc_banks:
                texp = R.op("act", lambda e, bk=bk: e.activation(out=PT[:, bk * 512:(bk + 1) * 512], in_=bank(bk),
                                                                 func=AF.Exp, scale=0.125),
                            [tsck[kvh] if kind == "prompt" else tsc, state["PT_free"]])
                texpk[kvh] = texp
            if kind == "sample":
                R.op("act", lambda e: e.activation(out=PTc[:], in_=bank(2, 2), func=AF.Exp, scale=0.125), [tsc])
                tpc = ("act", R.cnt["act"])
                texp = R.op("dve", lambda e: e.tensor_tensor(
                    out=PTc[:].rearrange("p (a t) -> p a t", t=8), in0=PTc[:].rearrange("p (a t) -> p a t", t=8),
                    in1=m01c_sb[:].unsqueeze(1).broadcast_to([128, 128, 8]), op=ALU.mult), [tpc, t_m01])
                texp = [texp, tpc]
            t6 = R.op("act", lambda e: e.copy(out=gvb[:], in_=gvf[:]), [t5, state["gv_free"]])
            if N is not None:
                front_pre_b(N)
            if F is not None:
                t3f = ln_stats_b_act(F["t1"], sb3)
                F["t4"] = R.op("act", lambda e: e.activation(out=xn2[:], in_=F["src"], func=AF.Identity, bias=nmr3[:, 0:1],
                                                             scale=rstd3[:, 0:1]), [t3f, state.get("xn2_free")])
            tpv = None
            for h in range(8):
                kvh, g = h // 4, h % 4
                ocol = (h // 4) * 512 + (h % 4) * 65
                for bi, (ks, mi_) in enumerate(blks):
                    bk = kvh * nb + bi
                    tpv = R.op("pe", lambda e, bk=bk, g=g, ks=ks, kvh=kvh, ocol=ocol, bi=bi: e.matmul(
                        out=ps[:, ocol:ocol + 65], lhsT=PT[:, bk * 512 + g * 128:bk * 512 + (g + 1) * 128],
                        rhs=vext[:, ks, kvh, :], start=(bi == 0), stop=(bi == nb - 1)),
                        [texpk[kvh] if kind == "prompt" else texp, t_vx])
            state["PT_free"] = tpv
            tm = None
            for h in range(8):
                tm = R.op("pe", lambda e, h=h: e.matmul(out=bank(7)[:, h * 64:(h + 1) * 64], lhsT=wsT_sb[:, grp, h, :],
                                                        rhs=gvb[:, h * 64:(h + 1) * 64], start=True, stop=True),
                          [t6, t_ws, state["pS_free"]])
            state["gv_free"] = [tm, t_sgvo]
            if kind == "prompt":
                tf_ = pe_fill(10, 5, [state["pQK_free"]])
                state["pQK_free"] = [state["pQK_free"], tf_]
            if N is not None:
                front_post_a(N)
            Oview = ps[:, 0:1024].rearrange("p (a n) -> p a n", a=2)[:, :, 0:260].rearrange("p a (g c) -> p a g c", c=65)
            if kind == "sample":
                for b in range(16):
                    for kvh in range(2):
                        col = b * 64 + kvh * 32
                        tpv = R.op("pe", lambda e, b=b, kvh=kvh, col=col: e.matmul(
                            out=ps[0:65, 1024 + col:1024 + col + 32], lhsT=cvb[:, b, kvh, :], rhs=PTc[:, col:col + 32],
                            start=True, stop=True), [texp, t_cvb[0]])
                t_oc = None
                for kvh_ in range(2):
                    t_oc = R.op("act", lambda e, kvh_=kvh_: e.copy(
                        out=OcT_sb[0:65, kvh_ * 512:(kvh_ + 1) * 512].rearrange("p (g b t) -> p b g t", g=4, t=8),
                        in_=ps[0:65, 1024:2048].rearrange("p (b k r) -> p b k r", k=2, r=32)[:, :, kvh_, :].rearrange("p b (g t) -> p b g t", t=8)),
                        [tpv, state["tmpA_free"]])
                ttr = None
                for h in range(8):
                    src_ = OcT_sb[0:65, h * 128:(h + 1) * 128]
                    ocol = (5 + h // 4) * 512 + (h % 4) * 65
                    ttr = R.op("pe", lambda e, src_=src_, ocol=ocol: e.transpose(
                        out=ps[:, ocol:ocol + 65], in_=src_, identity=identf[0:65, 0:65]), [t_oc, state["pQK_free"]])
                Ocv = ps[:, 2560:3584].rearrange("p (a n) -> p a n", a=2)[:, :, 0:260].rearrange("p a (g c) -> p a g c", c=65)
                t_o1 = R.op("act", lambda e: e.copy(out=Osum[:], in_=Ocv), [ttr])
                t_o2 = R.op("dve", lambda e: e.tensor_tensor(out=Osum[:], in0=Osum[:], in1=Oview, op=ALU.add), [t_o1, tpv])
                state["pQK_free"] = t_o1
                Osrc = Osum[:]
                tpv = t_o2
            else:
                Osrc = Oview
            R.op("dve", lambda e: e.tensor_tensor(out=den[:].rearrange("p (a g) -> p a g", a=2), in0=Osrc[:, :, :, 64],
                                                  in1=expsink[:].rearrange("p (a g) -> p a g", a=2), op=ALU.add), [tpv, t_es])
            R.op("dve", lambda e: e.reciprocal(out=rden[:], in_=den[:]))
            t_att = R.op("dve", lambda e: e.tensor_tensor(
                out=mixcat[:, 0:512].rearrange("p (a g d) -> p a g d", a=2, g=4),
                in0=Osrc[:, :, :, 0:64],
                in1=rden[:].rearrange("p (a g) -> p a g", a=2).unsqueeze(3).broadcast_to([128, 2, 4, 64]), op=ALU.mult),
                [state["mixcat_free"]])
            R.op("dve", lambda e: e.tensor_tensor(
                out=sg_sb[:].rearrange("p (h d) -> p h d", d=64), in0=bank(7).rearrange("p (h d) -> p h d", d=64),
                in1=bsT_sb[:, grp, :].unsqueeze(2).broadcast_to([128, 8, 64]), op=ALU.add), [tm, t_bsT])
            tsg = R.op("dve", lambda e: e.tensor_tensor(out=mixcat[:, 512:1024], in0=sg_sb[:], in1=u_sb[:], op=ALU.mult),
                       [state["mixcat_free"]])
            state["pS_free"] = tsg
            state["u_free"] = tsg
            tpa = None
            for c in range(4):
                tpa = R.op("pe", lambda e, c=c: e.transpose(out=bankbf(5)[:, c * 128:(c + 1) * 128],
                                                            in_=mixcat[:, c * 128:(c + 1) * 128], identity=ident[:]),
                           [t_att, state["pQK_free"]])
            tp_ = None
            for c in range(4, 8):
                tp_ = R.op("pe", lambda e, c=c: e.transpose(out=bankbf(7)[:, (c - 4) * 128:(c - 3) * 128],
                                                            in_=mixcat[:, c * 128:(c + 1) * 128], identity=ident[:]),
                           [tsg, state["pS_free"]])
            state["mixcat_free"] = tp_
            t_mTa = R.op("act", lambda e: e.copy(out=mixcatT[:, 0:4, :].rearrange("p k t -> p (k t)"), in_=bankbf(5)[:, 0:512]),
                         [tpa, state["mixcatT_free"]])
            t_mT = R.op("act", lambda e: e.copy(out=mixcatT[:, 4:8, :].rearrange("p k t -> p (k t)"), in_=bankbf(7)[:, 0:512]), [tp_])
            state["pQK_free"] = t_mTa
            state["pS_free"] = [state["pS_free"], t_mT]
            two = None
            for half in range(2):
                for k in range(8):
                    two = R.op("pe", lambda e, half=half, k=k: e.matmul(
                        out=bank(2 + half), lhsT=mixcatT[:, k, :], rhs=w_o_sb[:, k, half * 512:(half + 1) * 512],
                        start=(k == 0), stop=(k == 7)), [t_mTa if k < 4 else t_mT, t_wo, wo_tok[0], t_att])
            state["mixcatT_free"] = two
            if F is not None:
                def _ftr(F=F, tsg=tsg):
                    tpf_ = None
                    for c in range(8):
                        tpf_ = R.op("pe", lambda e, c=c: e.transpose(out=bankbf(7)[:, c * 128:(c + 1) * 128],
                                                                     in_=xn2[:, c * 128:(c + 1) * 128], identity=ident[:]),
                                    [F["t4"], tsg, state["pS_free"], t_ident])
                    state["xn2_free"] = tpf_
                    F["tpf"] = tpf_
                pending_pe.append(_ftr)
            if N is not None:
                front_post_b(N)
            tpz, txr, t5 = post_ln(bank(2, 2), xin_, (gates[:, 1, :] if kind == "sample" else None), lnv[:, 0, :], lnv[:, 1, :], x1dst, [two, t_lnv[0], t_lnv[1]], dst_free=x1free, gelu_hint=(N is not None), on_pool=False)
            state["pZ_free"] = tpz
            state["pZ01_once"] = t_att
            xin_free[T["xi"]] = txr
            if F is not None:
                def _evac(F=F):
                    tpf = F["tpf"]
                    tef = None
                    for c in range(8):
                        tef = R.op("dve", lambda e, c=c: e.tensor_scalar(
                            out=F["dst"](c), in0=bankbf(7)[:, c * 128:(c + 1) * 128],
                            scalar1=modT[:, 3, c, 0:1], scalar2=modT[:, 2, c, 0:1], op0=ALU.mult, op1=ALU.add),
                            [tpf, state.get("h2T_free")])
                    state["pS_free"] = [state["pS_free"], tef]
                    F["th"] = tef
                pending.append(_evac)
            return t5

        def ffn_group(ntiles, grp, x1_toks, y_dsts, ysem, h2_toks=None, post_hook=None, pre=None):
            N = ntiles * 128
            th = []
            for t_ in range(ntiles):
                if h2_toks is not None and h2_toks[t_] is not None:
                    th.append(h2_toks[t_])
                else:
                    th.append(norm_T(x1g[:, t_, :], grp, 2, 3, lambda c, t_=t_: h2T[:, c, t_ * 128:(t_ + 1) * 128],
                                     [x1_toks[t_], state.get("h2T_free")]))
            thid = None
            for f2 in range(NF // 2):
                if pre is not None and f2 in pre:
                    wg, wu, tl, s_ = pre[f2]
                else:
                    s_, tl = ring_load([
                        (lambda sl: sl[:, 0:2048].rearrange("p (k n) -> p k n", k=8),
                         w_gate[:, f2 * 256:(f2 + 1) * 256].rearrange("(k p) n -> p k n", p=128)),
                        (lambda sl: sl[:, 2048:4096].rearrange("p (k n) -> p k n", k=8),
                         w_up[:, f2 * 256:(f2 + 1) * 256].rearrange("(k p) n -> p k n", p=128))])
                    wg = ring[:, s_, 0:2048].rearrange("p (k n) -> p k n", k=8)
                    wu = ring[:, s_, 2048:4096].rearrange("p (k n) -> p k n", k=8)
                tlast = None
                for j in range(2):
                    f = f2 * 2 + j
                    bA, bB = (0, 1) if f % 2 == 0 else (2, 3)
                    key = "pF%d" % (f % 2)
                    for k in range(8):
                        R.op("pe", lambda e, k=k, j=j, bA=bA, wg=wg: e.matmul(
                            out=bank(bA)[:, 0:N], lhsT=wg[:, k, j * 128:(j + 1) * 128], rhs=h2T[:, k, 0:N],
                            start=(k == 0), stop=(k == 7)), [tl, th, state.get(key), state["pZ_free"]])
                    tg = ("pe", R.cnt["pe"])
                    for k in range(8):
                        R.op("pe", lambda e, k=k, j=j, bB=bB, wu=wu: e.matmul(
                            out=bank(bB)[:, 0:N], lhsT=wu[:, k, j * 128:(j + 1) * 128], rhs=h2T[:, k, 0:N],
                            start=(k == 0), stop=(k == 7)))
                    tu = ("pe", R.cnt["pe"])
                    tlast = tu
                    ts = R.op("act", lambda e, bA=bA: e.activation(out=sg_sb[:, 0:N], in_=bank(bA)[:, 0:N], func=AF.Silu),
                              [tg, thid])
                    thid = R.op("dve", lambda e, f=f, bB=bB: e.tensor_tensor(out=hidT[:, f, 0:N], in0=sg_sb[:, 0:N],
                                                                              in1=bank(bB)[:, 0:N], op=ALU.mult),
                                [ts, tu, state.get("hidT_free")])
                    state[key] = thid
                if s_ is not None:
                    slot_free[s_] = tlast
            state["h2T_free"] = tlast
            td = None
            for f2 in range(NF // 2):
                s_, tl = ring_load([(lambda sl: sl[:, 0:2048].rearrange("p (j n) -> p j n", j=2),
                                     w_down[f2 * 256:(f2 + 1) * 256, :].rearrange("(j p) n -> p j n", p=128))])
                wd = ring[:, s_, 0:2048].rearrange("p (j n) -> p j n", j=2)
                for j in range(2):
                    f = f2 * 2 + j
                    for t_ in range(ntiles):
                        for half in range(2):
                            td = R.op("pe", lambda e, f=f, j=j, t_=t_, half=half, wd=wd: e.matmul(
                                out=bank(2 * t_ + half), lhsT=hidT[:, f, t_ * 128:(t_ + 1) * 128],
                                rhs=wd[:, j, half * 512:(half + 1) * 512], start=(f == 0), stop=(f == NF - 1)),
                                [tl, thid, state["pT_free"], state["pQK_free"], state["pS_free"], state["pZ_free"],
                                 state.get("pF0"), state.get("pF1")])
                slot_free[s_] = td
            state["hidT_free"] = td
            last = None
            t1s, t4s = {}, {}

            def stage_a(t_):
                nonlocal last
                xt = x1g[:, t_, :]
                R.op("dve", lambda e: e.tensor_tensor(out=tmpA[:], in0=bank(2 * t_, 2), in1=gates[:, 2 + grp, :], op=ALU.mult),
                     [td, state["tmpA_free"]])
                last = ("dve", R.cnt["dve"])
                R.op("dve", lambda e: e.scalar_tensor_tensor(out=xt, in0=xt, scalar=ALPHA, in1=tmpA[:], op0=ALU.mult, op1=ALU.add))
                state["tmpA_free"] = ("dve", R.cnt["dve"])
                t1s[t_] = ln_stats_a(xt, (), sbP[t_])

            def stage_b(t_):
                xt = x1g[:, t_, :]
                t3 = ln_stats_b_act(t1s[t_], sbP[t_])
                t4s[t_] = R.op("act", lambda e: e.activation(out=xt, in_=xt, func=AF.Identity, bias=sbP[t_][3][:, 0:1],
                                                             scale=sbP[t_][2][:, 0:1]), [t3])

            def stage_c(t_):
                xt = x1g[:, t_, :]
                R.op("dve", lambda e: e.tensor_tensor(out=xt, in0=xt, in1=lnv[:, 2, :], op=ALU.mult), [t4s[t_], t_lnv[2]])
                t5 = R.op("dve", lambda e: e.tensor_tensor(out=xt, in0=xt, in1=lnv[:, 3, :], op=ALU.add), [t_lnv[3]])
                ty = R.dma("sp", "yout%d" % t_, y_dsts[t_], xt, [t5])
                x1g_free[t_] = ty
                out_tok.append(ty)

            order = []
            for step in range(ntiles + 2):
                if step < ntiles:
                    order.append(("a", step))
                if 0 <= step - 1 < ntiles:
                    order.append(("b", step - 1))
                if 0 <= step - 2 < ntiles:
                    order.append(("c", step - 2))
            for kind_, t_ in order:
                {"a": stage_a, "b": stage_b, "c": stage_c}[kind_](t_)
                if post_hook is not None and kind_ == "a" and t_ == 0:
                    post_hook(("dve", R.cnt["dve"]))
            state.pop("pZ01_once", None)
            for k_ in ("pT_free", "pQK_free", "pS_free", "pZ_free"):
                state[k_] = [state[k_], last] if state[k_] is not None else last
            state["x1g_free"] = ("dve", R.cnt["dve"])

        def SAMPLE_PREP(bank0_free=None):
            t_ckb = R.dma("pool", "c_ck", ckb, ck.rearrange("b j c -> j b c"), [state.get("h2T_free")])
            t_cm = R.op("dve", lambda e: e.memset(cvb, 1.0), [state.get("hidT_free")])
            for h_ in range(2):
                t_cvb[0] = R.dma("pool", "c_cv", cvb[:, :, h_, 0:64], cv[:, :, h_ * 64:(h_ + 1) * 64].rearrange("b j d -> j b d"), [t_cm])
            R.dma("sp", "roll", outk[:, 0:120, :], ck[:, 8:128, :])
            R.dma("sp", "roll", outv[:, 0:120, :], cv[:, 8:128, :])
            tev = [state["pZ_free"], state.get("hidT_free"), bank0_free]
            for r_ in range(4):
                tp_ = None
                for i in range(8):
                    b = r_ * 4 + i // 2
                    kvh = i % 2
                    tp_ = R.op("pe", lambda e, b=b, kvh=kvh, i=i: e.transpose(
                        out=bankbf(0, 1)[0:64, i * 128:(i + 1) * 128], in_=ckb[:, b, kvh * 64:(kvh + 1) * 64], identity=ident[:]),
                        [t_ckb, t_ident, tev])
                tev = R.op("act", lambda e, r_=r_: e.copy(out=kTc[:, r_ * 4:(r_ + 1) * 4, :, :].rearrange("p b h j -> p (b h j)"),
                                                          in_=bankbf(0, 1)[0:64, :]), [tp_])
            t_kTc[0] = tev
            state["pZ_free"] = [state["pZ_free"], tev]
            state.pop("pZ01_once", None)


        tiles = [dict(kind="halo", src=xh, ti=0, slot=0, prev_slot=None, x1dst=None, x1free=None, xi=0)]
        for i in range(16):
            slot = (i + 1) % 2
            tiles.append(dict(kind="prompt", src=xp[i * 128:(i + 1) * 128, :], ti=i + 1, slot=slot, prev_slot=1 - slot,
                              x1dst=x1g[:, i % 4, :], x1free=None, xi=(i + 1) % 2, slot4=i % 4))
        tiles.append(dict(kind="sample", src=xs, ti=17, slot=0, prev_slot=None, x1dst=x1g[:, 0, :], x1free=None, xi=1, slot4=0))
        mixer_tile(tiles[0], tiles[1])
        for g_ in range(4):
            toks = []
            fds = []
            for t_ in range(4):
                i = 1 + g_ * 4 + t_
                tiles[i]["x1free"] = x1g_free[t_]
                Fd = None
                if t_ >= 1 and (g_ != 0 or t_ == 3):
                    Fd = dict(src=x1g[:, t_ - 1, :], tok=toks[t_ - 1],
                              dst=(lambda c, tt=t_ - 1: h2T[:, c, tt * 128:(tt + 1) * 128]))
                hk = (lambda st_=i - 1: ada_deferred(st_)) if i in (1, 2, 3, 4) else None
                toks.append(mixer_tile(tiles[i], tiles[i + 1], Fd, hk))
                fds.append(Fd)
            flush_pending_pe()
            flush_pending()
            h2_toks = [fds[t_ + 1]["th"] if (t_ < 3 and fds[t_ + 1] is not None) else None for t_ in range(4)]
            if g_ == 3:
                wo_tok[0] = R.dma("pool", "w_o2", w_o_sb[:], w_o.rearrange("(k p) n -> p k n", p=128), [state["mixcatT_free"]])
            ffn_group(4, 0, toks, [yp[(g_ * 4 + t_) * 128:(g_ * 4 + t_ + 1) * 128, :] for t_ in range(4)], "y", h2_toks,
                      post_hook=(SAMPLE_PREP if g_ == 3 else None))
        spre = {}

        def _gsrc(f2):
            return w_gate[:, f2 * 256:(f2 + 1) * 256].rearrange("(k p) n -> p k n", p=128)

        def _usrc(f2):
            return w_up[:, f2 * 256:(f2 + 1) * 256].rearrange("(k p) n -> p k n", p=128)

        def _pf(f2, gview, uview, deps):
            R.dma("pool", "pfx%d" % f2, gview, _gsrc(f2), deps)
            tok = R.dma("pool", "pfx%d" % f2, uview, _usrc(f2))
            spre[f2] = (gview, uview, tok, None)

        for f2 in range(3):
            s_, tl = ring_load([
                (lambda sl: sl[:, 0:2048].rearrange("p (k n) -> p k n", k=8), _gsrc(f2)),
                (lambda sl: sl[:, 2048:4096].rearrange("p (k n) -> p k n", k=8), _usrc(f2))])
            spre[f2] = (ring[:, s_, 0:2048].rearrange("p (k n) -> p k n", k=8),
                        ring[:, s_, 2048:4096].rearrange("p (k n) -> p k n", k=8), tl, s_)
        xs_bf = x1g[:, 1:3, :].rearrange("p a n -> p (a n)").bitcast(BF16)
        _pf(3, xs_bf[:, 0:2048].rearrange("p (k n) -> p k n", k=8), xs_bf[:, 2048:4096].rearrange("p (k n) -> p k n", k=8),
            [x1g_free[1], x1g_free[2]])
        dve_now = ("dve", R.cnt["dve"])
        _pf(4, gates[:, 0, :].bitcast(BF16).rearrange("p (k n) -> p k n", k=8),
            gates[:, 2, :].bitcast(BF16).rearrange("p (k n) -> p k n", k=8), [dve_now])

        def _pf_win():
            wflat = w_in_sb[:].rearrange("p k n -> p (k n)")
            tzs = ("pe", R.cnt["pe"])
            for i_, f2 in enumerate((5, 6, 7)):
                _pf(f2, wflat[:, i_ * 4096:i_ * 4096 + 2048].rearrange("p (k n) -> p k n", k=8),
                    wflat[:, i_ * 4096 + 2048:(i_ + 1) * 4096].rearrange("p (k n) -> p k n", k=8), [tzs])

        tiles[17]["x1free"] = x1g_free[0]
        tk = mixer_tile(tiles[17], None, None, _pf_win)
        ffn_group(1, 1, [tk], [ys], "y", pre=spre)

        R._waits("sp", out_tok)
        R._waits("sp", [("okv", R.dsem["okv"][1]), ("roll", R.dsem["roll"][1])])

        with nc.Block() as block:
            @block.sync
            def _(e):
                R.replay("sp", e)

            @block.tensor
            def _(e):
                R.replay("pe", e)

            @block.scalar
            def _(e):
                R.replay("act", e)

            @block.vector
            def _(e):
                R.replay("dve", e)

            @block.gpsimd
            def _(e):
                R.replay("pool", e)
        if R.track and decisions is not None:
            R.check_psum()
    if decisions == "auto":
        return nc
    return nc, R


_NC = None


def _host_consts():
    tril = np.tril(np.ones((128, 128), np.float32))
    trilT_p = np.ascontiguousarray(tril.T)
    blk = np.zeros((128, 128), np.float32)
    for b in range(16):
        blk[b * 8:(b + 1) * 8, b * 8:(b + 1) * 8] = tril[:8, :8].T
    s_idx = np.arange(128)[:, None]
    q_idx = np.arange(128)[None, :]
    maskP = np.where(s_idx > q_idx, 0.0, NEG).astype(np.float32)
    maskC = np.where(s_idx <= q_idx, 0.0, NEG).astype(np.float32)
    sb_, st_ = s_idx // 8, s_idx % 8
    qb_, qt_ = q_idx // 8, q_idx % 8
    maskS = np.where((sb_ == qb_) & (st_ <= qt_), 0.0, NEG).astype(np.float32)
    m01c = (np.arange(128)[:, None] > np.arange(8)[None, :]).astype(np.float32)
    invf = (500000.0 ** (-np.arange(8, dtype=np.float32) * 2.0 / 16.0)).astype(np.float32)
    return trilT_p, blk, maskP, maskC, maskS, m01c, invf


def _prep(x_prompt, x_sample, cache_k_win, cache_v_win, c_prompt, c_sample,
          w_ada, b_ada, w_in, attn_sinks, sgu_ln_g, sgu_ln_b, w_s, b_s, w_o,
          ln1_g, ln1_b, w_gate, w_up, w_down, ln2_g, ln2_b):
    f32 = np.float32
    A = lambda a: np.ascontiguousarray(np.asarray(a, dtype=f32))
    x_prompt, x_sample = A(x_prompt), A(x_sample)
    ckw, cvw = A(cache_k_win), A(cache_v_win)
    trilT_p, blk, maskP, maskC, maskS, m01c, invf = _host_consts()
    w_s0 = A(w_s)[0]
    wsT_p = np.ascontiguousarray(w_s0.transpose(2, 0, 1))
    wsT_s = np.zeros((128, 8, 128), f32)
    for b in range(16):
        wsT_s[b * 8:(b + 1) * 8, :, b * 8:(b + 1) * 8] = w_s0[:, :8, :8].transpose(2, 0, 1)
    wsT = np.stack([wsT_p, wsT_s], 0)
    trilT = np.stack([trilT_p, blk], 0)
    b_s0 = A(b_s)[0]
    bsT = np.stack([np.ascontiguousarray(b_s0.T), np.tile(np.ascontiguousarray(b_s0[:, :8].T), (16, 1))], 0)
    b_ada0 = A(b_ada)[0]
    b_adaT = np.ascontiguousarray(b_ada0.reshape(48, 128).T)
    b_gate = np.stack([b_ada0[2048:3072], b_ada0[5120:6144]], 0)
    vecs = np.zeros((6, D), f32)
    vecs[0, :512] = A(sgu_ln_g)[0]
    vecs[0, 512:] = A(sgu_ln_b)[0]
    vecs[1], vecs[2], vecs[3], vecs[4] = A(ln1_g)[0], A(ln1_b)[0], A(ln2_g)[0], A(ln2_b)[0]
    common = {
        "w_ada": A(w_ada)[0], "b_adaT": b_adaT, "b_gate": np.ascontiguousarray(b_gate), "w_in": A(w_in)[0],
        "w_o": A(w_o)[0], "w_gate": A(w_gate)[0], "w_up": A(w_up)[0], "w_down": A(w_down)[0],
        "sinks": A(attn_sinks), "vecs": vecs, "wsT": np.ascontiguousarray(wsT), "trilT": np.ascontiguousarray(trilT),
        "bsT": np.ascontiguousarray(bsT), "m01c": m01c, "invf": np.tile(invf[None, :], (128, 1)).astype(f32),
        "identin": np.eye(128, dtype=f32),
    }
    maskNone = np.full((128, 128), NEG, f32)
    in_maps = []
    for c in range(8):
        b, hf = c // 2, c % 2
        m = dict(common)
        m["xp"] = np.ascontiguousarray(x_prompt[b, hf * 2048:(hf + 1) * 2048])
        m["xh"] = np.ascontiguousarray(x_prompt[b, 2048 - 128:2048]) if hf == 1 else np.zeros((128, D), f32)
        m["xs"] = np.ascontiguousarray(x_sample[c * 16:(c + 1) * 16].reshape(128, D))
        m["c17"] = np.ascontiguousarray(np.concatenate([A(c_prompt)[b:b + 1], A(c_sample)[c * 16:(c + 1) * 16]], 0))
        m["ck"] = np.ascontiguousarray(ckw[0, c * 16:(c + 1) * 16].reshape(16, 128, 128))
        m["cv"] = np.ascontiguousarray(cvw[0, c * 16:(c + 1) * 16].reshape(16, 128, 128))
        mp0 = maskP if hf == 1 else maskNone
        m["masks"] = np.ascontiguousarray(np.stack([np.tile(mp0, (1, 4)), np.tile(maskP, (1, 4)),
                                                    np.tile(maskC, (1, 4)), np.tile(maskS, (1, 4))], 0))
        pos = np.zeros((128, 18), f32)
        pos[:, 0] = hf * 2048 - 128 + np.arange(128)
        for i in range(16):
            pos[:, 1 + i] = hf * 2048 + i * 128 + np.arange(128)
        pos[:, 17] = PAST + (np.arange(128) % 8)
        m["posf"] = pos
        in_maps.append(m)
    return in_maps


def _assemble(r):
    f32 = np.float32
    y_prompt = np.stack([np.concatenate([r[2 * b]["yp"], r[2 * b + 1]["yp"]], 0) for b in range(4)], 0)
    y_sample = np.concatenate([r[c]["ys"].reshape(16, 8, D) for c in range(8)], 0)
    kwp = np.stack([r[2 * b + 1]["kwin"].reshape(128, 2, 64) for b in range(4)], 0)[None]
    vwp = np.stack([r[2 * b + 1]["vwin"].reshape(128, 2, 64) for b in range(4)], 0)[None]
    kws = np.concatenate([r[c]["outk"].reshape(16, 128, 2, 64) for c in range(8)], 0)[None]
    vws = np.concatenate([r[c]["outv"].reshape(16, 128, 2, 64) for c in range(8)], 0)[None]
    sg = np.concatenate([r[c]["sgv"].reshape(16, 8, 512) for c in range(8)], 0)[None]
    return (y_prompt.astype(f32), y_sample.astype(f32), kwp.astype(f32), vwp.astype(f32),
            kws.astype(f32), vws.astype(f32), sg.astype(f32))


def kernel(**inputs):
    global _NC
    in_maps = _prep(**inputs)
    if _NC is None:
        _NC = build_nc()
    res = run_bass_kernel_spmd(_NC, in_maps, core_ids=list(range(8)))
    return _assemble(res.results)
```

```python
import os
from contextlib import ExitStack
import numpy as np
import concourse.bass as bass
import concourse.mybir as mybir
from concourse.bass_utils import run_bass_kernel_spmd

F32 = mybir.dt.float32
BF16 = mybir.dt.bfloat16
I32 = mybir.dt.int32
AF = mybir.ActivationFunctionType
ALU = mybir.AluOpType

D = 1024
DFF = 2816
NF = 22
INW = 1792
ALPHA = 2.0 ** 0.25
EPS = 1e-5
NEG = -30000.0
PAST = 16384
NPT = 16


class Rec:
    def __init__(self, nc, es, decisions=None):
        self.nc = nc
        self.es = es
        self.decisions = decisions
        self.acc_log = {}
        self.eng = {"pe": nc.tensor, "act": nc.scalar, "dve": nc.vector, "pool": nc.gpsimd, "sp": nc.sync}
        self.ops = {k: [] for k in self.eng}
        self.sem = {k: es.enter_context(nc.semaphore("s_" + k)) for k in self.eng}
        self.cnt = {k: 0 for k in self.eng}
        self.waited = {k: {} for k in self.eng}
        self.dsem = {}
        self.psum_acc = {}
        self.track = bool(os.environ.get("KCHECK"))

    def _flat(self, deps, out):
        for d in deps:
            if d is None:
                continue
            if isinstance(d, tuple) and len(d) == 2 and isinstance(d[0], str):
                out.append(d)
            else:
                self._flat(d, out)

    def _waits(self, eng, deps):
        fl = []
        self._flat(deps, fl)
        for key, val in fl:
            if key == eng:
                continue
            if self.waited[eng].get(key, 0) >= val:
                continue
            self.waited[eng][key] = val
            self.ops[eng].append(("wait", key, val))

    def op(self, eng, fn, deps=()):
        self._waits(eng, deps)
        if eng in ("act", "dve", "pool") and self.cnt[eng] > 0:
            need = self.cnt[eng]
            if self.decisions is not None:
                need = self.decisions[eng][self.cnt[eng]]
            if need > 0 and self.waited[eng].get(eng, 0) < need:
                self.waited[eng][eng] = need
                self.ops[eng].append(("wait", eng, need))
        self.cnt[eng] += 1
        self.ops[eng].append(("op", fn))
        return (eng, self.cnt[eng])

    def dma(self, eng, semname, out, in_, deps=()):
        self._waits(eng, deps)
        if semname not in self.dsem:
            self.dsem[semname] = [self.es.enter_context(self.nc.semaphore("d_" + semname)), 0]
        self.dsem[semname][1] += 16
        self.ops[eng].append(("dma", out, in_, semname))
        return (semname, self.dsem[semname][1])

    def semh(self, key):
        return self.sem[key] if key in self.sem else self.dsem[key][0]

    def replay(self, eng, e):
        n = 0
        acc = self.psum_acc.setdefault(eng, [])
        for it in self.ops[eng]:
            if it[0] == "wait":
                e.wait_ge(self.semh(it[1]), it[2])
            elif it[0] == "op":
                bi = it[1](e)
                bi.then_inc(self.sem[eng], 1)
                n += 1
                if self.decisions is None and eng in ("act", "dve", "pool"):
                    self.acc_log.setdefault(eng, []).append((self._regions(bi.ins.ins), self._regions(bi.ins.outs)))
                if self.track:
                    banks = set()
                    for a in list(bi.ins.ins) + list(bi.ins.outs):
                        if getattr(a, "memref", None) == "ps":
                            es_ = 2 if "bfloat16" in str(a.dtype) else 4
                            row = 16384 // es_
                            col0 = a.offset % row
                            ext = 1 + sum((c - 1) * abs(st) for st, c in list(a.ap)[1:])
                            for b in range((col0 * es_) // 2048, ((col0 + ext - 1) * es_) // 2048 + 1):
                                banks.add(b)
                    if banks:
                        acc.append((n, banks))
            else:
                e.dma_start(out=it[1], in_=it[2]).then_inc(self.dsem[it[3]][0], 16)

    @staticmethod
    def _regions(args):
        out = []
        for a in args:
            mr = getattr(a, "memref", None)
            if mr is None:
                continue
            ds = str(a.dtype)
            es_ = 2 if ("bfloat16" in ds or "float16" in ds or "int16" in ds) else (1 if "int8" in ds else 4)
            ap = list(a.ap)
            pst = abs(ap[0][0]) if ap and ap[0][0] != 0 else 0
            off = a.offset % pst if pst > 0 else a.offset
            p0 = a.offset // pst if pst > 0 else 0
            p1 = p0 + (ap[0][1] if (ap and pst > 0) else 128)
            ext = 1 + sum((c - 1) * abs(st) for st, c in ap[1:])
            out.append((mr, off * es_, (off + ext) * es_, p0, p1))
        return out

    def make_decisions(self):
        def hit(A, B):
            for (m1, l1, h1, a0, a1) in A:
                for (m2, l2, h2, b0, b1) in B:
                    if m1 == m2 and l1 < h2 and l2 < h1 and a0 < b1 and b0 < a1:
                        return True
            return False
        dec = {}
        for eng, log in self.acc_log.items():
            d = [0] * (len(log) + 1)
            for i, (rd, wr) in enumerate(log):
                need = 0
                for j in range(i - 1, max(-1, i - 40), -1):
                    prd, pwr = log[j]
                    if hit(pwr, rd) or hit(pwr, wr) or hit(prd, wr):
                        need = j + 1
                        break
                if i >= 40:
                    need = max(need, i - 39)
                d[i] = need
            dec[eng] = d
        return dec

    def check_psum(self):
        engs = list(self.ops)
        pos = {k: 0 for k in engs}
        cnt = {k: 0 for k in engs}
        vc = {k: {x: 0 for x in engs} for k in engs}
        hist = {k: {0: dict(vc[k])} for k in engs}
        dcount = {}
        progress = True
        while progress:
            progress = False
            for eng in engs:
                while pos[eng] < len(self.ops[eng]):
                    it = self.ops[eng][pos[eng]]
                    if it[0] == "wait":
                        key, val = it[1], it[2]
                        if key in cnt:
                            if cnt[key] < val:
                                break
                            src = hist[key][val]
                        else:
                            if dcount.get(key, 0) < val:
                                break
                            src = hist[key][val]
                        for x in engs:
                            vc[eng][x] = max(vc[eng][x], src[x])
                    elif it[0] == "op":
                        cnt[eng] += 1
                        vc[eng][eng] = cnt[eng]
                        hist[eng][cnt[eng]] = dict(vc[eng])
                    else:
                        key = it[3]
                        dcount[key] = dcount.get(key, 0) + 16
                        hist.setdefault(key, {})[dcount[key]] = dict(vc[eng])
                    pos[eng] += 1
                    progress = True
        assert all(pos[k] == len(self.ops[k]) for k in engs), "deadlock in recorded program"
        import bisect
        nbad = 0
        per_bank = {}
        for eng, lst in self.psum_acc.items():
            for idx, banks in lst:
                for b in banks:
                    per_bank.setdefault(b, {}).setdefault(eng, []).append(idx)
        for b, d in per_bank.items():
            for E, xs in d.items():
                for F, fs in d.items():
                    if E == F:
                        continue
                    for x in xs:
                        seen = hist[E][x][F]
                        j = bisect.bisect_right(fs, seen)
                        if j < len(fs) and hist[F][fs[j]][E] < x:
                            nbad += 1
                            if nbad <= 20:
                                print("PSUM UNORDERED bank", b, E, x, "vs", F, fs[j])
        print("check_psum: unordered pairs =", nbad)
        return nbad


def build_nc(decisions="auto"):
    if decisions == "auto":
        _, rec1 = build_nc(decisions=None)
        return build_nc(decisions=rec1.make_decisions())[0]
    nc = bass.Bass("TRN2", target_bir_lowering=False)

    def din(name, shape, dt=F32):
        return nc.dram_tensor(name, list(shape), dt, kind="ExternalInput").ap()

    def dout(name, shape):
        return nc.dram_tensor(name, list(shape), F32, kind="ExternalOutput").ap()

    xp = din("xp", [NPT * 128, D])
    xh = din("xh", [128, D])
    xs = din("xs", [128, D])
    c17 = din("c17", [17, D])
    ck = din("ck", [16, 128, 128])
    cv = din("cv", [16, 128, 128])
    w_ada = din("w_ada", [D, 6 * D])
    b_adaT = din("b_adaT", [128, 48])
    b_gate = din("b_gate", [2, D])
    w_in = din("w_in", [D, INW])
    w_o = din("w_o", [D, D])
    w_gate = din("w_gate", [D, DFF])
    w_up = din("w_up", [D, DFF])
    w_down = din("w_down", [DFF, D])
    sinks = din("sinks", [1, 8])
    vecs = din("vecs", [6, D])
    wsT = din("wsT", [2, 128, 8, 128])
    trilT = din("trilT", [2, 128, 128])
    bsT = din("bsT", [2, 128, 8])
    masks = din("masks", [4, 128, 512])
    m01c = din("m01c", [128, 8])
    posf = din("posf", [128, 18])
    invf = din("invf", [128, 8])
    identin = din("identin", [128, 128])

    yp = dout("yp", [NPT * 128, D])
    ys = dout("ys", [128, D])
    kwin = dout("kwin", [128, 128])
    vwin = dout("vwin", [128, 128])
    outk = dout("outk", [16, 128, 128])
    outv = dout("outv", [16, 128, 128])
    sgv = dout("sgv", [128, 512])

    es = ExitStack()
    with es:
        R = Rec(nc, es, decisions)
        sb_bytes = [0]

        def sb(name, shape, dt=F32):
            n = 1
            for s_ in shape[1:]:
                n *= s_
            sb_bytes[0] += n * (2 if dt == BF16 else 4)
            return es.enter_context(nc.sbuf_tensor(name, list(shape), dt))

        ps = es.enter_context(nc.psum_tensor("ps", [128, 4096], F32))

        def bank(j, n=1):
            return ps[:, j * 512:(j + n) * 512]

        def bankbf(j, n=1):
            return ps[:, j * 512:(j + n) * 512].bitcast(BF16)

        ident = sb("ident", [128, 128], BF16)
        identf = sb("identf", [128, 128], F32)
        w_in_sb = sb("w_in_sb", [128, 8, INW], BF16)
        w_o_sb = sb("w_o_sb", [128, 8, D], BF16)
        gates = sb("gates", [128, 4, D], F32)
        lnv = sb("lnv", [128, 4, D], F32)
        sgu_gb = sb("sgu_gb", [128, D], F32)
        wsT_sb = sb("wsT_sb", [128, 2, 8, 128], BF16)
        bsT_sb = sb("bsT_sb", [128, 2, 8], F32)
        mask_sb = sb("mask_sb", [128, 4, 512], BF16)
        m01c_sb = sb("m01c_sb", [128, 8], F32)
        cos_sb = sb("cos_sb", [128, 18, 8], F32)
        sin_sb = sb("sin_sb", [128, 18, 8], F32)
        modT = sb("modT", [128, 4, 8, 17], F32)
        expsink = sb("expsink", [128, 8], F32)
        epst = sb("epst", [128, 1], F32)
        dmy = sb("dmy", [128, 4], F32)
        badaT_sb = sb("badaT_sb", [128, 48], F32)
        x1g = sb("x1g", [128, 4, D], F32)
        h2T = sb("h2T", [128, 8, 512], BF16)
        hidT = sb("hidT", [128, NF, 512], BF16)
        NSLOT = 3
        ring = sb("ring", [128, NSLOT, 4096], BF16)
        xinb = sb("xinb", [128, 2, D], F32)
        xin = xinb[:, 0, :]
        xn = sb("xn", [128, D], BF16)
        xn2 = sb("xn2", [128, D], BF16)
        hT = sb("hT", [128, 8, 128], BF16)
        qkf = sb("qkf", [128, 640], F32)
        vf = sb("vf", [128, 128], F32)
        qkb = sb("qkb", [128, 640], BF16)
        vext = sb("vext", [128, 2, 2, 65], BF16)
        qT = sb("qT", [64, 1024], BF16)
        kT = sb("kT", [64, 2, 256], BF16)
        u_sb = sb("u_sb", [128, 512], F32)
        gvf = sb("gvf", [128, 512], F32)
        gvb = sb("gvb", [128, 512], BF16)
        PT = sb("PT", [128, 2048], BF16)
        mixcat = sb("mixcat", [128, D], BF16)
        mixcatT = sb("mixcatT", [128, 8, 128], BF16)
        tmpA = sb("tmpA", [128, D], F32)
        st = sb("st", [128, 12], F32)
        mv = sb("mv", [128, 2], F32)
        rstd = sb("rstd", [128, 1], F32)
        nmr = sb("nmr", [128, 1], F32)
        st2 = sb("st2", [128, 12], F32)
        mv2 = sb("mv2", [128, 2], F32)
        rstd2 = sb("rstd2", [128, 1], F32)
        nmr2 = sb("nmr2", [128, 1], F32)
        st3 = sb("st3", [128, 12], F32)
        mv3 = sb("mv3", [128, 2], F32)
        rstd3 = sb("rstd3", [128, 1], F32)
        nmr3 = sb("nmr3", [128, 1], F32)
        den = sb("den", [128, 8], F32)
        rden = sb("rden", [128, 8], F32)
        sg_sb = sb("sg_sb", [128, 512], F32)
        Osum = sb("Osum", [128, 2, 4, 65], F32)
        siluT = sb("siluT", [128, 8, 17], BF16)
        posb = sb("posb", [128, 18], F32)
        invb = sb("invb", [128, 8], F32)
        print("SBUF bytes/partition:", sb_bytes[0])
        assert sb_bytes[0] < 212700, sb_bytes[0]
        HF = hidT[:].rearrange("p f n -> p (f n)")
        kTc = HF[0:64, 0:4096].rearrange("p (b h j) -> p b h j", b=16, h=2)
        cvb = HF[:, 4096:4096 + 2080].rearrange("p (b h c) -> p b h c", b=16, h=2)
        PTc = HF[:, 6656:7680]
        qTs = HF[0:64, 7680:8704].rearrange("p (k b r) -> p k b r", k=2, b=16)
        ang = tmpA[:, 256:400].rearrange("p (a b) -> p a b", b=8)
        kfl = tmpA[:, 512:656].rearrange("p (a b) -> p a b", b=8)
        ckb = h2T[:].rearrange("p k n -> p (k n)")[:, 0:2048].rearrange("p (b c) -> p b c", b=16)
        OcT_sb = tmpA
        c17_sb = x1g[0:17, 2, :]
        silu_bf = mixcat[0:17, :]
        siluTe = HF[:, 0:2048].rearrange("p (g k t) -> p g k t", g=2, k=8)
        bg1 = x1g[:, 1, :]
        rt = HF[:, 8704:9344].bitcast(F32).rearrange("p (a b) -> p a b", a=4)
        kin = tmpA[:, 768:912].bitcast(I32).rearrange("p (a b) -> p a b", b=8)
        trl = PT[:, 0:256].rearrange("p (g t) -> p g t", g=2)
        bg2 = HF[:, 2048:4096].bitcast(F32)

        t_id = R.dma("sp", "c_id", identf[:], identin)
        t_c17 = R.dma("sp", "c_c17", c17_sb, c17)
        t_badaT = R.dma("sp", "c_bada", badaT_sb[:], b_adaT)
        t_bg = R.dma("sp", "c_bg", bg1, b_gate[0:1, :].broadcast_to([128, D]))
        t_bg2 = R.dma("sp", "c_bg2", bg2, b_gate[1:2, :].broadcast_to([128, D]))
        t_pos = R.dma("sp", "c_pos", posb[:], posf)
        t_inv = R.dma("sp", "c_inv", invb[:], invf)
        t_bsT = R.dma("sp", "c_bsT", bsT_sb[:], bsT.rearrange("g p h -> p g h"))
        t_m01 = R.dma("sp", "c_m01", m01c_sb[:], m01c)
        t_snk = R.dma("sp", "c_snk", expsink[:], sinks[0:1, :].broadcast_to([128, 8]))
        t_lnv = [R.dma("sp", "c_lnv%d" % i, lnv[:, i, :], vecs[1 + i:2 + i, :].broadcast_to([128, D])) for i in range(4)]
        t_sgu = R.dma("sp", "c_sgu", sgu_gb[:], vecs[0:1, :].broadcast_to([128, D]))
        t_mask = R.dma("pool", "c_mask", mask_sb[:], masks.rearrange("m p n -> p m n"))
        t_wsT = R.dma("pool", "c_wsT", wsT_sb[:], wsT.rearrange("g p h t -> p g h t"))
        t_trl = R.dma("pool", "c_trl", trl, trilT.rearrange("g p t -> p g t"))

        t = R.op("dve", lambda e: e.tensor_copy(out=ident[:], in_=identf[:]), [t_id])
        t_ident = t
        R.op("dve", lambda e: e.memset(epst[:], EPS))
        R.op("dve", lambda e: e.memset(dmy[:], 0.0))
        t_vmem = R.op("dve", lambda e: e.memset(vext[:], 1.0))
        t_ws = R.op("dve", lambda e: e.tensor_tensor(
            out=wsT_sb[:], in0=wsT_sb[:],
            in1=trl.unsqueeze(2).broadcast_to([128, 2, 8, 128]), op=ALU.mult), [t_wsT, t_trl])
        t_es = R.op("act", lambda e: e.activation(out=expsink[:], in_=expsink[:], func=AF.Exp), [t_snk])

        TWO_PI = float(2.0 * np.pi)

        def trig(dst, shift):
            R.op("dve", lambda e: e.tensor_tensor(
                out=ang, in0=posb[:].unsqueeze(2).broadcast_to([128, 18, 8]),
                in1=invb[:].unsqueeze(1).broadcast_to([128, 18, 8]), op=ALU.mult), [t_pos, t_inv])
            if shift != 0.0:
                R.op("dve", lambda e: e.tensor_scalar(out=ang, in0=ang, scalar1=shift, scalar2=None, op0=ALU.add))
            R.op("dve", lambda e: e.tensor_scalar(out=kfl, in0=ang, scalar1=1.0 / TWO_PI, scalar2=None, op0=ALU.mult))
            R.op("dve", lambda e: e.tensor_copy(out=kin, in_=kfl))
            R.op("dve", lambda e: e.tensor_copy(out=kfl, in_=kin))
            R.op("dve", lambda e: e.scalar_tensor_tensor(out=ang, in0=kfl, scalar=-TWO_PI, in1=ang,
                                                         op0=ALU.mult, op1=ALU.add))
            tt = R.op("dve", lambda e: e.tensor_scalar(out=ang, in0=ang, scalar1=3.14159, scalar2=-3.14159,
                                                       op0=ALU.min, op1=ALU.max))
            ta = R.op("act", lambda e: e.activation(out=dst[:], in_=ang, func=AF.Sin), [tt])
            return ta

        ta = trig(sin_sb, 0.0)
        R._waits("dve", [ta])
        t_trig = trig(cos_sb, float(np.pi / 2))


        R.op("act", lambda e: e.activation(out=silu_bf, in_=c17_sb, func=AF.Silu), [t_c17])
        t_sl = ("act", R.cnt["act"])
        tp = None
        for k in range(8):
            tp = R.op("pe", lambda e, k=k: e.transpose(out=bankbf(4)[:, k * 32:k * 32 + 17], in_=mixcat[0:17, k * 128:(k + 1) * 128],
                                                       identity=ident[0:17, 0:17]), [t_sl, t_ident])
        t_sT = R.op("dve", lambda e: e.tensor_copy(
            out=siluT[:], in_=bankbf(4)[:, 0:256].rearrange("p (k c) -> p k c", c=32)[:, :, 0:17]), [tp])
        R.op("dve", lambda e: e.tensor_copy(out=siluTe[:, 0, :, :], in_=siluT[:, :, 0:1].broadcast_to([128, 8, 128])))
        t_sTe = R.op("dve", lambda e: e.tensor_copy(
            out=siluTe[:, 1, :, :].rearrange("p k (b t) -> p k b t", t=8),
            in_=siluT[:, :, 1:17].unsqueeze(3).broadcast_to([128, 8, 16, 8])))

        slot_free = [None] * NSLOT
        ring_n = [0]

        def ring_load(loads):
            s_ = ring_n[0] % NSLOT
            ring_n[0] += 1
            tok = None
            for i, (dstf, src) in enumerate(loads):
                tok = R.dma("pool", "ring%d" % s_, dstf(ring[:, s_, :]), src, [slot_free[s_]] if i == 0 else ())
            return s_, tok

        ada = {"last": None, "slots": {}}
        wo_tok = [None]

        def ada_load(cc):
            ada["slots"][cc] = ring_load([(lambda sl: sl.rearrange("p (k n) -> p k n", k=8),
                                           w_ada[:, cc * 512:(cc + 1) * 512].rearrange("(k p) n -> p k n", p=128))])

        def ada_compute(cc, pz=None, gb=(0, 1), mb=2):
            s_, tl = ada["slots"][cc]
            wa = ring[:, s_, :].rearrange("p (k n) -> p k n", k=8)
            which = cc // 2
            if which in (2, 5):
                half = cc % 2
                gi = 0 if which == 2 else 2
                bsrc = bg1 if which == 2 else bg2
                bdep = t_bg if which == 2 else t_bg2
                for grp in range(2):
                    tm = None
                    for k in range(8):
                        tm = R.op("pe", lambda e, k=k, grp=grp, wa=wa: e.matmul(
                            out=bank(gb[grp]), lhsT=siluTe[:, grp, k, :], rhs=wa[:, k, :], start=(k == 0), stop=(k == 7)),
                            [tl, t_sTe, pz, ada["last"] if k == 0 else None])
                    ada["last"] = R.op("dve", lambda e, grp=grp, gi=gi, half=half, bsrc=bsrc: e.tensor_tensor(
                        out=gates[:, gi + grp, half * 512:(half + 1) * 512], in0=bank(gb[grp]),
                        in1=bsrc[:, half * 512:(half + 1) * 512], op=ALU.add), [tm, bdep])
                slot_free[s_] = tm
            else:
                mi = {0: 0, 1: 1, 3: 2, 4: 3}[which]
                tm = None
                for j4 in range(4):
                    for k in range(8):
                        tm = R.op("pe", lambda e, k=k, j4=j4, wa=wa: e.matmul(
                            out=bank(mb)[:, j4 * 32:j4 * 32 + 17], lhsT=wa[:, k, j4 * 128:(j4 + 1) * 128],
                            rhs=siluT[:, k, :], start=(k == 0), stop=(k == 7)),
                            [tl, t_sT, pz, ada["last"] if (k == 0 and j4 == 0) else None])
                for j4 in range(4):
                    jc = (cc % 2) * 4 + j4
                    acol = cc * 4 + j4
                    ada["last"] = R.op("dve", lambda e, j4=j4, jc=jc, mi=mi, acol=acol: e.tensor_scalar(
                        out=modT[:, mi, jc, :], in0=bank(mb)[:, j4 * 32:j4 * 32 + 17],
                        scalar1=badaT_sb[:, acol:acol + 1], scalar2=(1.0 if mi in (1, 3) else 0.0),
                        op0=ALU.add, op1=ALU.add), [tm, t_badaT])
                slot_free[s_] = tm
            return ada["last"]

        for cc in range(3):
            ada_load(cc)
        t_win = R.dma("pool", "w_in", w_in_sb[:], w_in.rearrange("(k p) n -> p k n", p=128))
        for cc in range(4):
            ada_compute(cc)
            if cc + 3 < 6:
                ada_load(cc + 3)
            if cc == 2:
                t_wo = R.dma("pool", "w_o", w_o_sb[:], w_o.rearrange("(k p) n -> p k n", p=128))
        t_ada = ada["last"]
        ada_load(6)

        def ada_deferred(step):
            plan = {0: [("c", 4), ("l", 7), ("c", 5), ("l", 8)], 1: [("c", 6), ("l", 9), ("c", 7), ("l", 10)],
                    2: [("c", 8), ("l", 11), ("c", 9)], 3: [("c", 10), ("c", 11)]}
            for kind_, cc in plan[step]:
                if kind_ == "l":
                    ada_load(cc)
                else:
                    tk_ = ada_compute(cc, pz=[state["pS_free"], state["pQK_free"]], gb=(7, 5), mb=7)
                    state["pS_free"] = [state["pS_free"], tk_]
                    state["pQK_free"] = [state["pQK_free"], tk_]
            if step == 0:
                x1g_free[1] = [x1g_free[1], ada["last"]]
                wo_tok[0] = R.op("dve", lambda e: e.tensor_tensor(
                    out=w_o_sb[:], in0=w_o_sb[:], in1=gates[:, 0, :].unsqueeze(1).broadcast_to([128, 8, D]), op=ALU.mult),
                    [t_wo, ada["last"]])
            if step == 3:
                state["hidT_free"] = [state.get("hidT_free"), ("pe", R.cnt["pe"]), ada["last"]]

        state = {"tmpA_free": t_ada, "pT_free": t_ada,
                 "pZ_free": t_ada, "pS_free": None, "pQK_free": None, "mixcat_free": t_ada,
                 "PT_free": [t_ada, t_vmem, t_ws], "hT_free": None, "xn_free": None, "qT_free": None,
                 "u_free": None, "gv_free": None, "qkf_free": None, "mixcatT_free": None}
        x1g_free = [t_ada] * 4

        def ln_stats(src, deps, sb_=None):
            st_, mv_, rstd_, nmr_ = sb_ if sb_ is not None else (st, mv, rstd, nmr)
            n = src.shape[-1]
            nch = n // 512
            if sb_ is None:
                deps = [deps, state.get("stat_free")]
            for c in range(nch):
                R.op("dve", lambda e, c=c: e.bn_stats(out=st_[:, c * 6:(c + 1) * 6], in_=src[:, c * 512:(c + 1) * 512]), deps)
            t1 = R.op("dve", lambda e: e.bn_aggr(out=mv_[:], in_=st_[:, 0:6 * nch]))
            R.op("act", lambda e: e.activation(out=rstd_[:], in_=mv_[:, 1:2], func=AF.Ln, bias=epst[:, 0:1], scale=1.0), [t1])
            t2 = R.op("act", lambda e: e.activation(out=rstd_[:], in_=rstd_[:], func=AF.Exp, scale=-0.5))
            t3 = R.op("dve", lambda e: e.scalar_tensor_tensor(out=nmr_[:], in0=mv_[:, 0:1], scalar=-1.0, in1=rstd_[:],
                                                              op0=ALU.mult, op1=ALU.mult), [t2])
            return t3

        def ln_stats_a(src, deps, sb_=None):
            st_, mv_, rstd_, nmr_ = sb_ if sb_ is not None else (st, mv, rstd, nmr)
            nch = src.shape[-1] // 512
            if sb_ is None:
                deps = [deps, state.get("stat_free")]
            for c in range(nch):
                R.op("dve", lambda e, c=c: e.bn_stats(out=st_[:, c * 6:(c + 1) * 6], in_=src[:, c * 512:(c + 1) * 512]), deps)
            return R.op("dve", lambda e: e.bn_aggr(out=mv_[:], in_=st_[:, 0:6 * nch]))

        def ln_stats_b(t1, sb_=None):
            st_, mv_, rstd_, nmr_ = sb_ if sb_ is not None else (st, mv, rstd, nmr)
            R.op("act", lambda e: e.activation(out=rstd_[:], in_=mv_[:, 1:2], func=AF.Ln, bias=epst[:, 0:1], scale=1.0), [t1])
            t2 = R.op("act", lambda e: e.activation(out=rstd_[:], in_=rstd_[:], func=AF.Exp, scale=-0.5))
            return R.op("dve", lambda e: e.scalar_tensor_tensor(out=nmr_[:], in0=mv_[:, 0:1], scalar=-1.0, in1=rstd_[:],
                                                                op0=ALU.mult, op1=ALU.mult), [t2])

        def ln_stats_b_act(t1, sb_):
            st_, mv_, rstd_, nmr_ = sb_
            R.op("act", lambda e: e.activation(out=rstd_[:], in_=mv_[:, 1:2], func=AF.Ln, bias=epst[:, 0:1], scale=1.0), [t1])
            R.op("act", lambda e: e.activation(out=rstd_[:], in_=rstd_[:], func=AF.Exp, scale=-0.5))
            R.op("act", lambda e: e.activation(out=nmr_[:], in_=mv_[:, 0:1], func=AF.Identity, scale=rstd_[:, 0:1]))
            return R.op("act", lambda e: e.mul(out=nmr_[:], in_=nmr_[:], mul=-1.0))

        def norm_pre_b(src, t1, sb_=None):
            st_, mv_, rstd_, nmr_ = sb_ if sb_ is not None else (st, mv, rstd, nmr)
            t3 = ln_stats_b(t1, sb_) if sb_ is None else ln_stats_b_act(t1, sb_)
            return R.op("act", lambda e: e.activation(out=xn[:], in_=src, func=AF.Identity, bias=nmr_[:, 0:1], scale=rstd_[:, 0:1]),
                        [t3, state["xn_free"]])

        def norm_pre(src, deps, sb_=None):
            return norm_pre_b(src, ln_stats_a(src, deps, sb_), sb_)

        def _tb(grp, c):
            if grp == 0 and c >= 4:
                return bankbf(6)[:, (c - 4) * 128:(c - 3) * 128]
            return bankbf(4)[:, c * 128:(c + 1) * 128]

        def norm_post_a(t4, grp):
            tp_ = None
            for c in range(8):
                tp_ = R.op("pe", lambda e, c=c: e.transpose(out=_tb(grp, c), in_=xn[:, c * 128:(c + 1) * 128], identity=ident[:]),
                           [t4, state["pT_free"], state["pQK_free"] if grp == 0 else None, t_ident])
            state["xn_free"] = tp_
            return tp_

        def norm_post_b(tp_, grp, mi_sh, mi_sc, dstT_fn):
            te = []
            if grp == 0:
                for c in range(8):
                    if c < 4:
                        te.append(R.op("act", lambda e, c=c: e.activation(
                            out=dstT_fn(c), in_=_tb(grp, c), func=AF.Identity,
                            scale=modT[:, mi_sc, c, 0:1], bias=modT[:, mi_sh, c, 0:1]), [tp_, state["hT_free"]]))
                    else:
                        te.append(R.op("dve", lambda e, c=c: e.tensor_scalar(
                            out=dstT_fn(c), in0=_tb(grp, c),
                            scalar1=modT[:, mi_sc, c, 0:1], scalar2=modT[:, mi_sh, c, 0:1], op0=ALU.mult, op1=ALU.add),
                            [tp_, state["hT_free"]]))
                te = [te[3], te[7]]
                state["pQK_free"] = [state["pQK_free"], te[1]]
            else:
                for c in range(8):
                    R.op("dve", lambda e, c=c: e.tensor_tensor(
                        out=sg_sb[:, 0:128].rearrange("p (b t) -> p b t", t=8),
                        in0=_tb(grp, c).rearrange("p (b t) -> p b t", t=8),
                        in1=modT[:, mi_sc, c, 1:17].unsqueeze(2).broadcast_to([128, 16, 8]), op=ALU.mult),
                        [tp_, state["hT_free"]])
                    te = R.op("dve", lambda e, c=c: e.tensor_tensor(
                        out=dstT_fn(c).rearrange("p (b t) -> p b t", t=8),
                        in0=sg_sb[:, 0:128].rearrange("p (b t) -> p b t", t=8),
                        in1=modT[:, mi_sh, c, 1:17].unsqueeze(2).broadcast_to([128, 16, 8]), op=ALU.add))
            state["pT_free"] = te
            return te

        def norm_post(t4, grp, mi_sh, mi_sc, dstT_fn):
            return norm_post_b(norm_post_a(t4, grp), grp, mi_sh, mi_sc, dstT_fn)

        def norm_T(src, grp, mi_sh, mi_sc, dstT_fn, deps):
            t4 = norm_pre(src, deps)
            return norm_post(t4, grp, mi_sh, mi_sc, dstT_fn)

        def post_ln(psrc, xres, gate_ap, g_ap, b_ap, dst, deps, dst_free=None, gelu_hint=False, on_pool=True):
            if gate_ap is None:
                R.op("dve", lambda e: e.scalar_tensor_tensor(out=tmpA[:], in0=xres, scalar=ALPHA, in1=psrc,
                                                             op0=ALU.mult, op1=ALU.add), [deps, state["tmpA_free"]])
                tpz = ("dve", R.cnt["dve"])
            else:
                R.op("dve", lambda e: e.tensor_tensor(out=tmpA[:], in0=psrc, in1=gate_ap, op=ALU.mult),
                     [deps, state["tmpA_free"]])
                tpz = ("dve", R.cnt["dve"])
                R.op("dve", lambda e: e.scalar_tensor_tensor(out=tmpA[:], in0=xres, scalar=ALPHA, in1=tmpA[:],
                                                             op0=ALU.mult, op1=ALU.add))
            txr = ("dve", R.cnt["dve"])
            t3 = ln_stats(tmpA[:], ())
            if gelu_hint:
                R.op("act", lambda e: e.activation(out=dmy[:, 2:3], in_=dmy[:, 3:4], func=AF.Gelu_apprx_tanh))
            if not on_pool:
                R.op("dve", lambda e: e.tensor_scalar(out=tmpA[:], in0=tmpA[:], scalar1=rstd[:, 0:1], scalar2=nmr[:, 0:1],
                                                      op0=ALU.mult, op1=ALU.add), [t3])
                R.op("dve", lambda e: e.tensor_tensor(out=tmpA[:], in0=tmpA[:], in1=g_ap, op=ALU.mult))
                t5 = R.op("dve", lambda e: e.tensor_tensor(out=dst, in0=tmpA[:], in1=b_ap, op=ALU.add), [dst_free])
                state["tmpA_free"] = t5
                state["stat_free"] = t5
                return tpz, txr, t5
            R.op("pool", lambda e: e.tensor_scalar(out=tmpA[:], in0=tmpA[:], scalar1=rstd[:, 0:1], scalar2=nmr[:, 0:1],
                                                   op0=ALU.mult, op1=ALU.add), [t3])
            R.op("pool", lambda e: e.tensor_tensor(out=tmpA[:], in0=tmpA[:], in1=g_ap, op=ALU.mult))
            t5 = R.op("pool", lambda e: e.tensor_tensor(out=dst, in0=tmpA[:], in1=b_ap, op=ALU.add), [dst_free])
            state["tmpA_free"] = t5
            state["stat_free"] = t5
            return tpz, txr, t5

        out_tok = []
        t_kTc = [None]
        t_cvb = [None]

        xin_free = [t_ada, t_ada]
        sb2 = (st2, mv2, rstd2, nmr2)
        sb3 = (st3, mv3, rstd3, nmr3)
        sbP = [(sb("stP%d" % i, [128, 12], F32), sb("mvP%d" % i, [128, 2], F32), sb("rsP%d" % i, [128, 1], F32), sb("nmP%d" % i, [128, 1], F32)) for i in range(4)]
        sbG = (st, mv, sb("rsG", [128, 1], F32), sb("nmG", [128, 1], F32))

        def pe_fill(n, bk, deps=()):
            tk_ = None
            for _ in range(n):
                tk_ = R.op("pe", lambda e, bk=bk: e.matmul(out=bank(bk), lhsT=ident[:], rhs=mask_sb[:, 2, :], start=True, stop=True),
                           [t_mask, t_ident, deps])
            return tk_

        pending = []
        pending_pe = []

        def flush_pending_pe():
            while pending_pe:
                pending_pe.pop(0)()

        def flush_pending():
            while pending:
                pending.pop(0)()

        def front_load(T):
            T["t_x"] = R.dma("sp", "xin%d" % T["xi"], xinb[:, T["xi"], :], T["src"], [xin_free[T["xi"]]])

        def front_pre_a(T):
            T["t1"] = ln_stats_a(xinb[:, T["xi"], :], [T["t_x"]], sb2)

        def front_pre_b(T):
            T["t4"] = norm_pre_b(xinb[:, T["xi"], :], T["t1"], sb2)
            if T["kind"] == "halo":
                xin_free[T["xi"]] = T["t4"]

        def front_pre(T):
            front_pre_a(T)
            front_pre_b(T)

        def front_post_a(T):
            T["tp"] = norm_post_a(T["t4"], 1 if T["kind"] == "sample" else 0)

        def front_post_b(T):
            grp = 1 if T["kind"] == "sample" else 0
            T["t_h"] = norm_post_b(T["tp"], grp, 0, 1, lambda c: hT[:, c, :])

        def front_post(T):
            front_post_a(T)
            front_post_b(T)

        def mixer_tile(T, N=None, F=None, hook=None):
            kind, ti, slot, prev_slot, x1dst, x1free = T["kind"], T["ti"], T["slot"], T["prev_slot"], T["x1dst"], T["x1free"]
            xin_ = xinb[:, T["xi"], :]
            grp = 1 if kind == "sample" else 0
            if "t_h" not in T:
                front_load(T)
                front_pre(T)
                front_post(T)
            t_h = T["t_h"]
            if N is not None:
                front_load(N)
            groups = [(0, 0, 512), (1, 512, 256), (2, 768, 512), (3, 1280, 512)]
            if kind == "halo":
                groups = [(1, 512, 256)]
            tz = None
            tzg = {}
            dep01 = state.pop("pZ01_once", None) or state["pZ_free"]
            for (bk, c0, w) in groups:
                for k in range(8):
                    tz = R.op("pe", lambda e, bk=bk, c0=c0, w=w, k=k: e.matmul(
                        out=bank(bk)[:, 0:w], lhsT=hT[:, k, :], rhs=w_in_sb[:, k, c0:c0 + w], start=(k == 0), stop=(k == 7)),
                        [t_h, t_win, dep01 if bk < 2 else state["pZ_free"]])
                tzg[bk] = tz
            state["hT_free"] = tz
            flush_pending_pe()
            if hook is not None:
                hook()
            if kind != "halo":
                tf_ = pe_fill(32 if hook is None else 16, 4, [state["pT_free"]])
                state["pT_free"] = [state["pT_free"], tf_]
            if kind != "halo":
                R.op("act", lambda e: e.copy(out=qkf[:, 0:512], in_=bank(0)), [tzg[0], state["qkf_free"]])
            R.op("act", lambda e: e.copy(out=qkf[:, 512:640], in_=bank(1)[:, 0:128]), [tzg[1], state["qkf_free"]])
            t_v = R.op("act", lambda e: e.copy(out=vf[:], in_=bank(1)[:, 128:256]), [tzg[1], state["qkf_free"]])
            if kind != "halo":
                R.op("act", lambda e: e.activation(out=u_sb[:], in_=bank(2), func=AF.Gelu_apprx_tanh), [tzg[2], state["u_free"]])
                t_gl = R.op("act", lambda e: e.activation(out=gvf[:], in_=bank(3), func=AF.Gelu_apprx_tanh), [tzg[3], state["gv_free"]])
                t_zfree = t_gl
                R.op("act", lambda e: e.activation(out=dmy[:, 0:1], in_=dmy[:, 1:2], func=AF.Exp))
            else:
                t_zfree = t_v
            h0 = 8 if kind == "halo" else 0
            nh = 10 - h0
            qv = qkf[:].rearrange("p (h d) -> p h d", d=64)[:, h0:10, :]
            x1_ = qv[:, :, 0:8]
            x2_ = qv[:, :, 8:16]
            cs = cos_sb[:, ti, :].unsqueeze(1).broadcast_to([128, nh, 8])
            sn = sin_sb[:, ti, :].unsqueeze(1).broadcast_to([128, nh, 8])

            def rtv(i):
                return rt[:, i, 0:nh * 8].rearrange("p (h d) -> p h d", d=8)
            R.op("dve", lambda e: e.tensor_tensor(out=rtv(0), in0=x1_, in1=cs, op=ALU.mult), [t_v, t_trig])
            R.op("dve", lambda e: e.tensor_tensor(out=rtv(1), in0=x2_, in1=sn, op=ALU.mult))
            R.op("dve", lambda e: e.tensor_tensor(out=rtv(2), in0=x2_, in1=cs, op=ALU.mult))
            R.op("dve", lambda e: e.tensor_tensor(out=rtv(3), in0=x1_, in1=sn, op=ALU.mult))
            R.op("dve", lambda e: e.tensor_tensor(out=x1_, in0=rtv(0), in1=rtv(1), op=ALU.subtract))
            t_rot = R.op("dve", lambda e: e.tensor_tensor(out=x2_, in0=rtv(2), in1=rtv(3), op=ALU.add))
            t1g = ln_stats_a(gvf[:], [t_gl], sbG) if kind != "halo" else None
            flush_pending()
            if F is not None:
                F["t1"] = ln_stats_a(F["src"], [F["tok"]], sb3)
            t_qkb = R.op("act", lambda e: e.copy(out=qkb[:, h0 * 64:640], in_=qkf[:, h0 * 64:640]), [t_rot])
            t_vx = R.op("act", lambda e: e.copy(out=vext[:, slot, :, 0:64], in_=vf[:].rearrange("p (h d) -> p h d", d=64)),
                        [state["PT_free"]])
            t_kvout = []
            if kind == "sample":
                for b in range(16):
                    t_kvout.append(R.dma("sp", "okv", outk[b, 120:128, :], qkf[b * 8:(b + 1) * 8, 512:640], [t_rot]))
                    t_kvout.append(R.dma("sp", "okv", outv[b, 120:128, :], vf[b * 8:(b + 1) * 8, :], [t_v]))
            if kind == "prompt" and ti == NPT:
                t_kvout.append(R.dma("sp", "okv", kwin, qkf[:, 512:640], [t_rot]))
                t_kvout.append(R.dma("sp", "okv", vwin, vf[:], [t_v]))
            state["qkf_free"] = [t_qkb, t_vx] + t_kvout[-1:]
            tt_ = None
            ttk = None
            for h in ([8, 9] + list(range(h0, 8))):
                if h < 8:
                    o_ = bankbf(5)[0:64, h * 128:(h + 1) * 128]
                else:
                    o_ = bankbf(6)[0:64, (h - 8) * 128:(h - 7) * 128]
                tt_ = R.op("pe", lambda e, h=h, o_=o_: e.transpose(out=o_, in_=qkb[:, h * 64:(h + 1) * 64], identity=ident[:]),
                           [t_qkb, state["pQK_free"]])
                if h == 9:
                    ttk = tt_
            if kind != "halo":
                tf_ = pe_fill(10, 4, [state["pT_free"]])
                state["pT_free"] = [state["pT_free"], tf_]
            t_kT = R.op("act", lambda e: e.copy(out=kT[:, slot, :], in_=bankbf(6)[0:64, 0:256]), [ttk, state["PT_free"]])
            if kind == "halo":
                state["pQK_free"] = t_kT
                state["pZ_free"] = t_zfree
                if N is not None:
                    front_pre(N)
                    front_post(N)
                return
            t_qT = R.op("act", lambda e: e.copy(out=qT[:], in_=bankbf(5)[0:64, :]), [tt_, state["qT_free"]])
            if kind == "sample":
                for kvh_ in range(2):
                    t_qT = R.op("act", lambda e, kvh_=kvh_: e.copy(
                        out=qTs[:, kvh_, :, :].rearrange("p b (g t) -> p b g t", t=8),
                        in_=bankbf(5)[0:64, kvh_ * 512:(kvh_ + 1) * 512].rearrange("p (g b t) -> p b g t", g=4, t=8)))
            state["pQK_free"] = t_qT

            if kind == "prompt":
                blks = [(prev_slot, 0 if ti == 1 else 1), (slot, 2)]
            else:
                blks = [(slot, 3)]
            nb = len(blks)
            tsc = None
            jj = 0
            sc_banks = []
            tsck = {}
            for kvh in range(2):
                for (ks, mi_) in blks:
                    bk = jj
                    jj += 1
                    sc_banks.append((bk, kvh, ks))
                    R.op("pe", lambda e, bk=bk, kvh=kvh, ks=ks: e.matmul(
                        out=bank(bk), lhsT=kT[:, ks, kvh * 128:(kvh + 1) * 128], rhs=qT[:, kvh * 512:(kvh + 1) * 512],
                        start=True, stop=False), [t_kT, t_qT, t_zfree, state["pZ_free"]])
                    tsc = R.op("pe", lambda e, bk=bk, mi_=mi_: e.matmul(
                        out=bank(bk), lhsT=ident[:], rhs=mask_sb[:, mi_, :], start=False, stop=True), [t_mask])
                tsck[kvh] = tsc
            if kind == "sample":
                for b in range(16):
                    for kvh in range(2):
                        col = b * 64 + kvh * 32
                        tsc = R.op("pe", lambda e, b=b, kvh=kvh, col=col: e.matmul(
                            out=ps[:, 1024 + col:1024 + col + 32],
                            lhsT=kTc[:, b, kvh, :],
                            rhs=qTs[:, kvh, b, :],
                            start=True, stop=True), [t_kTc[0]])
            state["qT_free"] = tsc
            tf_ = pe_fill(8, 4, [state["pT_free"]])
            state["pT_free"] = [state["pT_free"], tf_]

            t3 = ln_stats_b_act(t1g, sbG)
            t4 = R.op("act", lambda e: e.activation(out=gvf[:], in_=gvf[:], func=AF.Identity, bias=sbG[3][:, 0:1], scale=sbG[2][:, 0:1]), [t3])
            R.op("dve", lambda e: e.tensor_tensor(out=gvf[:], in0=gvf[:], in1=sgu_gb[:, 0:512], op=ALU.mult), [t4, t_sgu])
            t5 = R.op("dve", lambda e: e.tensor_tensor(out=gvf[:], in0=gvf[:], in1=sgu_gb[:, 512:1024], op=ALU.add))
            t_sgvo = None
            if kind == "sample":
                t_sgvo = R.dma("sp", "osgv", sgv, gvf[:], [t5])
                out_tok.append(t_sgvo)

            if N is not None:
                front_pre_a(N)
            texp = None
            texpk = {}
            for (bk, kvh, ks) in sc_banks:
                texp = R.op("act", lambda e, bk=bk: e.activation(out=PT[:, bk * 512:(bk + 1) * 512], in_=bank(bk),
                                                                 func=AF.Exp, scale=0.125),
                            [tsck[kvh] if kind == "prompt" else tsc, state["PT_free"]])
                texpk[kvh] = texp
            if kind == "sample":
                R.op("act", lambda e: e.activation(out=PTc[:], in_=bank(2, 2), func=AF.Exp, scale=0.125), [tsc])
                tpc = ("act", R.cnt["act"])
                texp = R.op("dve", lambda e: e.tensor_tensor(
                    out=PTc[:].rearrange("p (a t) -> p a t", t=8), in0=PTc[:].rearrange("p (a t) -> p a t", t=8),
                    in1=m01c_sb[:].unsqueeze(1).broadcast_to([128, 128, 8]), op=ALU.mult), [tpc, t_m01])
                texp = [texp, tpc]
            t6 = R.op("act", lambda e: e.copy(out=gvb[:], in_=gvf[:]), [t5, state["gv_free"]])
            if N is not None:
                front_pre_b(N)
            if F is not None:
                t3f = ln_stats_b_act(F["t1"], sb3)
                F["t4"] = R.op("act", lambda e: e.activation(out=xn2[:], in_=F["src"], func=AF.Identity, bias=nmr3[:, 0:1],
                                                             scale=rstd3[:, 0:1]), [t3f, state.get("xn2_free")])
            tpv = None
            for h in range(8):
                kvh, g = h // 4, h % 4
                ocol = (h // 4) * 512 + (h % 4) * 65
                for bi, (ks, mi_) in enumerate(blks):
                    bk = kvh * nb + bi
                    tpv = R.op("pe", lambda e, bk=bk, g=g, ks=ks, kvh=kvh, ocol=ocol, bi=bi: e.matmul(
                        out=ps[:, ocol:ocol + 65], lhsT=PT[:, bk * 512 + g * 128:bk * 512 + (g + 1) * 128],
                        rhs=vext[:, ks, kvh, :], start=(bi == 0), stop=(bi == nb - 1)),
                        [texpk[kvh] if kind == "prompt" else texp, t_vx])
            state["PT_free"] = tpv
            tm = None
            for h in range(8):
                tm = R.op("pe", lambda e, h=h: e.matmul(out=bank(7)[:, h * 64:(h + 1) * 64], lhsT=wsT_sb[:, grp, h, :],
                                                        rhs=gvb[:, h * 64:(h + 1) * 64], start=True, stop=True),
                          [t6, t_ws, state["pS_free"]])
            state["gv_free"] = [tm, t_sgvo]
            if kind == "prompt":
                tf_ = pe_fill(10, 5, [state["pQK_free"]])
                state["pQK_free"] = [state["pQK_free"], tf_]
            if N is not None:
                front_post_a(N)
            Oview = ps[:, 0:1024].rearrange("p (a n) -> p a n", a=2)[:, :, 0:260].rearrange("p a (g c) -> p a g c", c=65)
            if kind == "sample":
                for b in range(16):
                    for kvh in range(2):
                        col = b * 64 + kvh * 32
                        tpv = R.op("pe", lambda e, b=b, kvh=kvh, col=col: e.matmul(
                            out=ps[0:65, 1024 + col:1024 + col + 32], lhsT=cvb[:, b, kvh, :], rhs=PTc[:, col:col + 32],
                            start=True, stop=True), [texp, t_cvb[0]])
                t_oc = None
                for kvh_ in range(2):
                    t_oc = R.op("act", lambda e, kvh_=kvh_: e.copy(
                        out=OcT_sb[0:65, kvh_ * 512:(kvh_ + 1) * 512].rearrange("p (g b t) -> p b g t", g=4, t=8),
                        in_=ps[0:65, 1024:2048].rearrange("p (b k r) -> p b k r", k=2, r=32)[:, :, kvh_, :].rearrange("p b (g t) -> p b g t", t=8)),
                        [tpv, state["tmpA_free"]])
                ttr = None
                for h in range(8):
                    src_ = OcT_sb[0:65, h * 128:(h + 1) * 128]
                    ocol = (5 + h // 4) * 512 + (h % 4) * 65
                    ttr = R.op("pe", lambda e, src_=src_, ocol=ocol: e.transpose(
                        out=ps[:, ocol:ocol + 65], in_=src_, identity=identf[0:65, 0:65]), [t_oc, state["pQK_free"]])
                Ocv = ps[:, 2560:3584].rearrange("p (a n) -> p a n", a=2)[:, :, 0:260].rearrange("p a (g c) -> p a g c", c=65)
                t_o1 = R.op("act", lambda e: e.copy(out=Osum[:], in_=Ocv), [ttr])
                t_o2 = R.op("dve", lambda e: e.tensor_tensor(out=Osum[:], in0=Osum[:], in1=Oview, op=ALU.add), [t_o1, tpv])
                state["pQK_free"] = t_o1
                Osrc = Osum[:]
                tpv = t_o2
            else:
                Osrc = Oview
            R.op("dve", lambda e: e.tensor_tensor(out=den[:].rearrange("p (a g) -> p a g", a=2), in0=Osrc[:, :, :, 64],
                                                  in1=expsink[:].rearrange("p (a g) -> p a g", a=2), op=ALU.add), [tpv, t_es])
            R.op("dve", lambda e: e.reciprocal(out=rden[:], in_=den[:]))
            t_att = R.op("dve", lambda e: e.tensor_tensor(
                out=mixcat[:, 0:512].rearrange("p (a g d) -> p a g d", a=2, g=4),
                in0=Osrc[:, :, :, 0:64],
                in1=rden[:].rearrange("p (a g) -> p a g", a=2).unsqueeze(3).broadcast_to([128, 2, 4, 64]), op=ALU.mult),
                [state["mixcat_free"]])
            R.op("dve", lambda e: e.tensor_tensor(
                out=sg_sb[:].rearrange("p (h d) -> p h d", d=64), in0=bank(7).rearrange("p (h d) -> p h d", d=64),
                in1=bsT_sb[:, grp, :].unsqueeze(2).broadcast_to([128, 8, 64]), op=ALU.add), [tm, t_bsT])
            tsg = R.op("dve", lambda e: e.tensor_tensor(out=mixcat[:, 512:1024], in0=sg_sb[:], in1=u_sb[:], op=ALU.mult),
                       [state["mixcat_free"]])
            state["pS_free"] = tsg
            state["u_free"] = tsg
            tpa = None
            for c in range(4):
                tpa = R.op("pe", lambda e, c=c: e.transpose(out=bankbf(5)[:, c * 128:(c + 1) * 128],
                                                            in_=mixcat[:, c * 128:(c + 1) * 128], identity=ident[:]),
                           [t_att, state["pQK_free"]])
            tp_ = None
            for c in range(4, 8):
                tp_ = R.op("pe", lambda e, c=c: e.transpose(out=bankbf(7)[:, (c - 4) * 128:(c - 3) * 128],
                                                            in_=mixcat[:, c * 128:(c + 1) * 128], identity=ident[:]),
                           [tsg, state["pS_free"]])
            state["mixcat_free"] = tp_
            t_mTa = R.op("act", lambda e: e.copy(out=mixcatT[:, 0:4, :].rearrange("p k t -> p (k t)"), in_=bankbf(5)[:, 0:512]),
                         [tpa, state["mixcatT_free"]])
            t_mT = R.op("act", lambda e: e.copy(out=mixcatT[:, 4:8, :].rearrange("p k t -> p (k t)"), in_=bankbf(7)[:, 0:512]), [tp_])
            state["pQK_free"] = t_mTa
            state["pS_free"] = [state["pS_free"], t_mT]
            two = None
            for half in range(2):
                for k in range(8):
                    two = R.op("pe", lambda e, half=half, k=k: e.matmul(
                        out=bank(2 + half), lhsT=mixcatT[:, k, :], rhs=w_o_sb[:, k, half * 512:(half + 1) * 512],
                        start=(k == 0), stop=(k == 7)), [t_mTa if k < 4 else t_mT, t_wo, wo_tok[0], t_att])
            state["mixcatT_free"] = two
            if F is not None:
                def _ftr(F=F, tsg=tsg):
                    tpf_ = None
                    for c in range(8):
                        tpf_ = R.op("pe", lambda e, c=c: e.transpose(out=bankbf(7)[:, c * 128:(c + 1) * 128],
                                                                     in_=xn2[:, c * 128:(c + 1) * 128], identity=ident[:]),
                                    [F["t4"], tsg, state["pS_free"], t_ident])
                    state["xn2_free"] = tpf_
                    F["tpf"] = tpf_
                pending_pe.append(_ftr)
            if N is not None:
                front_post_b(N)
            tpz, txr, t5 = post_ln(bank(2, 2), xin_, (gates[:, 1, :] if kind == "sample" else None), lnv[:, 0, :], lnv[:, 1, :], x1dst, [two, t_lnv[0], t_lnv[1]], dst_free=x1free, gelu_hint=(N is not None), on_pool=False)
            state["pZ_free"] = tpz
            state["pZ01_once"] = t_att
            xin_free[T["xi"]] = txr
            if F is not None:
                def _evac(F=F):
                    tpf = F["tpf"]
                    tef = None
                    for c in range(8):
                        tef = R.op("dve", lambda e, c=c: e.tensor_scalar(
                            out=F["dst"](c), in0=bankbf(7)[:, c * 128:(c + 1) * 128],
                            scalar1=modT[:, 3, c, 0:1], scalar2=modT[:, 2, c, 0:1], op0=ALU.mult, op1=ALU.add),
                            [tpf, state.get("h2T_free")])
                    state["pS_free"] = [state["pS_free"], tef]
                    F["th"] = tef
                pending.append(_evac)
            return t5

        def ffn_group(ntiles, grp, x1_toks, y_dsts, ysem, h2_toks=None, post_hook=None, pre=None):
            N = ntiles * 128
            th = []
            for t_ in range(ntiles):
                if h2_toks is not None and h2_toks[t_] is not None:
                    th.append(h2_toks[t_])
                else:
                    th.append(norm_T(x1g[:, t_, :], grp, 2, 3, lambda c, t_=t_: h2T[:, c, t_ * 128:(t_ + 1) * 128],
                                     [x1_toks[t_], state.get("h2T_free")]))
            thid = None
            for f2 in range(NF // 2):
                if pre is not None and f2 in pre:
                    wg, wu, tl, s_ = pre[f2]
                else:
                    s_, tl = ring_load([
                        (lambda sl: sl[:, 0:2048].rearrange("p (k n) -> p k n", k=8),
                         w_gate[:, f2 * 256:(f2 + 1) * 256].rearrange("(k p) n -> p k n", p=128)),
                        (lambda sl: sl[:, 2048:4096].rearrange("p (k n) -> p k n", k=8),
                         w_up[:, f2 * 256:(f2 + 1) * 256].rearrange("(k p) n -> p k n", p=128))])
                    wg = ring[:, s_, 0:2048].rearrange("p (k n) -> p k n", k=8)
                    wu = ring[:, s_, 2048:4096].rearrange("p (k n) -> p k n", k=8)
                tlast = None
                for j in range(2):
                    f = f2 * 2 + j
                    bA, bB = (0, 1) if f % 2 == 0 else (2, 3)
                    key = "pF%d" % (f % 2)
                    for k in range(8):
                        R.op("pe", lambda e, k=k, j=j, bA=bA, wg=wg: e.matmul(
                            out=bank(bA)[:, 0:N], lhsT=wg[:, k, j * 128:(j + 1) * 128], rhs=h2T[:, k, 0:N],
                            start=(k == 0), stop=(k == 7)), [tl, th, state.get(key), state["pZ_free"]])
                    tg = ("pe", R.cnt["pe"])
                    for k in range(8):
                        R.op("pe", lambda e, k=k, j=j, bB=bB, wu=wu: e.matmul(
                            out=bank(bB)[:, 0:N], lhsT=wu[:, k, j * 128:(j + 1) * 128], rhs=h2T[:, k, 0:N],
                            start=(k == 0), stop=(k == 7)))
                    tu = ("pe", R.cnt["pe"])
                    tlast = tu
                    ts = R.op("act", lambda e, bA=bA: e.activation(out=sg_sb[:, 0:N], in_=bank(bA)[:, 0:N], func=AF.Silu),
                              [tg, thid])
                    thid = R.op("dve", lambda e, f=f, bB=bB: e.tensor_tensor(out=hidT[:, f, 0:N], in0=sg_sb[:, 0:N],
                                                                              in1=bank(bB)[:, 0:N], op=ALU.mult),
                                [ts, tu, state.get("hidT_free")])
                    state[key] = thid
                if s_ is not None:
                    slot_free[s_] = tlast
            state["h2T_free"] = tlast
            td = None
            for f2 in range(NF // 2):
                s_, tl = ring_load([(lambda sl: sl[:, 0:2048].rearrange("p (j n) -> p j n", j=2),
                                     w_down[f2 * 256:(f2 + 1) * 256, :].rearrange("(j p) n -> p j n", p=128))])
                wd = ring[:, s_, 0:2048].rearrange("p (j n) -> p j n", j=2)
                for j in range(2):
                    f = f2 * 2 + j
                    for t_ in range(ntiles):
                        for half in range(2):
                            td = R.op("pe", lambda e, f=f, j=j, t_=t_, half=half, wd=wd: e.matmul(
                                out=bank(2 * t_ + half), lhsT=hidT[:, f, t_ * 128:(t_ + 1) * 128],
                                rhs=wd[:, j, half * 512:(half + 1) * 512], start=(f == 0), stop=(f == NF - 1)),
                                [tl, thid, state["pT_free"], state["pQK_free"], state["pS_free"], state["pZ_free"],
                                 state.get("pF0"), state.get("pF1")])
                slot_free[s_] = td
            state["hidT_free"] = td
            last = None
            t1s, t4s = {}, {}

            def stage_a(t_):
                nonlocal last
                xt = x1g[:, t_, :]
                R.op("dve", lambda e: e.tensor_tensor(out=tmpA[:], in0=bank(2 * t_, 2), in1=gates[:, 2 + grp, :], op=ALU.mult),
                     [td, state["tmpA_free"]])
                last = ("dve", R.cnt["dve"])
                R.op("dve", lambda e: e.scalar_tensor_tensor(out=xt, in0=xt, scalar=ALPHA, in1=tmpA[:], op0=ALU.mult, op1=ALU.add))
                state["tmpA_free"] = ("dve", R.cnt["dve"])
                t1s[t_] = ln_stats_a(xt, (), sbP[t_])

            def stage_b(t_):
                xt = x1g[:, t_, :]
                t3 = ln_stats_b_act(t1s[t_], sbP[t_])
                t4s[t_] = R.op("act", lambda e: e.activation(out=xt, in_=xt, func=AF.Identity, bias=sbP[t_][3][:, 0:1],
                                                             scale=sbP[t_][2][:, 0:1]), [t3])

            def stage_c(t_):
                xt = x1g[:, t_, :]
                R.op("dve", lambda e: e.tensor_tensor(out=xt, in0=xt, in1=lnv[:, 2, :], op=ALU.mult), [t4s[t_], t_lnv[2]])
                t5 = R.op("dve", lambda e: e.tensor_tensor(out=xt, in0=xt, in1=lnv[:, 3, :], op=ALU.add), [t_lnv[3]])
                ty = R.dma("sp", "yout%d" % t_, y_dsts[t_], xt, [t5])
                x1g_free[t_] = ty
                out_tok.append(ty)

            order = []
            for step in range(ntiles + 2):
                if step < ntiles:
                    order.append(("a", step))
                if 0 <= step - 1 < ntiles:
                    order.append(("b", step - 1))
                if 0 <= step - 2 < ntiles:
                    order.append(("c", step - 2))
            for kind_, t_ in order:
                {"a": stage_a, "b": stage_b, "c": stage_c}[kind_](t_)
                if post_hook is not None and kind_ == "a" and t_ == 0:
                    post_hook(("dve", R.cnt["dve"]))
            state.pop("pZ01_once", None)
            for k_ in ("pT_free", "pQK_free", "pS_free", "pZ_free"):
                state[k_] = [state[k_], last] if state[k_] is not None else last
            state["x1g_free"] = ("dve", R.cnt["dve"])

        def SAMPLE_PREP(bank0_free=None):
            t_ckb = R.dma("pool", "c_ck", ckb, ck.rearrange("b j c -> j b c"), [state.get("h2T_free")])
            t_cm = R.op("dve", lambda e: e.memset(cvb, 1.0), [state.get("hidT_free")])
            for h_ in range(2):
                t_cvb[0] = R.dma("pool", "c_cv", cvb[:, :, h_, 0:64], cv[:, :, h_ * 64:(h_ + 1) * 64].rearrange("b j d -> j b d"), [t_cm])
            R.dma("sp", "roll", outk[:, 0:120, :], ck[:, 8:128, :])
            R.dma("sp", "roll", outv[:, 0:120, :], cv[:, 8:128, :])
            tev = [state["pZ_free"], state.get("hidT_free"), bank0_free]
            for r_ in range(4):
                tp_ = None
                for i in range(8):
                    b = r_ * 4 + i // 2
                    kvh = i % 2
                    tp_ = R.op("pe", lambda e, b=b, kvh=kvh, i=i: e.transpose(
                        out=bankbf(0, 1)[0:64, i * 128:(i + 1) * 128], in_=ckb[:, b, kvh * 64:(kvh + 1) * 64], identity=ident[:]),
                        [t_ckb, t_ident, tev])
                tev = R.op("act", lambda e, r_=r_: e.copy(out=kTc[:, r_ * 4:(r_ + 1) * 4, :, :].rearrange("p b h j -> p (b h j)"),
                                                          in_=bankbf(0, 1)[0:64, :]), [tp_])
            t_kTc[0] = tev
            state["pZ_free"] = [state["pZ_free"], tev]
            state.pop("pZ01_once", None)


        tiles = [dict(kind="halo", src=xh, ti=0, slot=0, prev_slot=None, x1dst=None, x1free=None, xi=0)]
        for i in range(16):
            slot = (i + 1) % 2
            tiles.append(dict(kind="prompt", src=xp[i * 128:(i + 1) * 128, :], ti=i + 1, slot=slot, prev_slot=1 - slot,
                              x1dst=x1g[:, i % 4, :], x1free=None, xi=(i + 1) % 2, slot4=i % 4))
        tiles.append(dict(kind="sample", src=xs, ti=17, slot=0, prev_slot=None, x1dst=x1g[:, 0, :], x1free=None, xi=1, slot4=0))
        mixer_tile(tiles[0], tiles[1])
        for g_ in range(4):
            toks = []
            fds = []
            for t_ in range(4):
                i = 1 + g_ * 4 + t_
                tiles[i]["x1free"] = x1g_free[t_]
                Fd = None
                if t_ >= 1 and (g_ != 0 or t_ == 3):
                    Fd = dict(src=x1g[:, t_ - 1, :], tok=toks[t_ - 1],
                              dst=(lambda c, tt=t_ - 1: h2T[:, c, tt * 128:(tt + 1) * 128]))
                hk = (lambda st_=i - 1: ada_deferred(st_)) if i in (1, 2, 3, 4) else None
                toks.append(mixer_tile(tiles[i], tiles[i + 1], Fd, hk))
                fds.append(Fd)
            flush_pending_pe()
            flush_pending()
            h2_toks = [fds[t_ + 1]["th"] if (t_ < 3 and fds[t_ + 1] is not None) else None for t_ in range(4)]
            if g_ == 3:
                wo_tok[0] = R.dma("pool", "w_o2", w_o_sb[:], w_o.rearrange("(k p) n -> p k n", p=128), [state["mixcatT_free"]])
            ffn_group(4, 0, toks, [yp[(g_ * 4 + t_) * 128:(g_ * 4 + t_ + 1) * 128, :] for t_ in range(4)], "y", h2_toks,
                      post_hook=(SAMPLE_PREP if g_ == 3 else None))
        spre = {}

        def _gsrc(f2):
            return w_gate[:, f2 * 256:(f2 + 1) * 256].rearrange("(k p) n -> p k n", p=128)

        def _usrc(f2):
            return w_up[:, f2 * 256:(f2 + 1) * 256].rearrange("(k p) n -> p k n", p=128)

        def _pf(f2, gview, uview, deps):
            R.dma("pool", "pfx%d" % f2, gview, _gsrc(f2), deps)
            tok = R.dma("pool", "pfx%d" % f2, uview, _usrc(f2))
            spre[f2] = (gview, uview, tok, None)

        for f2 in range(3):
            s_, tl = ring_load([
                (lambda sl: sl[:, 0:2048].rearrange("p (k n) -> p k n", k=8), _gsrc(f2)),
                (lambda sl: sl[:, 2048:4096].rearrange("p (k n) -> p k n", k=8), _usrc(f2))])
            spre[f2] = (ring[:, s_, 0:2048].rearrange("p (k n) -> p k n", k=8),
                        ring[:, s_, 2048:4096].rearrange("p (k n) -> p k n", k=8), tl, s_)
        xs_bf = x1g[:, 1:3, :].rearrange("p a n -> p (a n)").bitcast(BF16)
        _pf(3, xs_bf[:, 0:2048].rearrange("p (k n) -> p k n", k=8), xs_bf[:, 2048:4096].rearrange("p (k n) -> p k n", k=8),
            [x1g_free[1], x1g_free[2]])
        dve_now = ("dve", R.cnt["dve"])
        _pf(4, gates[:, 0, :].bitcast(BF16).rearrange("p (k n) -> p k n", k=8),
            gates[:, 2, :].bitcast(BF16).rearrange("p (k n) -> p k n", k=8), [dve_now])

        def _pf_win():
            wflat = w_in_sb[:].rearrange("p k n -> p (k n)")
            tzs = ("pe", R.cnt["pe"])
            for i_, f2 in enumerate((5, 6, 7)):
                _pf(f2, wflat[:, i_ * 4096:i_ * 4096 + 2048].rearrange("p (k n) -> p k n", k=8),
                    wflat[:, i_ * 4096 + 2048:(i_ + 1) * 4096].rearrange("p (k n) -> p k n", k=8), [tzs])

        tiles[17]["x1free"] = x1g_free[0]
        tk = mixer_tile(tiles[17], None, None, _pf_win)
        ffn_group(1, 1, [tk], [ys], "y", pre=spre)

        R._waits("sp", out_tok)
        R._waits("sp", [("okv", R.dsem["okv"][1]), ("roll", R.dsem["roll"][1])])

        with nc.Block() as block:
            @block.sync
            def _(e):
                R.replay("sp", e)

            @block.tensor
            def _(e):
                R.replay("pe", e)

            @block.scalar
            def _(e):
                R.replay("act", e)

            @block.vector
            def _(e):
                R.replay("dve", e)

            @block.gpsimd
            def _(e):
                R.replay("pool", e)
        if R.track and decisions is not None:
            R.check_psum()
    if decisions == "auto":
        return nc
    return nc, R


_NC = None


def _host_consts():
    tril = np.tril(np.ones((128, 128), np.float32))
    trilT_p = np.ascontiguousarray(tril.T)
    blk = np.zeros((128, 128), np.float32)
    for b in range(16):
        blk[b * 8:(b + 1) * 8, b * 8:(b + 1) * 8] = tril[:8, :8].T
    s_idx = np.arange(128)[:, None]
    q_idx = np.arange(128)[None, :]
    maskP = np.where(s_idx > q_idx, 0.0, NEG).astype(np.float32)
    maskC = np.where(s_idx <= q_idx, 0.0, NEG).astype(np.float32)
    sb_, st_ = s_idx // 8, s_idx % 8
    qb_, qt_ = q_idx // 8, q_idx % 8
    maskS = np.where((sb_ == qb_) & (st_ <= qt_), 0.0, NEG).astype(np.float32)
    m01c = (np.arange(128)[:, None] > np.arange(8)[None, :]).astype(np.float32)
    invf = (500000.0 ** (-np.arange(8, dtype=np.float32) * 2.0 / 16.0)).astype(np.float32)
    return trilT_p, blk, maskP, maskC, maskS, m01c, invf


def _prep(x_prompt, x_sample, cache_k_win, cache_v_win, c_prompt, c_sample,
          w_ada, b_ada, w_in, attn_sinks, sgu_ln_g, sgu_ln_b, w_s, b_s, w_o,
          ln1_g, ln1_b, w_gate, w_up, w_down, ln2_g, ln2_b):
    f32 = np.float32
    A = lambda a: np.ascontiguousarray(np.asarray(a, dtype=f32))
    x_prompt, x_sample = A(x_prompt), A(x_sample)
    ckw, cvw = A(cache_k_win), A(cache_v_win)
    trilT_p, blk, maskP, maskC, maskS, m01c, invf = _host_consts()
    w_s0 = A(w_s)[0]
    wsT_p = np.ascontiguousarray(w_s0.transpose(2, 0, 1))
    wsT_s = np.zeros((128, 8, 128), f32)
    for b in range(16):
        wsT_s[b * 8:(b + 1) * 8, :, b * 8:(b + 1) * 8] = w_s0[:, :8, :8].transpose(2, 0, 1)
    wsT = np.stack([wsT_p, wsT_s], 0)
    trilT = np.stack([trilT_p, blk], 0)
    b_s0 = A(b_s)[0]
    bsT = np.stack([np.ascontiguousarray(b_s0.T), np.tile(np.ascontiguousarray(b_s0[:, :8].T), (16, 1))], 0)
    b_ada0 = A(b_ada)[0]
    b_adaT = np.ascontiguousarray(b_ada0.reshape(48, 128).T)
    b_gate = np.stack([b_ada0[2048:3072], b_ada0[5120:6144]], 0)
    vecs = np.zeros((6, D), f32)
    vecs[0, :512] = A(sgu_ln_g)[0]
    vecs[0, 512:] = A(sgu_ln_b)[0]
    vecs[1], vecs[2], vecs[3], vecs[4] = A(ln1_g)[0], A(ln1_b)[0], A(ln2_g)[0], A(ln2_b)[0]
    common = {
        "w_ada": A(w_ada)[0], "b_adaT": b_adaT, "b_gate": np.ascontiguousarray(b_gate), "w_in": A(w_in)[0],
        "w_o": A(w_o)[0], "w_gate": A(w_gate)[0], "w_up": A(w_up)[0], "w_down": A(w_down)[0],
        "sinks": A(attn_sinks), "vecs": vecs, "wsT": np.ascontiguousarray(wsT), "trilT": np.ascontiguousarray(trilT),
        "bsT": np.ascontiguousarray(bsT), "m01c": m01c, "invf": np.tile(invf[None, :], (128, 1)).astype(f32),
        "identin": np.eye(128, dtype=f32),
    }
    maskNone = np.full((128, 128), NEG, f32)
    in_maps = []
    for c in range(8):
        b, hf = c // 2, c % 2
        m = dict(common)
        m["xp"] = np.ascontiguousarray(x_prompt[b, hf * 2048:(hf + 1) * 2048])
        m["xh"] = np.ascontiguousarray(x_prompt[b, 2048 - 128:2048]) if hf == 1 else np.zeros((128, D), f32)
        m["xs"] = np.ascontiguousarray(x_sample[c * 16:(c + 1) * 16].reshape(128, D))
        m["c17"] = np.ascontiguousarray(np.concatenate([A(c_prompt)[b:b + 1], A(c_sample)[c * 16:(c + 1) * 16]], 0))
        m["ck"] = np.ascontiguousarray(ckw[0, c * 16:(c + 1) * 16].reshape(16, 128, 128))
        m["cv"] = np.ascontiguousarray(cvw[0, c * 16:(c + 1) * 16].reshape(16, 128, 128))
        mp0 = maskP if hf == 1 else maskNone
        m["masks"] = np.ascontiguousarray(np.stack([np.tile(mp0, (1, 4)), np.tile(maskP, (1, 4)),
                                                    np.tile(maskC, (1, 4)), np.tile(maskS, (1, 4))], 0))
        pos = np.zeros((128, 18), f32)
        pos[:, 0] = hf * 2048 - 128 + np.arange(128)
        for i in range(16):
            pos[:, 1 + i] = hf * 2048 + i * 128 + np.arange(128)
        pos[:, 17] = PAST + (np.arange(128) % 8)
        m["posf"] = pos
        in_maps.append(m)
    return in_maps


def _assemble(r):
    f32 = np.float32
    y_prompt = np.stack([np.concatenate([r[2 * b]["yp"], r[2 * b + 1]["yp"]], 0) for b in range(4)], 0)
    y_sample = np.concatenate([r[c]["ys"].reshape(16, 8, D) for c in range(8)], 0)
    kwp = np.stack([r[2 * b + 1]["kwin"].reshape(128, 2, 64) for b in range(4)], 0)[None]
    vwp = np.stack([r[2 * b + 1]["vwin"].reshape(128, 2, 64) for b in range(4)], 0)[None]
    kws = np.concatenate([r[c]["outk"].reshape(16, 128, 2, 64) for c in range(8)], 0)[None]
    vws = np.concatenate([r[c]["outv"].reshape(16, 128, 2, 64) for c in range(8)], 0)[None]
    sg = np.concatenate([r[c]["sgv"].reshape(16, 8, 512) for c in range(8)], 0)[None]
    return (y_prompt.astype(f32), y_sample.astype(f32), kwp.astype(f32), vwp.astype(f32),
            kws.astype(f32), vws.astype(f32), sg.astype(f32))


def kernel(**inputs):
    global _NC
    in_maps = _prep(**inputs)
    if _NC is None:
        _NC = build_nc()
    res = run_bass_kernel_spmd(_NC, in_maps, core_ids=list(range(8)))
    return _assemble(res.results)
```
